# Optimizing a Trainium2 kernel written in Bass

```python
import jax, jax.numpy as jnp
from jax import lax
import numpy as np

D_MODEL = 1024
BATCH = 16
SEQ = 2048
DEPTH = 2

CTX_LEN = 256
GRID_W = 64
RET_HEADS = 8
RET_QK_DIM = 64
RET_V_DIM = 128
RET_QK = RET_HEADS * RET_QK_DIM
RET_V = RET_HEADS * RET_V_DIM
RET_CHUNK = 128
ROPE_BASE = 10000.0
RET_NORM_EPS = 1e-5
RWKV_HEADS = 8
RWKV_HEAD_DIM = 64
RWKV_DIM = RWKV_HEADS * RWKV_HEAD_DIM
DECAY_LORA = 64
AAA_LORA = 64
GATE_LORA = 128
RWKV_NORM_EPS = 64e-5
RWKV_SPLITS = (RWKV_DIM, RWKV_DIM, RWKV_DIM, DECAY_LORA, DECAY_LORA, AAA_LORA, AAA_LORA, GATE_LORA)
RWKV_BLOCK = sum(RWKV_SPLITS)
COL_SPLITS = (RET_QK, RET_QK, RET_V, RET_V, RWKV_BLOCK, D_MODEL, D_MODEL)
W_IN_COLS = sum(COL_SPLITS)
FFN_HIDDEN = -(-8 * D_MODEL // (3 * 256)) * 256
RMS_EPS = 1e-6

kernel_name = 'hybrid_retention_rwkv7_dit_prefix'

F32 = jnp.float32


def split_cols(z, sizes):
    idx, acc = [], 0
    for s in sizes[:-1]:
        acc += s
        idx.append(acc)
    return jnp.split(z, idx, axis=-1)


def split_heads(z, n_heads):
    return z.reshape(*z.shape[:-1], n_heads, z.shape[-1] // n_heads)


def rms_norm(x, g):
    xf = x.astype(F32)
    y = xf * lax.rsqrt(jnp.mean(xf * xf, axis=-1, keepdims=True) + RMS_EPS)
    return (y * g.astype(F32)).astype(x.dtype)


def modulate(h, shift, scale):
    return h * (1 + scale) + shift


def head_norm(y, g, eps):
    yf = y.astype(F32)
    mu = jnp.mean(yf, axis=-1, keepdims=True)
    var = jnp.mean(jnp.square(yf - mu), axis=-1, keepdims=True)
    out = (yf - mu) * lax.rsqrt(var + eps)
    return out.reshape(*y.shape[:-2], -1) * g.astype(F32)


def rope_2d(z):
    n = z.shape[1]
    t = jnp.arange(n)
    row = (t // GRID_W).astype(F32)
    col = (t % GRID_W).astype(F32)
    half = z.shape[-1] // 2
    nfreq = half // 2
    inv = ROPE_BASE ** (-jnp.arange(nfreq, dtype=F32) / nfreq)

    def rot(u, pos):
        ang = pos[:, None] * inv[None, :]
        cos = jnp.cos(ang)[None, :, None, :]
        sin = jnp.sin(ang)[None, :, None, :]
        u1, u2 = u[..., :nfreq], u[..., nfreq:]
        return jnp.concatenate([u1 * cos - u2 * sin, u1 * sin + u2 * cos], axis=-1)

    zf = z.astype(F32)
    return jnp.concatenate([rot(zf[..., :half], row), rot(zf[..., half:], col)], axis=-1)


def qshift_latent(z):
    b, n, ch = z.shape
    rows = n // GRID_W
    g = z.reshape(b, rows, GRID_W, ch // 4, 4)
    gp = jnp.pad(g, ((0, 0), (1, 1), (1, 1), (0, 0), (0, 0)))
    from_left = gp[:, 1:-1, :-2, :, 0]
    from_right = gp[:, 1:-1, 2:, :, 1]
    from_up = gp[:, :-2, 1:-1, :, 2]
    from_down = gp[:, 2:, 1:-1, :, 3]
    return jnp.stack([from_left, from_right, from_up, from_down], axis=-1).reshape(b, n, ch)


def shift_ctx(z):
    b, n, ch = z.shape
    g = z.reshape(b, n, ch // 2, 2)
    gp = jnp.pad(g, ((0, 0), (1, 1), (0, 0), (0, 0)))
    return jnp.stack([gp[:, :-2, :, 0], gp[:, 2:, :, 1]], axis=-1).reshape(b, n, ch)


def retention_chunked(q, k, v, log_gamma, s0, exclusive):
    q, k, v, s0 = q.astype(F32), k.astype(F32), v.astype(F32), s0.astype(F32)
    b, t, h, dk = q.shape
    dv = v.shape[-1]
    n_chunks = t // RET_CHUNK

    def chunks(a):
        return a.reshape(b, n_chunks, RET_CHUNK, h, a.shape[-1]).transpose(1, 0, 2, 3, 4)

    lg = log_gamma.astype(F32)
    pos = jnp.arange(RET_CHUNK, dtype=F32)
    diff = pos[:, None] - pos[None, :]
    mask = diff > 0 if exclusive else diff >= 0
    intra = jnp.where(mask[None], jnp.exp(jnp.maximum(diff, 0.0)[None] * lg[:, None, None]), 0.0)
    q_decay = jnp.exp((pos[:, None] + 1.0) * lg[None, :])
    k_decay = jnp.exp((RET_CHUNK - 1.0 - pos)[:, None] * lg[None, :])
    chunk_decay = jnp.exp(RET_CHUNK * lg)

    def step(s, inp):
        qc, kc, vc = inp
        att = jnp.einsum('bnhd,bmhd->bhnm', qc, kc) * intra[None]
        y = (jnp.einsum('bhnm,bmhe->bnhe', att, vc)
             + jnp.einsum('bnhd,bhde->bnhe', qc, s) * q_decay[None, :, :, None])
        s = s * chunk_decay[None, :, None, None] + jnp.einsum('bmhd,bmhe->bhde', kc * k_decay[None, :, :, None], vc)
        return s, y

    s, y = lax.scan(step, s0, (chunks(q), chunks(k), chunks(v)))
    y = y.transpose(1, 0, 2, 3, 4).reshape(b, t, h, dv)
    return y, s


def retention_bidir(q, k, v, log_gammas, s0_fwd, s0_bwd):
    y_f, s_f = retention_chunked(q, k, v, log_gammas[0], s0_fwd, False)
    y_b, s_b = retention_chunked(q[:, ::-1], k[:, ::-1], v[:, ::-1], log_gammas[1], s0_bwd, True)
    return y_f + y_b[:, ::-1], s_f, s_b


def retention_ctx_states(k, v, log_gammas):
    k, v = k.astype(F32), v.astype(F32)
    n = k.shape[1]
    pos = jnp.arange(n, dtype=F32)
    lg = log_gammas.astype(F32)
    w_f = jnp.exp((n - 1.0 - pos)[:, None] * lg[0][None, :])
    w_b = jnp.exp(pos[:, None] * lg[1][None, :])
    s_f = jnp.einsum('blhd,blhe->bhde', k * w_f[None, :, :, None], v)
    s_b = jnp.einsum('blhd,blhe->bhde', k * w_b[None, :, :, None], v)
    return s_f, s_b


def rwkv7_scan(r, decay, k, v, kk, bvec, s0, exclusive):
    def to_t(z):
        return jnp.moveaxis(z.astype(F32), 1, 0)

    def update(s, w_t, k_t, v_t, kk_t, b_t):
        sk = jnp.einsum('bhvk,bhk->bhv', s, kk_t)
        return s * w_t[:, :, None, :] - sk[..., None] * b_t[:, :, None, :] + v_t[..., None] * k_t[:, :, None, :]

    xs = (to_t(decay), to_t(k), to_t(v), to_t(kk), to_t(bvec))
    s0 = s0.astype(F32)
    if r is None:
        def step_state(s, inp):
            return update(s, *inp), None
        s, _ = lax.scan(step_state, s0, xs)
        return None, s

    def step(s, inp):
        s_new = update(s, *inp[1:])
        y = jnp.einsum('bhvk,bhk->bhv', s if exclusive else s_new, inp[0])
        return s_new, y

    s, y = lax.scan(step, s0, (to_t(r),) + xs)
    return jnp.moveaxis(y, 0, 1), s


def rwkv7_bidir(r, decays, ks, v, kk, bs, s0_fwd, s0_bwd):
    def flip(z):
        return None if z is None else z[:, ::-1]
    y_f, s_f = rwkv7_scan(r, decays[0], ks[0], v, kk, bs[0], s0_fwd, False)
    y_b, s_b = rwkv7_scan(flip(r), flip(decays[1]), flip(ks[1]), flip(v), flip(kk), flip(bs[1]), s0_bwd, True)
    y = None if r is None else y_f + flip(y_b)
    return y, s_f, s_b


def rwkv7_prepare(zb, p):
    zr, zk, zv, zw_f, zw_b, za_f, za_b, zg = split_cols(zb.astype(F32), RWKV_SPLITS)
    kk = split_heads(zk * p['k_k'], RWKV_HEADS)
    kk = kk * lax.rsqrt(jnp.sum(kk * kk, axis=-1, keepdims=True) + 1e-12)
    decays, ks, bs = [], [], []
    for d, (zw, za) in enumerate(((zw_f, za_f), (zw_b, za_b))):
        w_log = -jax.nn.softplus(-(p['w0'][d] + jnp.tanh(zw) @ p['w2'][d])) - 0.5
        decays.append(split_heads(jnp.exp(-jnp.exp(w_log)), RWKV_HEADS))
        a = jax.nn.sigmoid(p['a0'][d] + za @ p['a2'][d])
        ks.append(split_heads(zk * (1 + (a - 1) * p['k_a']), RWKV_HEADS))
        bs.append(kk * split_heads(a, RWKV_HEADS))
    g = jax.nn.sigmoid(zg) @ p['g2']
    return split_heads(zr, RWKV_HEADS), decays, ks, split_heads(zv, RWKV_HEADS), kk, bs, g


def rwkv7_output(y, r, ks, v, g, p):
    k_bonus = 0.5 * (ks[0] + ks[1])
    bonus = jnp.sum(r * k_bonus * p['r_k'][None, None], axis=-1, keepdims=True) * v
    out = head_norm(y, p['rwkv_norm_g'], RWKV_NORM_EPS) + bonus.reshape(*bonus.shape[:-2], -1)
    return out * g


def merge_branches(y_ret, y_rwkv, zga, zgb, p):
    m = (jax.nn.sigmoid(zga.astype(F32)) * (y_ret @ p['w_branch_a'])
         + jax.nn.sigmoid(zgb.astype(F32)) * (y_rwkv @ p['w_branch_b']))
    return m @ p['w_out']


def token_mixer(h_lat, h_ctx, p, ctx_out):
    b = h_lat.shape[0]
    zq, zk, zv, zgr, zrw, zga, zgb = split_cols(h_lat @ p['w_in'], COL_SPLITS)
    cq, ck, cv, cgr, crw, cga, cgb = split_cols(h_ctx @ p['w_in'], COL_SPLITS)
    log_gammas = -jnp.exp(p['ret_decay'].astype(F32))
    k_scale = RET_QK_DIM ** -0.5

    q_c = split_heads(cq, RET_HEADS)
    k_c = split_heads(ck, RET_HEADS) * k_scale
    v_c = split_heads(cv, RET_HEADS)
    zeros_ret = jnp.zeros((b, RET_HEADS, RET_QK_DIM, RET_V_DIM), F32)
    if ctx_out:
        yr_c, sr_f, sr_b = retention_bidir(q_c, k_c, v_c, log_gammas, zeros_ret, zeros_ret)
    else:
        sr_f, sr_b = retention_ctx_states(k_c, v_c, log_gammas)
    q_l = rope_2d(split_heads(zq, RET_HEADS))
    k_l = rope_2d(split_heads(zk, RET_HEADS)) * k_scale
    v_l = split_heads(zv, RET_HEADS)
    yr_l, _, _ = retention_bidir(q_l, k_l, v_l, log_gammas, sr_f, sr_b)
    ret_lat = head_norm(yr_l, p['ret_norm_g'], RET_NORM_EPS) * jax.nn.silu(zgr.astype(F32))

    mu = p['rwkv_mu']
    zb_l = zrw + mu * (qshift_latent(zrw) - zrw)
    zb_c = crw + mu * (shift_ctx(crw) - crw)
    r_c, dec_c, ks_c, v_rc, kk_c, bs_c, g_c = rwkv7_prepare(zb_c, p)
    zeros_rw = jnp.zeros((b, RWKV_HEADS, RWKV_HEAD_DIM, RWKV_HEAD_DIM), F32)
    yw_c, sw_f, sw_b = rwkv7_bidir(r_c if ctx_out else None, dec_c, ks_c, v_rc, kk_c, bs_c, zeros_rw, zeros_rw)
    r_l, dec_l, ks_l, v_rl, kk_l, bs_l, g_l = rwkv7_prepare(zb_l, p)
    yw_l, _, _ = rwkv7_bidir(r_l, dec_l, ks_l, v_rl, kk_l, bs_l, sw_f, sw_b)
    rwkv_lat = rwkv7_output(yw_l, r_l, ks_l, v_rl, g_l, p)

    out_lat = merge_branches(ret_lat, rwkv_lat, zga, zgb, p)
    if not ctx_out:
        return out_lat, None
    ret_ctx = head_norm(yr_c, p['ret_norm_g'], RET_NORM_EPS) * jax.nn.silu(cgr.astype(F32))
    rwkv_ctx = rwkv7_output(yw_c, r_c, ks_c, v_rc, g_c, p)
    out_ctx = merge_branches(ret_ctx, rwkv_ctx, cga, cgb, p)
    return out_lat, out_ctx


def swiglu(h, w13, w2):
    a, g = jnp.split(h @ w13, 2, axis=-1)
    return (jax.nn.silu(a) * g) @ w2


def setup_inputs(seed: int = 0) -> dict:
    key = jax.random.key(seed)
    ks = jax.random.split(key, 32)

    def nrm(k, shape, s):
        return jax.random.normal(k, shape, F32) * s

    L = DEPTH
    hh = jnp.arange(RET_HEADS, dtype=F32)
    theta = jnp.log(-jnp.log1p(-jnp.power(2.0, -5.0 - hh)))
    nn_ = jnp.arange(RWKV_DIM, dtype=F32) / (RWKV_DIM - 1)
    w0_base = -7.0 + 5.0 * nn_ ** 1.35 + 0.5
    return {
        'x': nrm(ks[0], (BATCH, SEQ, D_MODEL), 1.0),
        'c': nrm(ks[1], (BATCH, D_MODEL), 1.0),
        'ctx': nrm(ks[2], (BATCH, CTX_LEN, D_MODEL), 1.0),
        'c_ctx': nrm(ks[3], (D_MODEL,), 1.0),
        'mod_w': nrm(ks[4], (L, D_MODEL, 6 * D_MODEL), 0.5 * D_MODEL ** -0.5),
        'mod_b': nrm(ks[5], (L, 6 * D_MODEL), 0.02),
        'norm1_g': 1.0 + nrm(ks[6], (L, D_MODEL), 0.02),
        'norm2_g': 1.0 + nrm(ks[7], (L, D_MODEL), 0.02),
        'w_in': nrm(ks[8], (L, D_MODEL, W_IN_COLS), D_MODEL ** -0.5),
        'ret_decay': theta[None, None, :] + nrm(ks[9], (L, 2, RET_HEADS), 0.05),
        'ret_norm_g': 1.0 + nrm(ks[10], (L, RET_V), 0.02),
        'rwkv_mu': jax.random.uniform(ks[11], (L, RWKV_BLOCK), F32),
        'rwkv_w0': w0_base[None, None, :] + nrm(ks[12], (L, 2, RWKV_DIM), 0.1),
        'rwkv_w2': nrm(ks[13], (L, 2, DECAY_LORA, RWKV_DIM), 0.1 * DECAY_LORA ** -0.5),
        'rwkv_a0': nrm(ks[14], (L, 2, RWKV_DIM), 0.1),
        'rwkv_a2': nrm(ks[15], (L, 2, AAA_LORA, RWKV_DIM), 0.5 * AAA_LORA ** -0.5),
        'rwkv_g2': nrm(ks[16], (L, GATE_LORA, RWKV_DIM), GATE_LORA ** -0.5),
        'rwkv_k_k': 0.85 + nrm(ks[17], (L, RWKV_DIM), 0.1),
        'rwkv_k_a': 1.0 + nrm(ks[18], (L, RWKV_DIM), 0.1),
        'rwkv_r_k': nrm(ks[19], (L, RWKV_HEADS, RWKV_HEAD_DIM), 0.1),
        'rwkv_norm_g': 1.0 + nrm(ks[20], (L, RWKV_DIM), 0.02),
        'w_branch_a': nrm(ks[21], (L, RET_V, D_MODEL), RET_V ** -0.5),
        'w_branch_b': nrm(ks[22], (L, RWKV_DIM, D_MODEL), RWKV_DIM ** -0.5),
        'w_out': nrm(ks[23], (L, D_MODEL, D_MODEL), D_MODEL ** -0.5),
        'ffn_w13': nrm(ks[24], (L, D_MODEL, 2 * FFN_HIDDEN), D_MODEL ** -0.5),
        'ffn_w2': nrm(ks[25], (L, FFN_HIDDEN, D_MODEL), FFN_HIDDEN ** -0.5),
        'final_norm_g': 1.0 + nrm(ks[26], (D_MODEL,), 0.02),
    }


def reference(x, c, ctx, c_ctx, mod_w, mod_b, norm1_g, norm2_g, w_in, ret_decay, ret_norm_g,
              rwkv_mu, rwkv_w0, rwkv_w2, rwkv_a0, rwkv_a2, rwkv_g2, rwkv_k_k, rwkv_k_a, rwkv_r_k,
              rwkv_norm_g, w_branch_a, w_branch_b, w_out, ffn_w13, ffn_w2, final_norm_g):
    xc = ctx
    for l in range(DEPTH):
        last = l == DEPTH - 1
        p = {
            'w_in': w_in[l], 'ret_decay': ret_decay[l], 'ret_norm_g': ret_norm_g[l],
            'rwkv_mu': rwkv_mu[l], 'w0': rwkv_w0[l], 'w2': rwkv_w2[l], 'a0': rwkv_a0[l], 'a2': rwkv_a2[l],
            'g2': rwkv_g2[l], 'k_k': rwkv_k_k[l], 'k_a': rwkv_k_a[l], 'r_k': rwkv_r_k[l],
            'rwkv_norm_g': rwkv_norm_g[l], 'w_branch_a': w_branch_a[l], 'w_branch_b': w_branch_b[l],
            'w_out': w_out[l],
        }
        sh1, sc1, g1, sh2, sc2, g2 = jnp.split(jax.nn.silu(c) @ mod_w[l] + mod_b[l], 6, axis=-1)
        csh1, csc1, cg1, csh2, csc2, cg2 = jnp.split(jax.nn.silu(c_ctx) @ mod_w[l] + mod_b[l], 6, axis=-1)
        h = modulate(rms_norm(x, norm1_g[l]), sh1[:, None], sc1[:, None])
        hc = modulate(rms_norm(xc, norm1_g[l]), csh1, csc1)
        out, out_c = token_mixer(h, hc, p, not last)
        x = x + (g1[:, None] * out).astype(x.dtype)
        h = modulate(rms_norm(x, norm2_g[l]), sh2[:, None], sc2[:, None])
        x = x + (g2[:, None] * swiglu(h, ffn_w13[l], ffn_w2[l])).astype(x.dtype)
        if not last:
            xc = xc + (cg1 * out_c).astype(xc.dtype)
            hc = modulate(rms_norm(xc, norm2_g[l]), csh2, csc2)
            xc = xc + (cg2 * swiglu(hc, ffn_w13[l], ffn_w2[l])).astype(xc.dtype)
    return rms_norm(x, final_norm_g)
```

```python
import contextlib
import os
import numpy as np
import ml_dtypes
import concourse.bass as bass
import concourse.mybir as mybir
from concourse.bass_utils import run_bass_kernel_spmd

F32 = mybir.dt.float32
BF16 = mybir.dt.bfloat16
AF = mybir.ActivationFunctionType
ALU = mybir.AluOpType
AX = mybir.AxisListType

D = 1024
T = 2048
TC = 256
TT = 2304
NCH = 18
NL = 2
NB = 2
FH = 2816
TILES = [(0, 512), (512, 512), (1024, 512), (1536, 512), (2048, 256)]
ORDER_F = [16, 17] + list(range(16))
ORDER_B = [17, 16] + list(range(15, -1, -1))
DECAY_C = -0.6065306597126334
EPOCH = 20000
ENGS = ('pe', 'act', 'dve', 'pool', 'sp')


class Buf:
    __slots__ = ('w', 'r')

    def __init__(self):
        self.w = None
        self.r = {}


class Prog:
    def __init__(self, nc, stack):
        self.nc = nc
        self.stack = stack
        self.stream = {e: [] for e in ENGS}
        self.n = {e: 0 for e in ENGS}
        self.seen = {e: {} for e in ENGS}
        self.sems = {}
        self.dcount = {}
        self.dnext = {'sp': 0, 'pool': 0}
        self.nslots = {'sp': 12, 'pool': 6}

    def _sem(self, key):
        if key not in self.sems:
            self.sems[key] = self.stack.enter_context(self.nc.semaphore("s_" + "_".join(str(k) for k in key)))
        return self.sems[key]

    def _deps(self, eng, reads, writes):
        deps = {}

        def add(key, val):
            if deps.get(key, 0) < val:
                deps[key] = val
        for b in reads:
            if b.w is not None:
                add(*b.w)
        for b in writes:
            if b.w is not None:
                add(*b.w)
            for k, v in b.r.items():
                add(k, v)
        waits = []
        for key, val in deps.items():
            if key[0] == 'e' and key[1] == 'pe' and eng == 'pe':
                continue
            if self.seen[eng].get(key, 0) >= val:
                continue
            self.seen[eng][key] = val
            waits.append((self._sem(key), val))
        return waits

    def _mark(self, tok, reads, writes):
        key, val = tok
        for b in reads:
            if b.r.get(key, 0) < val:
                b.r[key] = val
        for b in writes:
            b.w = tok
            b.r = {}

    def op(self, eng, fn, reads=(), writes=()):
        waits = self._deps(eng, reads, writes)
        i = self.n[eng]
        self.n[eng] += 1
        key = ('e', eng, i // EPOCH)
        val = i % EPOCH + 1
        mysem = self._sem(key)

        def run(e, waits=waits, fn=fn, mysem=mysem):
            for s, v in waits:
                e.wait_ge(s, v)
            fn(e).then_inc(mysem, 1)
        self.stream[eng].append(run)
        self._mark((key, val), reads, writes)

    def dma(self, q, out, in_, reads=(), writes=(), slow=False):
        idx = self.dnext[q]
        self.dnext[q] = (idx + 1) % self.nslots[q]
        key = ('d', q, idx)
        cnt = self.dcount.get(key, 0)
        waits = self._deps(q, reads, writes)
        sem = self._sem(key)
        if cnt > 0 and self.seen[q].get(key, 0) < 16 * cnt:
            self.seen[q][key] = 16 * cnt
            waits.append((sem, 16 * cnt))
        self.dcount[key] = cnt + 1

        def run(e, waits=waits, out=out, in_=in_, sem=sem, slow=slow):
            for s, v in waits:
                e.wait_ge(s, v)
            if slow:
                e.dma_start(out=out, in_=in_, allow_slow_non_contiguous=True).then_inc(sem, 16)
            else:
                e.dma_start(out=out, in_=in_).then_inc(sem, 16)
        self.stream[q].append(run)
        self._mark((key, 16 * (cnt + 1)), reads, writes)

    def barrier(self):
        toks = []
        for e in ('pe', 'act', 'dve', 'pool'):
            if self.n[e] > 0:
                i = self.n[e] - 1
                toks.append((e, ('e', e, i // EPOCH), i % EPOCH + 1))
        for key, cnt in self.dcount.items():
            toks.append((None, key, 16 * cnt))
        for eng in ENGS:
            waits = []
            for src, key, val in toks:
                if src == eng:
                    continue
                if self.seen[eng].get(key, 0) >= val:
                    continue
                self.seen[eng][key] = val
                waits.append((self._sem(key), val))
            if waits:
                def run(e, waits=waits):
                    for s, v in waits:
                        e.wait_ge(s, v)
                self.stream[eng].append(run)


def _host_consts():
    c = {}
    c['ident_f'] = np.eye(128, dtype=np.float32)
    c['ident_b'] = np.eye(128, dtype=np.float32).astype(ml_dtypes.bfloat16)
    t = np.arange(T)
    row = (t // 64).astype(np.float32)
    col = (t % 64).astype(np.float32)
    inv = (10000.0 ** (-np.arange(16, dtype=np.float32) / 16)).astype(np.float32)
    cos = np.ones((128, TT), np.float32)
    sin = np.zeros((128, TT), np.float32)
    for p in range(128):
        i = p % 64
        pos = row if i < 32 else col
        ii = i % 32
        f = ii % 16
        ang = (pos * inv[f]).astype(np.float32)
        cos[p, :T] = np.cos(ang)
        sin[p, :T] = -np.sin(ang) if ii < 16 else np.sin(ang)
    c['ropecos'] = cos
    c['ropesin'] = sin
    m6 = np.zeros((128, 6), np.float32)
    for p in range(128):
        m6[p, p % 4] = 1.0
        m6[p, 4 + p % 2] = 1.0
    c['m6'] = m6
    rm = np.ones((128, TT), np.float32)
    rm[:, ::128] = 0.0
    c['rmask'] = rm
    a = np.arange(128)[:, None]
    b = np.arange(128)[None, :]
    Lm = (a > b).astype(np.float32)
    Um = (a < b).astype(np.float32)
    UE = (a <= b).astype(np.float32)
    rw = np.zeros((128, 2, 5, 128), np.float32)
    rw[:, 0] = np.stack([Lm, Um, Um, UE, -UE], 1)
    rw[:, 1] = np.stack([Um, Lm, Lm, Lm, -Lm], 1)
    c['rwmask'] = rw
    retD = np.zeros((128, 2, 128), np.float32)
    retM = np.zeros((128, 2, 128), np.float32)
    retD[:, 0] = np.maximum(b - a, 0)
    retM[:, 0] = (a <= b)
    retD[:, 1] = np.maximum(a - b, 0)
    retM[:, 1] = (a > b)
    c['retD'] = retD
    c['retM'] = retM
    qdt = np.zeros((128, 2, 128), np.float32)
    qdt[:, 0] = (b + 1)
    qdt[:, 1] = (128 - b)
    c['qdt'] = qdt
    kdt = np.zeros((128, 2), np.float32)
    kdt[:, 0] = 127 - np.arange(128)
    kdt[:, 1] = np.arange(128)
    c['kdt'] = kdt
    return c


CONST_SHAPES = {
    'ident_f': ([128, 128], F32), 'ident_b': ([128, 128], BF16), 'ropecos': ([128, TT], F32),
    'ropesin': ([128, TT], F32), 'm6': ([128, 6], F32), 'rmask': ([128, TT], F32),
    'rwmask': ([128, 2, 5, 128], F32), 'retD': ([128, 2, 128], F32), 'retM': ([128, 2, 128], F32),
    'qdt': ([128, 2, 128], F32), 'kdt': ([128, 2], F32),
}

IN_SHAPES = {
    'x': [NB, T, D], 'c': [NB, D], 'ctx': [NB, TC, D], 'c_ctx': [D], 'mod_w': [NL, D, 6 * D],
    'mod_b': [NL, 6 * D], 'norm1_g': [NL, D], 'norm2_g': [NL, D], 'w_in': [NL, D, 7040],
    'w_rot': [NL, D, 1024],
    'ret_decay': [NL, 16], 'ret_norm_g': [NL, D], 'rwkv_mu': [NL, 1920], 'rwkv_w0': [NL, 2, 512],
    'rwkv_w2': [NL, 128, 512], 'rwkv_a0': [NL, 2, 512], 'rwkv_a2': [NL, 128, 512], 'rwkv_g2': [NL, 128, 512],
    'rwkv_k_k': [NL, 512], 'rwkv_k_a': [NL, 512], 'rwkv_r_k': [NL, 512], 'rwkv_norm_g': [NL, 512],
    'w_branch_a': [NL, D, D], 'w_branch_b': [NL, 512, D], 'w_out': [NL, D, D], 'ffn_w13': [NL, D, 2 * FH],
    'ffn_w2': [NL, FH, D], 'final_norm_g': [D],
}


def build(debug=None, nlayers=NL, nbatch=NB, stop_after=None):
    debug = debug or []
    nc = bass.Bass("TRN2", target_bir_lowering=False)
    stack = contextlib.ExitStack()
    with stack:
        P = Prog(nc, stack)
        I = {k: nc.dram_tensor(k, s, F32, kind="ExternalInput").ap() for k, s in IN_SHAPES.items()}
        C = {k: nc.dram_tensor(k, s, dt, kind="ExternalInput").ap() for k, (s, dt) in CONST_SHAPES.items()}
        out = nc.dram_tensor("out", [NB, T, D], F32, kind="ExternalOutput").ap()
        out_b = Buf()

        def scratch(name, shape, dt):
            kind = "ExternalOutput" if name in debug else "Internal"
            return nc.dram_tensor(name, shape, dt, kind=kind).ap(), Buf()
        qk_s, qk_b = scratch("qk_s", [8, 128, TT], BF16)
        v_s, v_b = scratch("v_s", [NCH, 128, 1024], BF16)
        gate_s, gate_b = scratch("gate_s", [24, 128, TT], BF16)
        zrw_s, zrw_b = scratch("zrw_s", [15, 128, TT], F32)
        ret_s, ret_b = scratch("ret_s", [8, 128, TT], BF16)
        rw_s, rw_b = scratch("rw_s", [4, 2, 128, 7, TT], BF16)
        bon_s, bon_b = scratch("bon_s", [4, 2, 128, TT], F32)
        rwkv_s, rwkv_b = scratch("rwkv_s", [4, 128, TT], BF16)
        dbg_h, dbg_hb = scratch("dbg_h", [8, 128, TT], BF16)
        dbg_x, dbg_xb = scratch("dbg_x", [8, 128, TT], F32)

        def sb(name, shape, dt):
            return stack.enter_context(nc.sbuf_tensor(name, shape, dt)), Buf()
        xT, xT_b = sb("xT", [128, 8, TT], F32)
        AW = 31616
        arena, _ = sb("arena", [128, AW], F32)
        cst = {}
        for k in ('ident_f', 'ident_b', 'm6', 'rwmask', 'retD', 'retM', 'qdt', 'kdt'):
            cst[k] = sb("c_" + k, CONST_SHAPES[k][0], CONST_SHAPES[k][1])
        ones_f, ones_fb = sb("ones_f", [128, 128], F32)
        bones_f, bones_fb = sb("bones_f", [128, 128], F32)
        modT, modT_b = sb("modT", [128, NL, 48, 3], F32)
        modA, modA_b = sb("modA", [128, NL, 2, 8, 3], F32)
        gC, gC_b = sb("gC", [128, 4, 2, NCH], F32)
        pst = []
        for i in range(8):
            t_ = stack.enter_context(nc.psum_tensor("ps%d" % i, [128, 512], F32))
            pst.append((t_, Buf()))
        pi = [0]

        def psum():
            i = pi[0]
            pi[0] = (i + 1) % 8
            return pst[i]

        class Arena:
            def __init__(self):
                self.off = 0

            def reset(self):
                self.off = 0

            def alloc(self, shape, dt):
                n = int(np.prod(shape[1:]))
                words = n if dt == F32 else (n + 1) // 2
                assert self.off + words <= AW, (self.off, words, AW)
                ap = arena[:, self.off:self.off + words]
                self.off += words
                if dt == BF16:
                    ap = ap.bitcast(BF16)
                    if n % 2:
                        ap = ap[:, 0:n]
                if len(shape) == 3:
                    ap = ap.rearrange("p (a b) -> p a b", a=shape[1])
                elif len(shape) == 4:
                    ap = ap.rearrange("p (a b c) -> p a b c", a=shape[1], b=shape[2])
                return ap, Buf()
        A = Arena()

        def mm(ps, psb, lhsT, rhs, start, stop, reads):
            P.op('pe', lambda e: e.matmul(ps, lhsT=lhsT, rhs=rhs, start=start, stop=stop),
                 reads=reads, writes=[psb])

        def transp(ps, psb, in_, ident, reads):
            P.op('pe', lambda e: e.transpose(out=ps, in_=in_, identity=ident), reads=reads, writes=[psb])

        def act(out_, in_, func, reads, writes, bias=0.0, scale=1.0):
            P.op('act', lambda e: e.activation(out=out_, in_=in_, func=func, bias=bias, scale=scale),
                 reads=reads, writes=writes)

        def tt(eng, out_, in0, in1, op, reads, writes):
            P.op(eng, lambda e: e.tensor_tensor(out=out_, in0=in0, in1=in1, op=op), reads=reads, writes=writes)

        def ts(eng, out_, in0, s1, s2, op0, op1, reads, writes):
            if op1 is None:
                P.op(eng, lambda e: e.tensor_scalar(out=out_, in0=in0, scalar1=s1, scalar2=None, op0=op0),
                     reads=reads, writes=writes)
            else:
                P.op(eng, lambda e: e.tensor_scalar(out=out_, in0=in0, scalar1=s1, scalar2=s2, op0=op0, op1=op1),
                     reads=reads, writes=writes)

        def stt(out_, in0, scalar, in1, op0, op1, reads, writes):
            P.op('dve', lambda e: e.scalar_tensor_tensor(out=out_, in0=in0, scalar=scalar, in1=in1, op0=op0, op1=op1),
                 reads=reads, writes=writes)

        def cp(eng, out_, in_, reads, writes):
            if eng == 'act':
                P.op('act', lambda e: e.copy(out=out_, in_=in_), reads=reads, writes=writes)
            else:
                P.op(eng, lambda e: e.tensor_copy(out=out_, in_=in_), reads=reads, writes=writes)

        def recip(out_, in_, reads, writes):
            P.op('dve', lambda e: e.reciprocal(out=out_, in_=in_), reads=reads, writes=writes)

        def memset(eng, ap, val, writes):
            P.op(eng, lambda e: e.memset(ap, val), writes=writes)

        for k in cst:
            P.dma('sp', cst[k][0][:], C[k], writes=[cst[k][1]])
        ident_f, ident_fb = cst['ident_f']
        ident_b, ident_bb = cst['ident_b']
        memset('pool', ones_f[:], 1.0, [ones_fb])
        memset('pool', bones_f[:], 0.0, [bones_fb])
        memset('pool', bones_f[0:64, 0:64], 1.0, [bones_fb])
        memset('pool', bones_f[64:128, 64:128], 1.0, [bones_fb])

        A.reset()
        c3, c3_b = A.alloc([128, 8, 3], F32)
        s3, s3_b = A.alloc([128, 8, 3], F32)
        mb, mb_b = A.alloc([128, NL, 48], F32)
        ng, ng_b = A.alloc([128, NL, 2, 8], F32)
        for r in range(NB):
            P.dma('sp', c3[:, :, r], I['c'][r].rearrange("(k p) -> p k", p=128), writes=[c3_b], slow=True)
        P.dma('sp', c3[:, :, 2], I['c_ctx'].rearrange("(k p) -> p k", p=128), writes=[c3_b], slow=True)
        for l in range(NL):
            P.dma('sp', mb[:, l, :], I['mod_b'][l].rearrange("(j p) -> p j", p=128), writes=[mb_b], slow=True)
            P.dma('sp', ng[:, l, 0, :], I['norm1_g'][l].rearrange("(k p) -> p k", p=128), writes=[ng_b], slow=True)
            P.dma('sp', ng[:, l, 1, :], I['norm2_g'][l].rearrange("(k p) -> p k", p=128), writes=[ng_b], slow=True)
        act(s3, c3, AF.Silu, [c3_b], [s3_b])
        wst = [A.alloc([128, 8, 512], F32) for _ in range(2)]
        wi = 0
        for l in range(nlayers):
            for cc in range(12):
                w_, w_b = wst[wi % 2]
                wi += 1
                P.dma('sp', w_, I['mod_w'][l, :, cc * 512:(cc + 1) * 512].rearrange("(k p) n -> p k n", p=128), writes=[w_b])
                for jj in range(4):
                    j = cc * 4 + jj
                    ps, psb = psum()
                    for k in range(8):
                        mm(ps[:, 0:3], psb, w_[:, k, jj * 128:(jj + 1) * 128], s3[:, k, :], k == 0, k == 7, [w_b, s3_b])
                    act(modT[:, l, j, :], ps[:, 0:3], AF.Identity, [psb, mb_b], [modT_b], bias=mb[:, l, j:j + 1])
            for n_ in range(2):
                j0 = 8 if n_ == 0 else 32
                ts('dve', modA[:, l, n_, :, :], modT[:, l, j0:j0 + 8, :], 1.0, None, ALU.add, None, [modT_b], [modA_b])
                tt('dve', modA[:, l, n_, :, :], modA[:, l, n_, :, :], ng[:, l, n_, :].unsqueeze(2).broadcast_to([128, 8, 3]),
                   ALU.mult, [modA_b, ng_b], [modA_b])
        P.barrier()

        def norm_mod(dst, dst_b, Afn, shfn, sq, sq_b, rs, rs_b, tmp, tmp_b, eps, tiles):
            for (t0, w) in tiles:
                for k in range(8):
                    act(sq[:, k, 0:w], xT[:, k, t0:t0 + w], AF.Square, [xT_b], [sq_b])
                ps, psb = psum()
                for k in range(8):
                    mm(ps[:, 0:w], psb, ones_f[:], sq[:, k, 0:w], k == 0, k == 7, [ones_fb, sq_b])
                act(rs[:, 0:w], ps[:, 0:w], AF.Sqrt, [psb], [rs_b], bias=eps_ap(eps), scale=1.0 / D)
                recip(rs[:, 0:w], rs[:, 0:w], [rs_b], [rs_b])
                r = 2 if t0 >= T else None
                for k in range(8):
                    tt('pool' if k % 2 else 'dve', tmp[:, k % 2, 0:w], xT[:, k, t0:t0 + w], rs[:, 0:w], ALU.mult,
                       [xT_b, rs_b], [tmp_b[k % 2]])
                    a_ap, a_bufs = Afn(k, r)
                    s_ap, s_bufs = shfn(k, r)
                    act(dst(k, t0, w), tmp[:, k % 2, 0:w], AF.Identity, [tmp_b[k % 2]] + a_bufs + s_bufs, [dst_b],
                        bias=s_ap, scale=a_ap)

        epsT, epsT_b = sb("epsT", [128, 4], F32)
        memset('pool', epsT[:, 0:1], 1e-6, [epsT_b])
        memset('pool', epsT[:, 1:2], 1e-5 * 64.0, [epsT_b])
        memset('pool', epsT[:, 2:3], 64e-5, [epsT_b])
        memset('pool', epsT[:, 3:4], 1e-12, [epsT_b])
        EPSI = {1e-6: 0, 1e-5 * 64.0: 1, 64e-5: 2, 1e-12: 3}

        def eps_ap(eps):
            i = EPSI[eps]
            return epsT[:, i:i + 1]

        fng, fng_b = sb("fng", [128, 8], F32)
        P.dma('sp', fng[:], I['final_norm_g'].rearrange("(k p) -> p k", p=128), writes=[fng_b], slow=True)
        zcol, zcol_b = sb("zcol", [128, 1], F32)
        memset('pool', zcol[:], 0.0, [zcol_b])

        def dense_fm(w_ap, w_b, kc, src, src_b, tiles, evac):
            for ti, (t0, w) in enumerate(tiles):
                ps, psb = psum()
                for k in range(kc):
                    mm(ps[:, 0:w], psb, w_ap[:, k, :], src(k, t0, w), k == 0, k == kc - 1, [w_b, src_b])
                evac(ps, psb, t0, w)

        def phase_ret(bi, l):
            A.reset()
            lgt, lgt_b = A.alloc([128, 16], F32)
            gcr, gcr_b = A.alloc([128, 16], F32)
            rng, rng_b = A.alloc([128, 8], F32)
            P.dma('sp', lgt, I['ret_decay'][l].partition_broadcast(128), writes=[lgt_b], slow=True)
            P.dma('sp', rng, I['ret_norm_g'][l].rearrange("(h p) -> p h", p=128), writes=[rng_b], slow=True)
            act(lgt, lgt, AF.Exp, [lgt_b], [lgt_b])
            ts('pool', lgt, lgt, -1.0, None, ALU.mult, None, [lgt_b], [lgt_b])
            act(gcr, lgt, AF.Exp, [lgt_b], [gcr_b], scale=128.0)
            retD, retD_b = cst['retD']
            retM, retM_b = cst['retM']
            qdt, qdt_b = cst['qdt']
            kdt, kdt_b = cst['kdt']
            masks, masks_b = A.alloc([128, 8, 128], F32)
            qd, qd_b = A.alloc([128, 16, 128], F32)
            kd, kd_b = A.alloc([128, 16], F32)
            e2, e2_b = A.alloc([128, 128], F32)
            for h in range(8):
                lf = lgt[:, h:h + 1]
                lb = lgt[:, 8 + h:9 + h]
                act(masks[:, h, :], retD[:, 0, :], AF.Exp, [retD_b, lgt_b], [masks_b], scale=lf)
                tt('pool', masks[:, h, :], masks[:, h, :], retM[:, 0, :], ALU.mult, [masks_b, retM_b], [masks_b])
                act(e2, retD[:, 1, :], AF.Exp, [retD_b, lgt_b], [e2_b], scale=lb)
                tt('pool', e2, e2, retM[:, 1, :], ALU.mult, [e2_b, retM_b], [e2_b])
                tt('pool', masks[:, h, :], masks[:, h, :], e2, ALU.add, [masks_b, e2_b], [masks_b])
                act(qd[:, h * 2, :], qdt[:, 0, :], AF.Exp, [qdt_b, lgt_b], [qd_b], scale=lf)
                act(qd[:, h * 2 + 1, :], qdt[:, 1, :], AF.Exp, [qdt_b, lgt_b], [qd_b], scale=lb)
                act(kd[:, h * 2:h * 2 + 1], kdt[:, 0:1], AF.Exp, [kdt_b, lgt_b], [kd_b], scale=lf)
                act(kd[:, h * 2 + 1:h * 2 + 2], kdt[:, 1:2], AF.Exp, [kdt_b, lgt_b], [kd_b], scale=lb)
            base_off = A.off
            for j in range(4):
                P.barrier()
                A.off = base_off
                qT, qT_b = A.alloc([128, NCH, 128], BF16)
                kT, kT_b = A.alloc([128, NCH, 128], BF16)
                vt, vt_b = A.alloc([128, NCH, 256], BF16)
                P.dma('sp', qT, qk_s[j].rearrange("p (c t) -> p c t", c=NCH), reads=[qk_b], writes=[qT_b])
                P.dma('sp', kT, qk_s[4 + j].rearrange("p (c t) -> p c t", c=NCH), reads=[qk_b], writes=[kT_b])
                P.dma('sp', vt, v_s[:, :, j * 256:(j + 1) * 256].rearrange("c p e -> p c e"), reads=[v_b], writes=[vt_b])
                kdp, kdp_b = A.alloc([128, 2, 128], F32)
                for d in range(2):
                    for hp in range(2):
                        h = 2 * j + hp
                        ts('pool', kdp[:, d, hp * 64:(hp + 1) * 64], ones_f[:, 0:64], kd[:, h * 2 + d:h * 2 + d + 1], None,
                           ALU.mult, None, [ones_fb, kd_b], [kdp_b])
                ktd = [A.alloc([128, NCH, 128], BF16) for _ in range(2)]
                for c0 in range(0, NCH, 8):
                    n = min(8, NCH - c0)
                    ps, psb = psum()
                    psv = ps[:].bitcast(BF16)
                    for cc in range(n):
                        transp(psv[:, cc * 128:(cc + 1) * 128], psb, kT[:, c0 + cc, :], ident_b[:], [kT_b, ident_bb])
                    for d in range(2):
                        tt('dve', ktd[d][0][:, c0:c0 + n, :], psv[:, 0:n * 128].rearrange("p (c t) -> p c t", c=n),
                           kdp[:, d, :].unsqueeze(1).broadcast_to([128, n, 128]), ALU.mult, [psb, kdp_b], [ktd[d][1]])
                KV = [A.alloc([128, NCH, 128], F32) for _ in range(2)]
                Sbf = [A.alloc([128, NCH, 128], BF16) for _ in range(2)]
                Srun = [A.alloc([128, 128], F32) for _ in range(2)]
                qfb = [A.alloc([128, NCH, 128], BF16) for _ in range(2)]
                att = [A.alloc([128, 4, 128], BF16) for _ in range(2)]
                ys, ys_b = A.alloc([128, 512], F32)
                yc, yc_b = A.alloc([128, 512], F32)
                sq, sq_b = A.alloc([128, 512], F32)
                rs, rs_b = A.alloc([128, 512], F32)
                yn, yn_b = A.alloc([128, 512], F32)
                gts = [A.alloc([128, 512], BF16) for _ in range(2)]
                ros = [A.alloc([128, TT], BF16) for _ in range(2)]
                ai = 0
                for hp in range(2):
                    h = 2 * j + hp
                    r0, r1 = hp * 64, hp * 64 + 64
                    for d in range(2):
                        for (t0, w) in TILES:
                            c0, n = t0 // 128, w // 128
                            ps, psb = psum()
                            for cc in range(n):
                                mm(ps[:, cc * 128:(cc + 1) * 128], psb, ktd[d][0][:, c0 + cc, :], vt[:, c0 + cc, hp * 128:(hp + 1) * 128],
                                   True, True, [ktd[d][1], vt_b])
                            cp('act' if d else 'dve', KV[d][0][:, c0:c0 + n, :], ps[:, 0:w].rearrange("p (c t) -> p c t", c=n), [psb], [KV[d][1]])
                        S_, S_b = Srun[d]
                        memset('pool', S_[:], 0.0, [S_b])
                        for c in (ORDER_F if d == 0 else ORDER_B):
                            cp('pool', Sbf[d][0][r0:r1, c, :], S_[r0:r1, :], [S_b], [Sbf[d][1]])
                            stt(S_[r0:r1, :], S_[r0:r1, :], gcr[r0:r1, d * 8 + h:d * 8 + h + 1], KV[d][0][r0:r1, c, :], ALU.mult, ALU.add,
                                [S_b, gcr_b, KV[d][1]], [S_b])
                        tt('pool', qfb[d][0][r0:r1], qT[r0:r1], qd[r0:r1, h * 2 + d, :].unsqueeze(1).broadcast_to([64, NCH, 128]),
                           ALU.mult, [qT_b, qd_b], [qfb[d][1]])
                    ro, ro_b = ros[hp]
                    for ti, (t0, w) in enumerate(TILES):
                        c0, n = t0 // 128, w // 128
                        at_, at_b = att[ai % 2]
                        gt_, gt_b = gts[ai % 2]
                        ai += 1
                        P.dma('sp', gt_[:, 0:w], gate_s[h][:, t0:t0 + w], reads=[gate_b], writes=[gt_b])
                        ps, psb = psum()
                        for cc in range(n):
                            mm(ps[:, cc * 128:(cc + 1) * 128], psb, kT[r0:r1, c0 + cc, :], qT[r0:r1, c0 + cc, :], True, True, [kT_b, qT_b])
                        tt('dve', at_[:, 0:n, :], ps[:, 0:w].rearrange("p (c t) -> p c t", c=n),
                           masks[:, h, :].unsqueeze(1).broadcast_to([128, n, 128]), ALU.mult, [psb, masks_b], [at_b])
                        ps2, ps2b = psum()
                        for cc in range(n):
                            c = c0 + cc
                            o_ = ps2[:, cc * 128:(cc + 1) * 128]
                            mm(o_, ps2b, vt[:, c, hp * 128:(hp + 1) * 128], at_[:, cc, :], True, False, [vt_b, at_b])
                            mm(o_, ps2b, Sbf[0][0][r0:r1, c, :], qfb[0][0][r0:r1, c, :], False, False, [Sbf[0][1], qfb[0][1]])
                            mm(o_, ps2b, Sbf[1][0][r0:r1, c, :], qfb[1][0][r0:r1, c, :], False, True, [Sbf[1][1], qfb[1][1]])
                        cp('act', ys[:, 0:w], ps2[:, 0:w], [ps2b], [ys_b])
                        ps3, ps3b = psum()
                        mm(ps3[:, 0:w], ps3b, ones_f[:], ys[:, 0:w], True, True, [ones_fb, ys_b])
                        stt(yc[:, 0:w], ps3[:, 0:w], -1.0 / 128, ys[:, 0:w], ALU.mult, ALU.add, [ps3b, ys_b], [yc_b])
                        act(sq[:, 0:w], yc[:, 0:w], AF.Square, [yc_b], [sq_b])
                        ps4, ps4b = psum()
                        mm(ps4[:, 0:w], ps4b, ones_f[:], sq[:, 0:w], True, True, [ones_fb, sq_b])
                        act(rs[:, 0:w], ps4[:, 0:w], AF.Sqrt, [ps4b, epsT_b], [rs_b], bias=eps_ap(1e-5 * 64.0), scale=1.0 / 128)
                        recip(rs[:, 0:w], rs[:, 0:w], [rs_b], [rs_b])
                        tt('pool', yn[:, 0:w], yc[:, 0:w], rs[:, 0:w], ALU.mult, [yc_b, rs_b], [yn_b])
                        stt(ro[:, t0:t0 + w], yn[:, 0:w], rng[:, h:h + 1], gt_[:, 0:w], ALU.mult, ALU.mult, [yn_b, rng_b, gt_b], [ro_b])
                    P.dma('sp', ret_s[h], ro[:], reads=[ro_b], writes=[ret_b])

        def phase_rwkv(bi, l):
            A.reset()
            rwmask, rwmask_b = cst['rwmask']
            m6, m6_b = cst['m6']
            mu, mu_b = A.alloc([128, 15], F32)
            mus, mus_b = A.alloc([128, 15, 7], F32)
            w0, w0_b = A.alloc([128, 2, 4], F32)
            a0, a0_b = A.alloc([128, 2, 4], F32)
            kkc, kkc_b = A.alloc([128, 4], F32)
            kac, kac_b = A.alloc([128, 4], F32)
            omk, omk_b = A.alloc([128, 4], F32)
            hrk, hrk_b = A.alloc([128, 4], F32)
            ngc, ngc_b = A.alloc([128, 4], F32)
            P.dma('sp', mu, I['rwkv_mu'][l].rearrange("(j p) -> p j", p=128), writes=[mu_b], slow=True)
            for d in range(2):
                P.dma('sp', w0[:, d, :], I['rwkv_w0'][l, d].rearrange("(j p) -> p j", p=128), writes=[w0_b], slow=True)
                P.dma('sp', a0[:, d, :], I['rwkv_a0'][l, d].rearrange("(j p) -> p j", p=128), writes=[a0_b], slow=True)
            P.dma('sp', kkc, I['rwkv_k_k'][l].rearrange("(j p) -> p j", p=128), writes=[kkc_b], slow=True)
            P.dma('sp', kac, I['rwkv_k_a'][l].rearrange("(j p) -> p j", p=128), writes=[kac_b], slow=True)
            P.dma('sp', hrk, I['rwkv_r_k'][l].rearrange("(j p) -> p j", p=128), writes=[hrk_b], slow=True)
            P.dma('sp', ngc, I['rwkv_norm_g'][l].rearrange("(j p) -> p j", p=128), writes=[ngc_b], slow=True)
            ts('pool', omk, kac, -1.0, 1.0, ALU.mult, ALU.add, [kac_b], [omk_b])
            ts('pool', hrk, hrk, 0.5, None, ALU.mult, None, [hrk_b], [hrk_b])
            ts('pool', mus[:, :, 0], mu, -1.0, 1.0, ALU.mult, ALU.add, [mu_b], [mus_b])
            for g in range(6):
                ts('pool', mus[:, :, 1 + g], mu, m6[:, g:g + 1], None, ALU.mult, None, [mu_b, m6_b], [mus_b])
            w2, w2_b = A.alloc([128, 512], BF16)
            a2, a2_b = A.alloc([128, 512], BF16)
            g2, g2_b = A.alloc([128, 512], BF16)
            P.dma('pool', w2, I['rwkv_w2'][l], writes=[w2_b])
            P.dma('pool', a2, I['rwkv_a2'][l], writes=[a2_b])
            P.dma('pool', g2, I['rwkv_g2'][l], writes=[g2_b])
            tw, tw_b = A.alloc([128, TT], BF16)
            za, za_b = A.alloc([128, TT], BF16)
            sg, sg_b = A.alloc([128, TT], BF16)
            base_off = A.off
            zin, zin_b = A.alloc([128, TT], F32)

            def zb(ci, dst, dst_b):
                P.dma('sp', zin, zrw_s[ci], reads=[zrw_b], writes=[zin_b])
                ts('dve', dst, zin, mus[:, ci, 0:1], None, ALU.mult, None, [zin_b, mus_b], [dst_b])
                z3 = zin[:, 0:T].rearrange("p (r c) -> p r c", c=64)
                d3 = dst[:, 0:T].rearrange("p (r c) -> p r c", c=64)
                rb = [zin_b, mus_b, dst_b]
                stt(d3[:, :, 1:64], z3[:, :, 0:63], mus[:, ci, 1:2], d3[:, :, 1:64], ALU.mult, ALU.add, rb, [dst_b])
                stt(d3[:, :, 0:63], z3[:, :, 1:64], mus[:, ci, 2:3], d3[:, :, 0:63], ALU.mult, ALU.add, rb, [dst_b])
                stt(dst[:, 64:T], zin[:, 0:T - 64], mus[:, ci, 3:4], dst[:, 64:T], ALU.mult, ALU.add, rb, [dst_b])
                stt(dst[:, 0:T - 64], zin[:, 64:T], mus[:, ci, 4:5], dst[:, 0:T - 64], ALU.mult, ALU.add, rb, [dst_b])
                stt(dst[:, T + 1:TT], zin[:, T:TT - 1], mus[:, ci, 5:6], dst[:, T + 1:TT], ALU.mult, ALU.add, rb, [dst_b])
                stt(dst[:, T:TT - 1], zin[:, T + 1:TT], mus[:, ci, 6:7], dst[:, T:TT - 1], ALU.mult, ALU.add, rb, [dst_b])

            ztmp, ztmp_b = A.alloc([128, TT], F32)
            zb(12, ztmp, ztmp_b)
            act(tw, ztmp, AF.Tanh, [ztmp_b], [tw_b])
            zb(13, ztmp, ztmp_b)
            cp('act', za, ztmp, [ztmp_b], [za_b])
            zb(14, ztmp, ztmp_b)
            act(sg, ztmp, AF.Sigmoid, [ztmp_b], [sg_b])

            WSTOP = os.environ.get('WSTOP', '')
            if WSTOP == 'pro':
                return
            for j in range(4):
                P.barrier()
                A.off = base_off
                zin, zin_b = A.alloc([128, TT], F32)
                zr, zr_b = A.alloc([128, TT], F32)
                zk, zk_b = A.alloc([128, TT], F32)
                zv, zv_b = A.alloc([128, TT], F32)
                kkn, kkn_b = A.alloc([128, TT], F32)
                ksum, ksum_b = A.alloc([128, TT], F32)
                Lw, Lw_b = A.alloc([128, TT], F32)
                Ic, Ic_b = A.alloc([128, TT], F32)
                Aa, Aa_b = A.alloc([128, TT], F32)
                Tk, Tk_b = A.alloc([128, TT], F32)
                stg = [A.alloc([128, TT], BF16) for _ in range(2)]
                rt, rt_b = A.alloc([128, 512], F32)
                sn = [0]

                def emit(idx, d, fn):
                    so, so_b = stg[sn[0] % 2]
                    sn[0] += 1
                    fn(so, so_b)
                    P.dma('sp', rw_s[j, d, :, idx, :], so, reads=[so_b], writes=[rw_b])
                zb(j, zr, zr_b)
                zb(4 + j, zk, zk_b)
                zb(8 + j, zv, zv_b)
                X, X_b = zin, zin_b
                for d in range(2):
                    emit(6, d, lambda so, so_b: cp('act', so, zv, [zv_b], [so_b]))
                ts('pool', kkn, zk, kkc[:, j:j + 1], None, ALU.mult, None, [zk_b, kkc_b], [kkn_b])
                tt('pool', X, kkn, kkn, ALU.mult, [kkn_b], [X_b])
                for (t0, w) in TILES:
                    ps, psb = psum()
                    mm(ps[:, 0:w], psb, bones_f[:], X[:, t0:t0 + w], True, True, [bones_fb, X_b])
                    act(rt[:, 0:w], ps[:, 0:w], AF.Sqrt, [psb, epsT_b], [rt_b], bias=eps_ap(1e-12))
                    recip(rt[:, 0:w], rt[:, 0:w], [rt_b], [rt_b])
                    tt('pool', kkn[:, t0:t0 + w], kkn[:, t0:t0 + w], rt[:, 0:w], ALU.mult, [kkn_b, rt_b], [kkn_b])
                I3 = Ic.rearrange("p (c t) -> p c t", t=128)
                X3 = X.rearrange("p (c t) -> p c t", t=128)
                L3 = Lw.rearrange("p (c t) -> p c t", t=128)
                totb = I3[:, :, 127:128].broadcast_to([128, NCH, 128])
                for d in range(2):
                    r0, r1 = d * 64, d * 64 + 64
                    for (t0, w) in TILES:
                        ps, psb = psum()
                        mm(ps[:, 0:w], psb, w2[r0:r1, j * 128:(j + 1) * 128], tw[r0:r1, t0:t0 + w], True, True, [w2_b, tw_b])
                        act(Lw[:, t0:t0 + w], ps[:, 0:w], AF.Sigmoid, [psb, w0_b], [Lw_b], bias=w0[:, d, j:j + 1])
                        ps2, ps2b = psum()
                        mm(ps2[:, 0:w], ps2b, a2[r0:r1, j * 128:(j + 1) * 128], za[r0:r1, t0:t0 + w], True, True, [a2_b, za_b])
                        act(Aa[:, t0:t0 + w], ps2[:, 0:w], AF.Sigmoid, [ps2b, a0_b], [Aa_b], bias=a0[:, d, j:j + 1])
                    ts('pool', Lw, Lw, DECAY_C, None, ALU.mult, None, [Lw_b], [Lw_b])
                    for c in range(NCH):
                        P.op('dve', lambda e, c=c: e.tensor_tensor_scan(out=Ic[:, c * 128:(c + 1) * 128], data0=ones_f[:],
                                                                        data1=Lw[:, c * 128:(c + 1) * 128], initial=0.0,
                                                                        op0=ALU.mult, op1=ALU.add),
                             reads=[ones_fb, Lw_b], writes=[Ic_b])
                    ts('dve', Tk, Aa, kac[:, j:j + 1], omk[:, j:j + 1], ALU.mult, ALU.add, [Aa_b, kac_b, omk_b], [Tk_b])
                    tt('pool', Tk, Tk, zk, ALU.mult, [Tk_b, zk_b], [Tk_b])
                    if d == 0:
                        cp('pool', ksum, Tk, [Tk_b], [ksum_b])
                    else:
                        tt('pool', ksum, ksum, Tk, ALU.add, [ksum_b, Tk_b], [ksum_b])
                    tt('pool', Aa, Aa, kkn, ALU.mult, [Aa_b, kkn_b], [Aa_b])
                    act(gC[:, j, d, :], I3[:, :, 127], AF.Exp, [Ic_b], [gC_b])
                    mul = lambda a_, a_b: (lambda so, so_b: tt('pool', so, a_, X, ALU.mult, [a_b, X_b], [so_b]))
                    nmul = lambda a_, a_b: (lambda so, so_b: stt(so, a_, -1.0, X, ALU.mult, ALU.mult, [a_b, X_b], [so_b]))
                    if d == 0:
                        act(X, Ic, AF.Exp, [Ic_b], [X_b], scale=-1.0)
                        emit(1, d, mul(Tk, Tk_b))
                        emit(2, d, mul(Aa, Aa_b))
                        tt('dve', X3, totb, I3, ALU.subtract, [Ic_b], [X_b])
                        act(X, X, AF.Exp, [X_b], [X_b])
                        emit(4, d, mul(Tk, Tk_b))
                        emit(5, d, nmul(Aa, Aa_b))
                        act(X, Ic, AF.Exp, [Ic_b], [X_b])
                        emit(3, d, mul(zr, zr_b))
                        tt('dve', X, Ic, Lw, ALU.subtract, [Ic_b, Lw_b], [X_b])
                        act(X, X, AF.Exp, [X_b], [X_b])
                        emit(0, d, mul(kkn, kkn_b))
                    else:
                        tt('dve', X3, totb, I3, ALU.subtract, [Ic_b], [X_b])
                        act(X, X, AF.Exp, [X_b], [X_b])
                        emit(0, d, mul(kkn, kkn_b))
                        emit(3, d, mul(zr, zr_b))
                        tt('dve', Lw, Ic, Lw, ALU.subtract, [Ic_b, Lw_b], [Lw_b])
                        tt('dve', X3, L3, totb, ALU.subtract, [Ic_b, Lw_b], [X_b])
                        act(X, X, AF.Exp, [X_b], [X_b])
                        emit(1, d, mul(Tk, Tk_b))
                        emit(2, d, mul(Aa, Aa_b))
                        act(X, Lw, AF.Exp, [Lw_b], [X_b])
                        emit(4, d, mul(Tk, Tk_b))
                        emit(5, d, nmul(Aa, Aa_b))
                tt('pool', X, zr, ksum, ALU.mult, [zr_b, ksum_b], [X_b])
                ts('pool', X, X, hrk[:, j:j + 1], None, ALU.mult, None, [X_b, hrk_b], [X_b])
                for (t0, w) in TILES:
                    ps, psb = psum()
                    mm(ps[:, 0:w], psb, bones_f[:], X[:, t0:t0 + w], True, True, [bones_fb, X_b])
                    tt('dve', Lw[:, t0:t0 + w], ps[:, 0:w], zv[:, t0:t0 + w], ALU.mult, [psb, zv_b], [Lw_b])
                    ps2, ps2b = psum()
                    mm(ps2[:, 0:w], ps2b, g2[:, j * 128:(j + 1) * 128], sg[:, t0:t0 + w], True, True, [g2_b, sg_b])
                    cp('act', Ic[:, t0:t0 + w], ps2[:, 0:w], [ps2b], [Ic_b])
                P.dma('sp', bon_s[j, 0], Lw, reads=[Lw_b], writes=[bon_b])
                P.dma('sp', bon_s[j, 1], Ic, reads=[Ic_b], writes=[bon_b])

                if WSTOP == 'w1':
                    return
                P.barrier()
                A.off = base_off
                yacc, yacc_b = A.alloc([128, NCH, 128], F32)
                w3_off = A.off
                GI = 3
                nslot = GI * 2
                lds = [A.alloc([128, 7, 128], BF16) for _ in range(nslot)]
                toks = [A.alloc([128, 3, 128], BF16) for _ in range(nslot)]
                Gn = [A.alloc([128, 2, 128], F32) for _ in range(nslot * 2)]
                Gb = [A.alloc([128, 3, 128], BF16) for _ in range(nslot * 2)]
                XZ = [[A.alloc([128, 2, 128], F32) for _ in range(2)] for _ in range(nslot * 2)]
                PTs = [[A.alloc([128, 128], F32) for _ in range(2)] for _ in range(nslot * 2)]
                r0s = [A.alloc([128, 128], F32) for _ in range(nslot)]
                Us = [A.alloc([128, 128], BF16) for _ in range(nslot)]
                Hf = [A.alloc([128, 128], F32) for _ in range(2)]
                Hb = [A.alloc([128, 128], BF16) for _ in range(2)]
                for d in range(2):
                    memset('pool', Hf[d][0], 0.0, [Hf[d][1]])
                    memset('pool', Hb[d][0], 0.0, [Hb[d][1]])
                ywritten = set()
                ev = [0]
                for g0 in range(0, NCH, GI):
                    items = []
                    for i in range(g0, min(g0 + GI, NCH)):
                        for d in range(2):
                            c = (ORDER_F if d == 0 else ORDER_B)[i]
                            sl = (i - g0) * 2 + d
                            ld, ld_b = lds[sl]
                            tok, tok_b = toks[sl]
                            P.dma('sp', ld, rw_s[j, d, :, :, c * 128:(c + 1) * 128], reads=[rw_b], writes=[ld_b])
                            ps, psb = psum()
                            psv = ps[:].bitcast(BF16)
                            for n_, idx in enumerate((6, 4, 5)):
                                transp(psv[:, n_ * 128:(n_ + 1) * 128], psb, ld[:, idx, :], ident_b[:], [ld_b, ident_bb])
                            cp('act', tok, psv[:, 0:384].rearrange("p (a b) -> p a b", a=3), [psb], [tok_b])
                            mats = []
                            for hp in range(2):
                                r0, r1 = hp * 64, hp * 64 + 64
                                mi = sl * 2 + hp
                                Gn_, Gn_b = Gn[mi]
                                Gb_, Gb_b = Gb[mi]
                                Qt, Kt, Bt, Rt = ld[r0:r1, 0, :], ld[r0:r1, 1, :], ld[r0:r1, 2, :], ld[r0:r1, 3, :]
                                psA, psAb = psum()
                                psB, psBb = psum()
                                mm(psA[:, 0:128], psAb, Qt, Bt, True, True, [ld_b])
                                mm(psA[:, 128:256], psAb, Bt, Qt, True, True, [ld_b])
                                mm(psA[:, 256:384], psAb, Kt, Qt, True, True, [ld_b])
                                mm(psA[:, 384:512], psAb, Kt, Rt, True, True, [ld_b])
                                mm(psB[:, 0:128], psBb, Bt, Rt, True, True, [ld_b])
                                tt('dve', Gn_[:, 0:2, :], psA[:, 0:256].rearrange("p (a b) -> p a b", a=2), rwmask[:, d, 0:2, :], ALU.mult,
                                   [psAb, rwmask_b], [Gn_b])
                                tt('dve', Gb_[:, 0:2, :], psA[:, 256:512].rearrange("p (a b) -> p a b", a=2), rwmask[:, d, 2:4, :], ALU.mult,
                                   [psAb, rwmask_b], [Gb_b])
                                tt('dve', Gb_[:, 2, :], psB[:, 0:128], rwmask[:, d, 4, :], ALU.mult, [psBb, rwmask_b], [Gb_b])
                                tt('pool', PTs[mi][0][0], ident_f[:], Gn_[:, 1, :], ALU.subtract, [ident_fb, Gn_b], [PTs[mi][0][1]])
                                mats.append(mi)
                            items.append((i, d, c, sl, mats))
                    allm = [mi for it in items for mi in it[4]]
                    cur = {mi: (Gn[mi][0][:, 1, :], Gn[mi][0][:, 0, :], Gn[mi][1]) for mi in allm}
                    pcur = {mi: 0 for mi in allm}
                    for lev in range(6):
                        last = lev == 5
                        nxt = {}
                        for mi in allm:
                            Xm, Zm, XZb_ = cur[mi]
                            n_, n_b = XZ[mi][lev % 2]
                            ps, psb = psum()
                            mm(ps[:, 0:128], psb, Xm, Zm, True, True, [XZb_])
                            if not last:
                                mm(ps[:, 128:256], psb, Zm, Xm, True, True, [XZb_])
                            k_ = 1 if last else 2
                            ev[0] += 1
                            cp('act' if ev[0] % 2 else 'dve', n_[:, 0:k_, :], ps[:, 0:k_ * 128].rearrange("p (a b) -> p a b", a=k_), [psb], [n_b])
                            nxt[mi] = (n_[:, 1, :], n_[:, 0, :], n_b)
                        for mi in allm:
                            Z2 = nxt[mi][1]
                            po, po_b = PTs[mi][pcur[mi]]
                            pn, pn_b = PTs[mi][1 - pcur[mi]]
                            ps, psb = psum()
                            mm(ps[:, 0:128], psb, Z2, po, True, True, [nxt[mi][2], po_b])
                            tt('dve', pn, ps[:, 0:128], po, ALU.add, [psb, po_b], [pn_b])
                            pcur[mi] = 1 - pcur[mi]
                        cur = nxt
                    for (i, d, c, sl, mats) in items:
                        ld, ld_b = lds[sl]
                        tok, tok_b = toks[sl]
                        Hb_, Hb_b = Hb[d]
                        Hf_, Hf_b = Hf[d]
                        rb_, rb_b = r0s[sl]
                        U_, U_b = Us[sl]
                        for hp in range(2):
                            r0, r1 = hp * 64, hp * 64 + 64
                            G_, G_b = Gb[mats[hp]]
                            ps, psb = psum()
                            mm(ps[:, 0:64], psb, ld[r0:r1, 0, :], Hb_[r0:r1, r0:r1], True, False, [ld_b, Hb_b])
                            mm(ps[:, 0:64], psb, G_[:, 0, :], tok[:, 0, r0:r1], False, True, [G_b, tok_b])
                            cp('act', rb_[:, r0:r1], ps[:, 0:64], [psb], [rb_b])
                        ps, psb = psum()
                        for hp in range(2):
                            r0, r1 = hp * 64, hp * 64 + 64
                            pt_, pt_b = PTs[mats[hp]][pcur[mats[hp]]]
                            mm(ps[:, r0:r1], psb, pt_, rb_[:, r0:r1], True, True, [pt_b, rb_b])
                        cp('act', U_, ps[:, 0:128], [psb], [U_b])
                        for hp in range(2):
                            r0, r1 = hp * 64, hp * 64 + 64
                            G_, G_b = Gb[mats[hp]]
                            ps, psb = psum()
                            mm(ps[:, 0:64], psb, ld[r0:r1, 3, :], Hb_[r0:r1, r0:r1], True, False, [ld_b, Hb_b])
                            mm(ps[:, 0:64], psb, G_[:, 1, :], tok[:, 0, r0:r1], False, False, [G_b, tok_b])
                            mm(ps[:, 0:64], psb, G_[:, 2, :], U_[:, r0:r1], False, True, [G_b, U_b])
                            if c not in ywritten:
                                cp('dve', yacc[:, c, r0:r1], ps[:, 0:64], [psb], [yacc_b])
                            else:
                                tt('dve', yacc[:, c, r0:r1], ps[:, 0:64], yacc[:, c, r0:r1], ALU.add, [psb, yacc_b], [yacc_b])
                        ywritten.add(c)
                        ps, psb = psum()
                        mm(ps[:, 0:128], psb, tok[:, 1, :], tok[:, 0, :], True, False, [tok_b])
                        mm(ps[:, 0:128], psb, tok[:, 2, :], U_, False, True, [tok_b, U_b])
                        stt(Hf_, Hf_, gC[:, j, d, c:c + 1], ps[:, 0:128], ALU.mult, ALU.add, [Hf_b, gC_b, psb], [Hf_b])
                        cp('pool', Hb_, Hf_, [Hf_b], [Hb_b])

                if WSTOP == 'w2':
                    return
                P.barrier()
                A.off = w3_off
                mn, mn_b = A.alloc([128, 36], F32)
                vr, vr_b = A.alloc([128, 36], F32)
                sqb, sqb_b = A.alloc([128, 36, 64], F32)
                ynb, ynb_b = A.alloc([128, NCH, 128], BF16)
                bon, bon_bb = A.alloc([128, TT], F32)
                gg, gg_b = A.alloc([128, TT], F32)
                tmp, tmp_b = A.alloc([128, 1024], F32)
                ro, ro_b = A.alloc([128, TT], BF16)
                P.dma('sp', bon, bon_s[j, 0], reads=[bon_b], writes=[bon_bb])
                P.dma('sp', gg, bon_s[j, 1], reads=[bon_b], writes=[gg_b])
                y4 = yacc.rearrange("p c (h v) -> p (c h) v", h=2)
                P.op('dve', lambda e: e.tensor_reduce(out=mn, in_=y4, axis=AX.X, op=ALU.add), reads=[yacc_b], writes=[mn_b])
                ts('pool', mn, mn, 1.0 / 64, None, ALU.mult, None, [mn_b], [mn_b])
                tt('dve', y4, y4, mn.unsqueeze(2).broadcast_to([128, 36, 64]), ALU.subtract, [yacc_b, mn_b], [yacc_b])
                tt('pool', sqb, y4, y4, ALU.mult, [yacc_b], [sqb_b])
                P.op('dve', lambda e: e.tensor_reduce(out=vr, in_=sqb, axis=AX.X, op=ALU.add), reads=[sqb_b], writes=[vr_b])
                act(vr, vr, AF.Sqrt, [vr_b, epsT_b], [vr_b], bias=eps_ap(64e-5), scale=1.0 / 64)
                recip(vr, vr, [vr_b], [vr_b])
                tt('dve', ynb.rearrange("p c (h v) -> p (c h) v", h=2), y4, vr.unsqueeze(2).broadcast_to([128, 36, 64]), ALU.mult,
                   [yacc_b, vr_b], [ynb_b])
                for c0 in range(0, NCH, 8):
                    n = min(8, NCH - c0)
                    ps, psb = psum()
                    psv = ps[:].bitcast(BF16)
                    for cc in range(n):
                        transp(psv[:, cc * 128:(cc + 1) * 128], psb, ynb[:, c0 + cc, :], ident_b[:], [ynb_b, ident_bb])
                    cs = slice(c0 * 128, (c0 + n) * 128)
                    stt(tmp[:, 0:n * 128], psv[:, 0:n * 128], ngc[:, j:j + 1], bon[:, cs], ALU.mult, ALU.add, [psb, ngc_b, bon_bb], [tmp_b])
                    tt('pool', ro[:, cs], tmp[:, 0:n * 128], gg[:, cs], ALU.mult, [tmp_b, gg_b], [ro_b])
                P.dma('sp', rwkv_s[j], ro, reads=[ro_b], writes=[rwkv_b])

        def phase_merge(bi, l):
            A.reset()
            last = (l == nlayers - 1)
            tiles = TILES[:4] if last else TILES
            Wa, Wa_b = A.alloc([128, 8, 1024], BF16)
            Wb, Wb_b = A.alloc([128, 4, 1024], BF16)
            Wo, Wo_b = A.alloc([128, 8, 1024], BF16)
            for hf in range(2):
                P.dma('pool', Wa[:, :, hf * 512:(hf + 1) * 512], I['w_branch_a'][l, :, hf * 512:(hf + 1) * 512].rearrange("(k p) n -> p k n", p=128), writes=[Wa_b])
                P.dma('pool', Wb[:, :, hf * 512:(hf + 1) * 512], I['w_branch_b'][l, :, hf * 512:(hf + 1) * 512].rearrange("(k p) n -> p k n", p=128), writes=[Wb_b])
                P.dma('pool', Wo[:, :, hf * 512:(hf + 1) * 512], I['w_out'][l, :, hf * 512:(hf + 1) * 512].rearrange("(k p) n -> p k n", p=128), writes=[Wo_b])
            bufs = []
            for _ in range(2):
                bufs.append((A.alloc([128, 8, 512], BF16), A.alloc([128, 4, 512], BF16), A.alloc([128, 8, 512], BF16), A.alloc([128, 8, 512], BF16)))
            mT, mT_b = A.alloc([128, 8, 512], BF16)
            m1, m1_b = A.alloc([128, 512], F32)
            m2, m2_b = A.alloc([128, 512], F32)
            for ti, (t0, w) in enumerate(tiles):
                (rt_, rt_b), (wt_, wt_b), (ga, ga_b), (gb, gb_b) = bufs[ti % 2]
                r = 2 if t0 >= T else bi
                P.dma('sp', rt_[:, :, 0:w], ret_s[:, :, t0:t0 + w].rearrange("h p t -> p h t"), reads=[ret_b], writes=[rt_b])
                P.dma('sp', wt_[:, :, 0:w], rwkv_s[:, :, t0:t0 + w].rearrange("h p t -> p h t"), reads=[rwkv_b], writes=[wt_b])
                P.dma('sp', ga[:, :, 0:w], gate_s[8:16, :, t0:t0 + w].rearrange("h p t -> p h t"), reads=[gate_b], writes=[ga_b])
                P.dma('sp', gb[:, :, 0:w], gate_s[16:24, :, t0:t0 + w].rearrange("h p t -> p h t"), reads=[gate_b], writes=[gb_b])
                for jo in range(8):
                    ps, psb = psum()
                    for k in range(8):
                        mm(ps[:, 0:w], psb, Wa[:, k, jo * 128:(jo + 1) * 128], rt_[:, k, 0:w], k == 0, k == 7, [Wa_b, rt_b])
                    tt('dve', m1[:, 0:w], ps[:, 0:w], ga[:, jo, 0:w], ALU.mult, [psb, ga_b], [m1_b])
                    ps2, ps2b = psum()
                    for k in range(4):
                        mm(ps2[:, 0:w], ps2b, Wb[:, k, jo * 128:(jo + 1) * 128], wt_[:, k, 0:w], k == 0, k == 3, [Wb_b, wt_b])
                    tt('dve', m2[:, 0:w], ps2[:, 0:w], gb[:, jo, 0:w], ALU.mult, [ps2b, gb_b], [m2_b])
                    tt('pool', mT[:, jo, 0:w], m1[:, 0:w], m2[:, 0:w], ALU.add, [m1_b, m2_b], [mT_b])
                for jo in range(8):
                    ps, psb = psum()
                    for k in range(8):
                        mm(ps[:, 0:w], psb, Wo[:, k, jo * 128:(jo + 1) * 128], mT[:, k, 0:w], k == 0, k == 7, [Wo_b, mT_b])
                    stt(xT[:, jo, t0:t0 + w], ps[:, 0:w], modT[:, l, 16 + jo, r:r + 1], xT[:, jo, t0:t0 + w], ALU.mult, ALU.add,
                        [psb, modT_b, xT_b], [xT_b])

        def phase_ffn(bi, l):
            A.reset()
            last = (l == nlayers - 1)
            sups = [(0, 1024), (1024, 1024)] + ([] if last else [(2048, 256)])
            h2, h2_b = A.alloc([128, 8, 1024], BF16)
            hid, hid_b = A.alloc([128, 22, 1024], BF16)
            sq, sq_b = A.alloc([128, 8, 512], F32)
            rs, rs_b = A.alloc([128, 512], F32)
            tmp, _ = A.alloc([128, 2, 512], F32)
            tmp_b = [Buf(), Buf()]
            wa = [A.alloc([128, 8, 128], BF16) for _ in range(3)]
            wg = [A.alloc([128, 8, 128], BF16) for _ in range(3)]
            w2c = [A.alloc([128, 22, 128], BF16) for _ in range(2)]
            sa = [A.alloc([128, 512], F32) for _ in range(2)]
            si = 0
            for (T0, W) in sups:
                subt = [(t0, min(512, T0 + W - t0)) for t0 in range(T0, T0 + W, 512)]
                norm_mod(lambda k, t0, w: h2[:, k, t0 - T0:t0 - T0 + w], h2_b,
                         lambda k, r: (modA[:, l, 1, k, (bi if r is None else r):(bi if r is None else r) + 1], [modA_b]),
                         lambda k, r: (modT[:, l, 24 + k, (bi if r is None else r):(bi if r is None else r) + 1], [modT_b]),
                         sq, sq_b, rs, rs_b, tmp, tmp_b, 1e-6, subt)
                for jh in range(22):
                    wa_, wa_b = wa[jh % 3]
                    wg_, wg_b = wg[jh % 3]
                    P.dma('pool', wa_, I['ffn_w13'][l, :, jh * 128:(jh + 1) * 128].rearrange("(k p) n -> p k n", p=128), writes=[wa_b])
                    P.dma('pool', wg_, I['ffn_w13'][l, :, FH + jh * 128:FH + (jh + 1) * 128].rearrange("(k p) n -> p k n", p=128), writes=[wg_b])
                    for (t0, w) in subt:
                        o0 = t0 - T0
                        ps, psb = psum()
                        for k in range(8):
                            mm(ps[:, 0:w], psb, wa_[:, k, :], h2[:, k, o0:o0 + w], k == 0, k == 7, [wa_b, h2_b])
                        ps2, ps2b = psum()
                        for k in range(8):
                            mm(ps2[:, 0:w], ps2b, wg_[:, k, :], h2[:, k, o0:o0 + w], k == 0, k == 7, [wg_b, h2_b])
                        sa_, sa_b = sa[si % 2]
                        si += 1
                        act(sa_[:, 0:w], ps[:, 0:w], AF.Silu, [psb], [sa_b])
                        tt('dve', hid[:, jh, o0:o0 + w], ps2[:, 0:w], sa_[:, 0:w], ALU.mult, [ps2b, sa_b], [hid_b])
                for jo in range(8):
                    w2_, w2_b = w2c[jo % 2]
                    for hf in range(2):
                        P.dma('pool', w2_[:, hf * 11:(hf + 1) * 11, :],
                              I['ffn_w2'][l, hf * 1408:(hf + 1) * 1408, jo * 128:(jo + 1) * 128].rearrange("(k p) n -> p k n", p=128), writes=[w2_b])
                    for (t0, w) in subt:
                        o0 = t0 - T0
                        r = 2 if t0 >= T else bi
                        ps, psb = psum()
                        for k in range(22):
                            mm(ps[:, 0:w], psb, w2_[:, k, :], hid[:, k, o0:o0 + w], k == 0, k == 21, [w2_b, hid_b])
                        stt(xT[:, jo, t0:t0 + w], ps[:, 0:w], modT[:, l, 40 + jo, r:r + 1], xT[:, jo, t0:t0 + w], ALU.mult, ALU.add,
                            [psb, modT_b, xT_b], [xT_b])

        def phase_final(bi):
            A.reset()
            yf, yf_b = A.alloc([128, 8, 512], F32)
            sq, sq_b = A.alloc([128, 8, 512], F32)
            rs, rs_b = A.alloc([128, 512], F32)
            tmp, _ = A.alloc([128, 2, 512], F32)
            tmp_b = [Buf(), Buf()]
            ost = [A.alloc([128, D], F32) for _ in range(2)]
            oi = 0
            for (t0, w) in TILES[:4]:
                norm_mod(lambda k, t0_, w_: yf[:, k, 0:w_], yf_b,
                         lambda k, r: (fng[:, k:k + 1], [fng_b]),
                         lambda k, r: (zcol[:, 0:1], [zcol_b]),
                         sq, sq_b, rs, rs_b, tmp, tmp_b, 1e-6, [(t0, w)])
                for cc in range(4):
                    o_, o_b = ost[oi % 2]
                    oi += 1
                    for hf in range(2):
                        ps, psb = psum()
                        for kk in range(4):
                            k = hf * 4 + kk
                            transp(ps[:, kk * 128:(kk + 1) * 128], psb, yf[:, k, cc * 128:(cc + 1) * 128], ident_f[:], [yf_b, ident_fb])
                        cp('act' if hf else 'dve', o_[:, hf * 512:(hf + 1) * 512], ps[:], [psb], [o_b])
                    P.dma('sp', out[bi, t0 + cc * 128:t0 + (cc + 1) * 128, :], o_, reads=[o_b], writes=[out_b])
        for bi in range(nbatch):
            A.reset()
            stg = [A.alloc([128, D], F32) for _ in range(2)]
            for c in range(NCH):
                s_, s_b = stg[c % 2]
                src_ = I['x'][bi, c * 128:(c + 1) * 128, :] if c < 16 else I['ctx'][bi, (c - 16) * 128:(c - 15) * 128, :]
                P.dma('sp', s_[:], src_, writes=[s_b])
                for hf in range(2):
                    ps, psb = psum()
                    for kk in range(4):
                        k = hf * 4 + kk
                        transp(ps[:, kk * 128:(kk + 1) * 128], psb, s_[:, k * 128:(k + 1) * 128], ident_f[:], [s_b, ident_fb])
                    cp('act' if hf else 'dve', xT[:, hf * 4:(hf + 1) * 4, c * 128:(c + 1) * 128],
                       ps[:].rearrange("p (a b) -> p a b", a=4), [psb], [xT_b])
            P.barrier()

            for l in range(nlayers):
                A.reset()
                hT, hT_b = A.alloc([128, 8, TT], BF16)
                sq, sq_b = A.alloc([128, 8, 512], F32)
                rs, rs_b = A.alloc([128, 512], F32)
                tmp, _ = A.alloc([128, 2, 512], F32)
                tmp_b = [Buf(), Buf()]
                norm_mod(lambda k, t0, w: hT[:, k, t0:t0 + w], hT_b,
                         lambda k, r: (modA[:, l, 0, k, (bi if r is None else r):(bi if r is None else r) + 1], [modA_b]),
                         lambda k, r: (modT[:, l, 0 + k, (bi if r is None else r):(bi if r is None else r) + 1], [modT_b]),
                         sq, sq_b, rs, rs_b, tmp, tmp_b, 1e-6, TILES)
                if 'dbg_h' in debug:
                    for k in range(8):
                        P.dma('sp', dbg_h[k], hT[:, k, :], reads=[hT_b], writes=[dbg_hb])
                P.barrier()
                A.off = 8 * TT // 2
                rc, rc_b = A.alloc([128, TT], F32)
                rsn, rsn_b = A.alloc([128, TT], F32)
                P.dma('sp', rc[:], C['ropecos'], writes=[rc_b])
                P.dma('sp', rsn[:], C['ropesin'], writes=[rsn_b])
                wch = [A.alloc([128, 8, 128], BF16) for _ in range(4)]
                wn = [0]

                def loadw(src2d):
                    w_, w_b = wch[wn[0] % 4]
                    wn[0] += 1
                    P.dma('pool', w_, src2d.rearrange("(k p) n -> p k n", p=128), writes=[w_b])
                    return w_, w_b
                hsrc = lambda k, t0, w: hT[:, k, t0:t0 + w]
                stgb = [A.alloc([128, TT], BF16) for _ in range(2)]
                stgf = [A.alloc([128, TT], F32) for _ in range(2)]
                t1, t1_b = A.alloc([128, 512], F32)
                t2, t2_b = A.alloc([128, 512], F32)
                sn = [0]
                for qk in range(2):
                    for j in range(4):
                        c0 = qk * 512 + j * 128
                        w_, w_b = loadw(I['w_in'][l, :, c0:c0 + 128])
                        wr_, wr_b = loadw(I['w_rot'][l, :, c0:c0 + 128])
                        so, so_b = stgb[sn[0] % 2]
                        sn[0] += 1
                        for (t0, w) in TILES:
                            ps, psb = psum()
                            ps2, ps2b = psum()
                            for k in range(8):
                                mm(ps[:, 0:w], psb, w_[:, k, :], hsrc(k, t0, w), k == 0, k == 7, [w_b, hT_b])
                            for k in range(8):
                                mm(ps2[:, 0:w], ps2b, wr_[:, k, :], hsrc(k, t0, w), k == 0, k == 7, [wr_b, hT_b])
                            tt('dve', t1[:, 0:w], ps[:, 0:w], rc[:, t0:t0 + w], ALU.mult, [psb, rc_b], [t1_b])
                            tt('dve', t2[:, 0:w], ps2[:, 0:w], rsn[:, t0:t0 + w], ALU.mult, [ps2b, rsn_b], [t2_b])
                            tt('pool', so[:, t0:t0 + w], t1[:, 0:w], t2[:, 0:w], ALU.add, [t1_b, t2_b], [so_b])
                        P.dma('sp', qk_s[qk * 4 + j], so[:], reads=[so_b], writes=[qk_b])
                for g in range(3):
                    base = [2048, 4992, 6016][g]
                    fn = AF.Silu if g == 0 else AF.Sigmoid
                    for j in range(8):
                        w_, w_b = loadw(I['w_in'][l, :, base + j * 128:base + (j + 1) * 128])
                        so, so_b = stgb[sn[0] % 2]
                        sn[0] += 1
                        dense_fm(w_, w_b, 8, hsrc, hT_b, TILES,
                                 lambda ps, psb, t0, w, so=so, so_b=so_b, fn=fn: act(so[:, t0:t0 + w], ps[:, 0:w], fn, [psb], [so_b]))
                        P.dma('sp', gate_s[g * 8 + j], so[:], reads=[so_b], writes=[gate_b])
                for j in range(15):
                    w_, w_b = loadw(I['w_in'][l, :, 3072 + j * 128:3072 + (j + 1) * 128])
                    so, so_b = stgf[j % 2]
                    dense_fm(w_, w_b, 8, hsrc, hT_b, TILES,
                             lambda ps, psb, t0, w, so=so, so_b=so_b: cp('act', so[:, t0:t0 + w], ps[:, 0:w], [psb], [so_b]))
                    P.dma('sp', zrw_s[j], so[:], reads=[so_b], writes=[zrw_b])
                P.barrier()
                A.off = 8 * TT // 2
                wv = [A.alloc([128, 8, 512], BF16) for _ in range(2)]
                for hf in range(2):
                    P.dma('pool', wv[hf][0], I['w_in'][l, :, 1024 + hf * 512:1024 + (hf + 1) * 512].rearrange("(k p) n -> p k n", p=128),
                          writes=[wv[hf][1]])
                vst = [A.alloc([128, 1024], BF16) for _ in range(2)]
                for c in range(NCH):
                    vo, vo_b = vst[c % 2]
                    for hf in range(2):
                        ps, psb = psum()
                        for k in range(8):
                            mm(ps[:], psb, hT[:, k, c * 128:(c + 1) * 128], wv[hf][0][:, k, :], k == 0, k == 7, [hT_b, wv[hf][1]])
                        cp('act' if hf else 'dve', vo[:, hf * 512:(hf + 1) * 512], ps[:], [psb], [vo_b])
                    P.dma('sp', v_s[c], vo[:], reads=[vo_b], writes=[v_b])
                P.barrier()
                if stop_after == 'A':
                    break

                phase_ret(bi, l)
                P.barrier()
                if stop_after == 'R':
                    break
                phase_rwkv(bi, l)
                P.barrier()
                if stop_after == 'W':
                    break
                phase_merge(bi, l)
                P.barrier()
                if stop_after == 'G':
                    break
                phase_ffn(bi, l)
                P.barrier()
            if 'dbg_x' in debug:
                for k in range(8):
                    P.dma('sp', dbg_x[k], xT[:, k, :], reads=[xT_b], writes=[dbg_xb])
            if stop_after is None:
                phase_final(bi)
            P.barrier()

        P.barrier()
        with nc.Block() as block:
            @block.sync
            def _(e):
                for f in P.stream['sp']:
                    f(e)

            @block.tensor
            def _(e):
                for f in P.stream['pe']:
                    f(e)

            @block.scalar
            def _(e):
                for f in P.stream['act']:
                    f(e)

            @block.vector
            def _(e):
                for f in P.stream['dve']:
                    f(e)

            @block.gpsimd
            def _(e):
                for f in P.stream['pool']:
                    f(e)
    return nc


def prep_inputs(inputs):
    consts = _host_consts()
    f = lambda a: np.ascontiguousarray(np.asarray(a, dtype=np.float32))
    w_in = f(inputs['w_in'])
    perm = np.zeros(1024, np.int64)
    for cidx in range(1024):
        h, i = divmod(cidx % 512, 64)
        ii = i % 32
        partner = i + 16 if ii < 16 else i - 16
        perm[cidx] = (cidx // 512) * 512 + h * 64 + partner
    w_rot = np.ascontiguousarray(w_in[:, :, perm])
    shared = {
        'c_ctx': f(inputs['c_ctx']), 'mod_w': f(inputs['mod_w']), 'mod_b': f(inputs['mod_b']),
        'norm1_g': f(inputs['norm1_g']), 'norm2_g': f(inputs['norm2_g']), 'w_in': w_in, 'w_rot': w_rot,
        'ret_decay': f(inputs['ret_decay']).reshape(NL, 16), 'ret_norm_g': f(inputs['ret_norm_g']),
        'rwkv_mu': f(inputs['rwkv_mu']), 'rwkv_w0': f(inputs['rwkv_w0']),
        'rwkv_w2': f(inputs['rwkv_w2']).reshape(NL, 128, 512), 'rwkv_a0': f(inputs['rwkv_a0']),
        'rwkv_a2': f(inputs['rwkv_a2']).reshape(NL, 128, 512), 'rwkv_g2': f(inputs['rwkv_g2']),
        'rwkv_k_k': f(inputs['rwkv_k_k']), 'rwkv_k_a': f(inputs['rwkv_k_a']),
        'rwkv_r_k': f(inputs['rwkv_r_k']).reshape(NL, 512), 'rwkv_norm_g': f(inputs['rwkv_norm_g']),
        'w_branch_a': f(inputs['w_branch_a']), 'w_branch_b': f(inputs['w_branch_b']), 'w_out': f(inputs['w_out']),
        'ffn_w13': f(inputs['ffn_w13']), 'ffn_w2': f(inputs['ffn_w2']), 'final_norm_g': f(inputs['final_norm_g']),
    }
    shared.update(consts)
    x = f(inputs['x']); c = f(inputs['c']); ctx = f(inputs['ctx'])
    in_maps = []
    for i in range(8):
        m = dict(shared)
        m['x'] = x[i * NB:(i + 1) * NB]
        m['c'] = c[i * NB:(i + 1) * NB]
        m['ctx'] = ctx[i * NB:(i + 1) * NB]
        in_maps.append(m)
    return in_maps


def kernel(**inputs):
    in_maps = prep_inputs(inputs)
    nc = build()
    res = run_bass_kernel_spmd(nc, in_maps, core_ids=list(range(8)))
    return np.concatenate([np.asarray(r['out'], dtype=np.float32) for r in res.results], axis=0)
```

```python
import contextlib
import os
import numpy as np
import ml_dtypes
import concourse.bass as bass
import concourse.mybir as mybir
from concourse.bass_utils import run_bass_kernel_spmd

F32 = mybir.dt.float32
F32R = mybir.dt.float32r
BF16 = mybir.dt.bfloat16
AF = mybir.ActivationFunctionType
ALU = mybir.AluOpType
AX = mybir.AxisListType

D = 1024
T = 2048
TC = 256
TT = 2304
NCH = 18
NL = 2
NB = 2
FH = 2816
TILES = [(0, 512), (512, 512), (1024, 512), (1536, 512), (2048, 256)]
ORDER_F = [16, 17] + list(range(16))
ORDER_B = [17, 16] + list(range(15, -1, -1))
DECAY_C = -0.6065306597126334
EPOCH = 20000
ENGS = ('pe', 'act', 'dve', 'pool', 'sp')


class Buf:
    __slots__ = ('w', 'r')

    def __init__(self):
        self.w = None
        self.r = {}


class Prog:
    def __init__(self, nc, stack):
        self.nc = nc
        self.stack = stack
        self.stream = {e: [] for e in ENGS}
        self.n = {e: 0 for e in ENGS}
        self.seen = {e: {} for e in ENGS}
        self.sems = {}
        self.dcount = {}
        self.dnext = {'sp': 0, 'pool': 0}
        self.nslots = {'sp': 12, 'pool': 6}

    def _sem(self, key):
        if key not in self.sems:
            self.sems[key] = self.stack.enter_context(self.nc.semaphore("s_" + "_".join(str(k) for k in key)))
        return self.sems[key]

    def _deps(self, eng, reads, writes):
        deps = {}

        def add(key, val):
            if deps.get(key, 0) < val:
                deps[key] = val
        for b in reads:
            if b.w is not None:
                add(*b.w)
        for b in writes:
            if b.w is not None:
                add(*b.w)
            for k, v in b.r.items():
                add(k, v)
        waits = []
        for key, val in deps.items():
            if key[0] == 'e' and key[1] == 'pe' and eng == 'pe':
                continue
            if self.seen[eng].get(key, 0) >= val:
                continue
            self.seen[eng][key] = val
            waits.append((self._sem(key), val))
        return waits

    def _mark(self, tok, reads, writes):
        key, val = tok
        for b in reads:
            if b.r.get(key, 0) < val:
                b.r[key] = val
        for b in writes:
            b.w = tok
            b.r = {}

    def op(self, eng, fn, reads=(), writes=()):
        waits = self._deps(eng, reads, writes)
        i = self.n[eng]
        self.n[eng] += 1
        key = ('e', eng, i // EPOCH)
        val = i % EPOCH + 1
        mysem = self._sem(key)

        def run(e, waits=waits, fn=fn, mysem=mysem):
            for s, v in waits:
                e.wait_ge(s, v)
            fn(e).then_inc(mysem, 1)
        self.stream[eng].append(run)
        self._mark((key, val), reads, writes)

    def dma(self, q, out, in_, reads=(), writes=(), slow=False):
        idx = self.dnext[q]
        self.dnext[q] = (idx + 1) % self.nslots[q]
        key = ('d', q, idx)
        cnt = self.dcount.get(key, 0)
        waits = self._deps(q, reads, writes)
        sem = self._sem(key)
        if cnt > 0 and self.seen[q].get(key, 0) < 16 * cnt:
            self.seen[q][key] = 16 * cnt
            waits.append((sem, 16 * cnt))
        self.dcount[key] = cnt + 1

        def run(e, waits=waits, out=out, in_=in_, sem=sem, slow=slow):
            for s, v in waits:
                e.wait_ge(s, v)
            if slow:
                e.dma_start(out=out, in_=in_, allow_slow_non_contiguous=True).then_inc(sem, 16)
            else:
                e.dma_start(out=out, in_=in_).then_inc(sem, 16)
        self.stream[q].append(run)
        self._mark((key, 16 * (cnt + 1)), reads, writes)

    def barrier(self):
        toks = []
        for e in ('pe', 'act', 'dve', 'pool'):
            if self.n[e] > 0:
                i = self.n[e] - 1
                toks.append((e, ('e', e, i // EPOCH), i % EPOCH + 1))
        for key, cnt in self.dcount.items():
            toks.append((None, key, 16 * cnt))
        for eng in ENGS:
            waits = []
            for src, key, val in toks:
                if src == eng:
                    continue
                if self.seen[eng].get(key, 0) >= val:
                    continue
                self.seen[eng][key] = val
                waits.append((self._sem(key), val))
            if waits:
                def run(e, waits=waits):
                    for s, v in waits:
                        e.wait_ge(s, v)
                self.stream[eng].append(run)


def _host_consts():
    c = {}
    c['ident_f'] = np.eye(128, dtype=np.float32)
    c['ident_b'] = np.eye(128, dtype=np.float32).astype(ml_dtypes.bfloat16)
    t = np.arange(T)
    row = (t // 64).astype(np.float32)
    col = (t % 64).astype(np.float32)
    inv = (10000.0 ** (-np.arange(16, dtype=np.float32) / 16)).astype(np.float32)
    cos = np.ones((128, TT), np.float32)
    sin = np.zeros((128, TT), np.float32)
    for p in range(128):
        i = p % 64
        pos = row if i < 32 else col
        ii = i % 32
        f = ii % 16
        ang = (pos * inv[f]).astype(np.float32)
        cos[p, :T] = np.cos(ang)
        sin[p, :T] = -np.sin(ang) if ii < 16 else np.sin(ang)
    c['ropecos'] = cos
    c['ropesin'] = sin
    m6 = np.zeros((128, 6), np.float32)
    for p in range(128):
        m6[p, p % 4] = 1.0
        m6[p, 4 + p % 2] = 1.0
    c['m6'] = m6
    rm = np.ones((128, TT), np.float32)
    rm[:, ::128] = 0.0
    c['rmask'] = rm
    a = np.arange(128)[:, None]
    b = np.arange(128)[None, :]
    Lm = (a > b).astype(np.float32)
    Um = (a < b).astype(np.float32)
    UE = (a <= b).astype(np.float32)
    rw = np.zeros((128, 2, 5, 128), np.float32)
    rw[:, 0] = np.stack([Lm, Um, Um, UE, -UE], 1)
    rw[:, 1] = np.stack([Um, Lm, Lm, Lm, -Lm], 1)
    c['rwmask'] = rw
    retD = np.zeros((128, 2, 128), np.float32)
    retM = np.zeros((128, 2, 128), np.float32)
    retD[:, 0] = np.maximum(b - a, 0)
    retM[:, 0] = (a <= b)
    retD[:, 1] = np.maximum(a - b, 0)
    retM[:, 1] = (a > b)
    c['retD'] = retD
    c['retM'] = retM
    qdt = np.zeros((128, 2, 128), np.float32)
    qdt[:, 0] = (b + 1)
    qdt[:, 1] = (128 - b)
    c['qdt'] = qdt
    kdt = np.zeros((128, 2), np.float32)
    kdt[:, 0] = 127 - np.arange(128)
    kdt[:, 1] = np.arange(128)
    c['kdt'] = kdt
    return c


CONST_SHAPES = {
    'ident_f': ([128, 128], F32), 'ident_b': ([128, 128], BF16), 'ropecos': ([128, TT], F32),
    'ropesin': ([128, TT], F32), 'm6': ([128, 6], F32), 'rmask': ([128, TT], F32),
    'rwmask': ([128, 2, 5, 128], F32), 'retD': ([128, 2, 128], F32), 'retM': ([128, 2, 128], F32),
    'qdt': ([128, 2, 128], F32), 'kdt': ([128, 2], F32),
}

IN_SHAPES = {
    'x': [NB, T, D], 'c': [NB, D], 'ctx': [NB, TC, D], 'c_ctx': [D], 'mod_w': [NL, D, 6 * D],
    'mod_b': [NL, 6 * D], 'norm1_g': [NL, D], 'norm2_g': [NL, D], 'w_in': [NL, D, 7040],
    'w_rot': [NL, D, 1024],
    'ret_decay': [NL, 16], 'ret_norm_g': [NL, D], 'rwkv_mu': [NL, 1920], 'rwkv_w0': [NL, 2, 512],
    'rwkv_w2': [NL, 128, 512], 'rwkv_a0': [NL, 2, 512], 'rwkv_a2': [NL, 128, 512], 'rwkv_g2': [NL, 128, 512],
    'rwkv_k_k': [NL, 512], 'rwkv_k_a': [NL, 512], 'rwkv_r_k': [NL, 512], 'rwkv_norm_g': [NL, 512],
    'w_branch_a': [NL, D, D], 'w_branch_b': [NL, 512, D], 'w_out': [NL, D, D], 'ffn_w13': [NL, D, 2 * FH],
    'ffn_w2': [NL, FH, D], 'final_norm_g': [D],
}


def build(debug=None, nlayers=NL, nbatch=NB, stop_after=None):
    debug = debug or []
    nc = bass.Bass("TRN2", target_bir_lowering=False)
    stack = contextlib.ExitStack()
    with stack:
        P = Prog(nc, stack)
        I = {k: nc.dram_tensor(k, s, F32, kind="ExternalInput").ap() for k, s in IN_SHAPES.items()}
        C = {k: nc.dram_tensor(k, s, dt, kind="ExternalInput").ap() for k, (s, dt) in CONST_SHAPES.items()}
        out = nc.dram_tensor("out", [NB, T, D], F32, kind="ExternalOutput").ap()
        out_b = Buf()

        def scratch(name, shape, dt):
            kind = "ExternalOutput" if name in debug else "Internal"
            return nc.dram_tensor(name, shape, dt, kind=kind).ap(), Buf()
        qk_s, qk_b = scratch("qk_s", [8, 128, TT], BF16)
        v_s, v_b = scratch("v_s", [NCH, 128, 1024], BF16)
        gate_s, gate_b = scratch("gate_s", [24, 128, TT], BF16)
        zrw_s, zrw_b = scratch("zrw_s", [15, 128, TT], F32)
        ret_s, ret_b = scratch("ret_s", [8, 128, TT], BF16)
        rw_s, rw_b = scratch("rw_s", [4, 2, 128, 7, TT], BF16)
        bon_s, bon_b = scratch("bon_s", [4, 2, 128, TT], F32)
        rwkv_s, rwkv_b = scratch("rwkv_s", [4, 128, TT], BF16)
        dbg_h, dbg_hb = scratch("dbg_h", [8, 128, TT], BF16)
        dbg_x, dbg_xb = scratch("dbg_x", [8, 128, TT], F32)

        def sb(name, shape, dt):
            return stack.enter_context(nc.sbuf_tensor(name, shape, dt)), Buf()
        xT, xT_b = sb("xT", [128, 8, TT], F32)
        AW = 31616
        arena, _ = sb("arena", [128, AW], F32)
        cst = {}
        for k in ('ident_f', 'ident_b', 'm6', 'rwmask', 'retD', 'retM', 'qdt', 'kdt'):
            cst[k] = sb("c_" + k, CONST_SHAPES[k][0], CONST_SHAPES[k][1])
        ones_f, ones_fb = sb("ones_f", [128, 128], F32)
        bones_f, bones_fb = sb("bones_f", [128, 128], F32)
        modT, modT_b = sb("modT", [128, NL, 48, 3], F32)
        modA, modA_b = sb("modA", [128, NL, 2, 8, 3], F32)
        gC, gC_b = sb("gC", [128, 4, 2, NCH], F32)
        pst = []
        for i in range(8):
            t_ = stack.enter_context(nc.psum_tensor("ps%d" % i, [128, 512], F32))
            pst.append((t_, Buf()))
        pi = [0]

        def psum():
            i = pi[0]
            pi[0] = (i + 1) % 8
            return pst[i]

        class Arena:
            def __init__(self):
                self.off = 0

            def reset(self):
                self.off = 0

            def alloc(self, shape, dt):
                n = int(np.prod(shape[1:]))
                words = n if dt in (F32, F32R) else (n + 1) // 2
                assert self.off + words <= AW, (self.off, words, AW)
                ap = arena[:, self.off:self.off + words]
                self.off += words
                if dt == F32R:
                    ap = ap.bitcast(F32R)
                if dt == BF16:
                    ap = ap.bitcast(BF16)
                    if n % 2:
                        ap = ap[:, 0:n]
                if len(shape) == 3:
                    ap = ap.rearrange("p (a b) -> p a b", a=shape[1])
                elif len(shape) == 4:
                    ap = ap.rearrange("p (a b c) -> p a b c", a=shape[1], b=shape[2])
                return ap, Buf()
        A = Arena()

        def mm(ps, psb, lhsT, rhs, start, stop, reads):
            P.op('pe', lambda e: e.matmul(ps, lhsT=lhsT, rhs=rhs, start=start, stop=stop),
                 reads=reads, writes=[psb])

        def transp(ps, psb, in_, ident, reads):
            P.op('pe', lambda e: e.transpose(out=ps, in_=in_, identity=ident), reads=reads, writes=[psb])

        def act(out_, in_, func, reads, writes, bias=0.0, scale=1.0):
            P.op('act', lambda e: e.activation(out=out_, in_=in_, func=func, bias=bias, scale=scale),
                 reads=reads, writes=writes)

        def tt(eng, out_, in0, in1, op, reads, writes):
            P.op(eng, lambda e: e.tensor_tensor(out=out_, in0=in0, in1=in1, op=op), reads=reads, writes=writes)

        def ts(eng, out_, in0, s1, s2, op0, op1, reads, writes):
            if op1 is None:
                P.op(eng, lambda e: e.tensor_scalar(out=out_, in0=in0, scalar1=s1, scalar2=None, op0=op0),
                     reads=reads, writes=writes)
            else:
                P.op(eng, lambda e: e.tensor_scalar(out=out_, in0=in0, scalar1=s1, scalar2=s2, op0=op0, op1=op1),
                     reads=reads, writes=writes)

        def stt(out_, in0, scalar, in1, op0, op1, reads, writes):
            P.op('dve', lambda e: e.scalar_tensor_tensor(out=out_, in0=in0, scalar=scalar, in1=in1, op0=op0, op1=op1),
                 reads=reads, writes=writes)

        def cp(eng, out_, in_, reads, writes):
            if eng == 'act':
                P.op('act', lambda e: e.copy(out=out_, in_=in_), reads=reads, writes=writes)
            else:
                P.op(eng, lambda e: e.tensor_copy(out=out_, in_=in_), reads=reads, writes=writes)

        def recip(out_, in_, reads, writes):
            P.op('dve', lambda e: e.reciprocal(out=out_, in_=in_), reads=reads, writes=writes)

        def memset(eng, ap, val, writes):
            P.op(eng, lambda e: e.memset(ap, val), writes=writes)

        for k in cst:
            P.dma('sp', cst[k][0][:], C[k], writes=[cst[k][1]])
        ident_f, ident_fb = cst['ident_f']
        ident_b, ident_bb = cst['ident_b']
        memset('pool', ones_f[:], 1.0, [ones_fb])
        memset('pool', bones_f[:], 0.0, [bones_fb])
        memset('pool', bones_f[0:64, 0:64], 1.0, [bones_fb])
        memset('pool', bones_f[64:128, 64:128], 1.0, [bones_fb])

        A.reset()
        c3, c3_b = A.alloc([128, 8, 3], F32)
        s3, s3_b = A.alloc([128, 8, 3], F32)
        mb, mb_b = A.alloc([128, NL, 48], F32)
        ng, ng_b = A.alloc([128, NL, 2, 8], F32)
        for r in range(NB):
            P.dma('sp', c3[:, :, r], I['c'][r].rearrange("(k p) -> p k", p=128), writes=[c3_b], slow=True)
        P.dma('sp', c3[:, :, 2], I['c_ctx'].rearrange("(k p) -> p k", p=128), writes=[c3_b], slow=True)
        for l in range(NL):
            P.dma('sp', mb[:, l, :], I['mod_b'][l].rearrange("(j p) -> p j", p=128), writes=[mb_b], slow=True)
            P.dma('sp', ng[:, l, 0, :], I['norm1_g'][l].rearrange("(k p) -> p k", p=128), writes=[ng_b], slow=True)
            P.dma('sp', ng[:, l, 1, :], I['norm2_g'][l].rearrange("(k p) -> p k", p=128), writes=[ng_b], slow=True)
        act(s3, c3, AF.Silu, [c3_b], [s3_b])
        wst = [A.alloc([128, 8, 512], F32) for _ in range(2)]
        wi = 0
        for l in range(nlayers):
            for cc in range(12):
                w_, w_b = wst[wi % 2]
                wi += 1
                P.dma('sp', w_, I['mod_w'][l, :, cc * 512:(cc + 1) * 512].rearrange("(k p) n -> p k n", p=128), writes=[w_b])
                for jj in range(4):
                    j = cc * 4 + jj
                    ps, psb = psum()
                    for k in range(8):
                        mm(ps[:, 0:3], psb, w_[:, k, jj * 128:(jj + 1) * 128], s3[:, k, :], k == 0, k == 7, [w_b, s3_b])
                    act(modT[:, l, j, :], ps[:, 0:3], AF.Identity, [psb, mb_b], [modT_b], bias=mb[:, l, j:j + 1])
            for n_ in range(2):
                j0 = 8 if n_ == 0 else 32
                ts('dve', modA[:, l, n_, :, :], modT[:, l, j0:j0 + 8, :], 1.0, None, ALU.add, None, [modT_b], [modA_b])
                tt('dve', modA[:, l, n_, :, :], modA[:, l, n_, :, :], ng[:, l, n_, :].unsqueeze(2).broadcast_to([128, 8, 3]),
                   ALU.mult, [modA_b, ng_b], [modA_b])
        P.barrier()

        def norm_mod(dst, dst_b, Afn, shfn, sq, sq_b, rs, rs_b, tmp, tmp_b, eps, tiles):
            for (t0, w) in tiles:
                for k in range(8):
                    act(sq[:, k, 0:w], xT[:, k, t0:t0 + w], AF.Square, [xT_b], [sq_b])
                ps, psb = psum()
                for k in range(8):
                    mm(ps[:, 0:w], psb, ones_f[:], sq[:, k, 0:w], k == 0, k == 7, [ones_fb, sq_b])
                act(rs[:, 0:w], ps[:, 0:w], AF.Sqrt, [psb], [rs_b], bias=eps_ap(eps), scale=1.0 / D)
                recip(rs[:, 0:w], rs[:, 0:w], [rs_b], [rs_b])
                r = 2 if t0 >= T else None
                for k in range(8):
                    tt('dve', tmp[:, k % 2, 0:w], xT[:, k, t0:t0 + w], rs[:, 0:w], ALU.mult,
                       [xT_b, rs_b], [tmp_b[k % 2]])
                    a_ap, a_bufs = Afn(k, r)
                    s_ap, s_bufs = shfn(k, r)
                    act(dst(k, t0, w), tmp[:, k % 2, 0:w], AF.Identity, [tmp_b[k % 2]] + a_bufs + s_bufs, [dst_b],
                        bias=s_ap, scale=a_ap)

        epsT, epsT_b = sb("epsT", [128, 4], F32)
        memset('pool', epsT[:, 0:1], 1e-6, [epsT_b])
        memset('pool', epsT[:, 1:2], 1e-5 * 64.0, [epsT_b])
        memset('pool', epsT[:, 2:3], 64e-5, [epsT_b])
        memset('pool', epsT[:, 3:4], 1e-12, [epsT_b])
        EPSI = {1e-6: 0, 1e-5 * 64.0: 1, 64e-5: 2, 1e-12: 3}

        def eps_ap(eps):
            i = EPSI[eps]
            return epsT[:, i:i + 1]

        fng, fng_b = sb("fng", [128, 8], F32)
        P.dma('sp', fng[:], I['final_norm_g'].rearrange("(k p) -> p k", p=128), writes=[fng_b], slow=True)
        zcol, zcol_b = sb("zcol", [128, 1], F32)
        memset('pool', zcol[:], 0.0, [zcol_b])

        def dense_fm(w_ap, w_b, kc, src, src_b, tiles, evac):
            for ti, (t0, w) in enumerate(tiles):
                ps, psb = psum()
                for k in range(kc):
                    mm(ps[:, 0:w], psb, w_ap[:, k, :], src(k, t0, w), k == 0, k == kc - 1, [w_b, src_b])
                evac(ps, psb, t0, w)

        def phase_ret(bi, l):
            A.reset()
            lgt, lgt_b = A.alloc([128, 16], F32)
            gcr, gcr_b = A.alloc([128, 16], F32)
            rng, rng_b = A.alloc([128, 8], F32)
            P.dma('sp', lgt, I['ret_decay'][l].partition_broadcast(128), writes=[lgt_b], slow=True)
            P.dma('sp', rng, I['ret_norm_g'][l].rearrange("(h p) -> p h", p=128), writes=[rng_b], slow=True)
            act(lgt, lgt, AF.Exp, [lgt_b], [lgt_b])
            ts('pool', lgt, lgt, -1.0, None, ALU.mult, None, [lgt_b], [lgt_b])
            act(gcr, lgt, AF.Exp, [lgt_b], [gcr_b], scale=128.0)
            retD, retD_b = cst['retD']
            retM, retM_b = cst['retM']
            qdt, qdt_b = cst['qdt']
            kdt, kdt_b = cst['kdt']
            masks, masks_b = A.alloc([128, 8, 128], F32)
            qd, qd_b = A.alloc([128, 16, 128], F32)
            kd, kd_b = A.alloc([128, 16], F32)
            e2, e2_b = A.alloc([128, 128], F32)
            for h in range(8):
                lf = lgt[:, h:h + 1]
                lb = lgt[:, 8 + h:9 + h]
                act(masks[:, h, :], retD[:, 0, :], AF.Exp, [retD_b, lgt_b], [masks_b], scale=lf)
                tt('pool', masks[:, h, :], masks[:, h, :], retM[:, 0, :], ALU.mult, [masks_b, retM_b], [masks_b])
                act(e2, retD[:, 1, :], AF.Exp, [retD_b, lgt_b], [e2_b], scale=lb)
                tt('pool', e2, e2, retM[:, 1, :], ALU.mult, [e2_b, retM_b], [e2_b])
                tt('pool', masks[:, h, :], masks[:, h, :], e2, ALU.add, [masks_b, e2_b], [masks_b])
                act(qd[:, h * 2, :], qdt[:, 0, :], AF.Exp, [qdt_b, lgt_b], [qd_b], scale=lf)
                act(qd[:, h * 2 + 1, :], qdt[:, 1, :], AF.Exp, [qdt_b, lgt_b], [qd_b], scale=lb)
                act(kd[:, h * 2:h * 2 + 1], kdt[:, 0:1], AF.Exp, [kdt_b, lgt_b], [kd_b], scale=lf)
                act(kd[:, h * 2 + 1:h * 2 + 2], kdt[:, 1:2], AF.Exp, [kdt_b, lgt_b], [kd_b], scale=lb)
            base_off = A.off
            for j in range(4):
                P.barrier()
                A.off = base_off
                qT, qT_b = A.alloc([128, NCH, 128], BF16)
                kT, kT_b = A.alloc([128, NCH, 128], BF16)
                vt, vt_b = A.alloc([128, NCH, 256], BF16)
                P.dma('sp', qT, qk_s[j].rearrange("p (c t) -> p c t", c=NCH), reads=[qk_b], writes=[qT_b])
                P.dma('sp', kT, qk_s[4 + j].rearrange("p (c t) -> p c t", c=NCH), reads=[qk_b], writes=[kT_b])
                P.dma('sp', vt, v_s[:, :, j * 256:(j + 1) * 256].rearrange("c p e -> p c e"), reads=[v_b], writes=[vt_b])
                kdp, kdp_b = A.alloc([128, 2, 128], F32)
                for d in range(2):
                    for hp in range(2):
                        h = 2 * j + hp
                        ts('pool', kdp[:, d, hp * 64:(hp + 1) * 64], ones_f[:, 0:64], kd[:, h * 2 + d:h * 2 + d + 1], None,
                           ALU.mult, None, [ones_fb, kd_b], [kdp_b])
                ktd = [A.alloc([128, NCH, 128], BF16) for _ in range(2)]
                for c0 in range(0, NCH, 8):
                    n = min(8, NCH - c0)
                    ps, psb = psum()
                    psv = ps[:].bitcast(BF16)
                    for cc in range(n):
                        transp(psv[:, cc * 128:(cc + 1) * 128], psb, kT[:, c0 + cc, :], ident_b[:], [kT_b, ident_bb])
                    for d in range(2):
                        tt('dve', ktd[d][0][:, c0:c0 + n, :], psv[:, 0:n * 128].rearrange("p (c t) -> p c t", c=n),
                           kdp[:, d, :].unsqueeze(1).broadcast_to([128, n, 128]), ALU.mult, [psb, kdp_b], [ktd[d][1]])
                KV = [A.alloc([128, NCH, 128], F32) for _ in range(2)]
                Sbf = [A.alloc([128, NCH, 128], BF16) for _ in range(2)]
                Srun = [[A.alloc([128, 128], F32) for _ in range(2)] for _ in range(2)]
                qfb = [A.alloc([128, NCH, 128], BF16) for _ in range(2)]
                att = [A.alloc([128, 4, 128], BF16) for _ in range(2)]
                hn = [[A.alloc([128, 512], F32) for _ in range(5)] for _ in range(2)]
                gts = [A.alloc([128, 512], BF16) for _ in range(2)]
                ros = [A.alloc([128, TT], BF16) for _ in range(2)]
                ai = 0
                for hp in range(2):
                    h = 2 * j + hp
                    r0, r1 = hp * 64, hp * 64 + 64
                    for d in range(2):
                        for (t0, w) in TILES:
                            c0, n = t0 // 128, w // 128
                            ps, psb = psum()
                            for cc in range(n):
                                mm(ps[:, cc * 128:(cc + 1) * 128], psb, ktd[d][0][:, c0 + cc, :], vt[:, c0 + cc, hp * 128:(hp + 1) * 128],
                                   True, True, [ktd[d][1], vt_b])
                            cp('act' if d else 'dve', KV[d][0][:, c0:c0 + n, :], ps[:, 0:w].rearrange("p (c t) -> p c t", c=n), [psb], [KV[d][1]])
                        memset('pool', Srun[d][0][0][:], 0.0, [Srun[d][0][1]])
                        for ci_, c in enumerate(ORDER_F if d == 0 else ORDER_B):
                            S_, S_b = Srun[d][ci_ % 2]
                            Sn_, Sn_b = Srun[d][(ci_ + 1) % 2]
                            cp('act', Sbf[d][0][r0:r1, c, :], S_[r0:r1, :], [S_b], [Sbf[d][1]])
                            stt(Sn_[r0:r1, :], S_[r0:r1, :], gcr[r0:r1, d * 8 + h:d * 8 + h + 1], KV[d][0][r0:r1, c, :], ALU.mult, ALU.add,
                                [S_b, gcr_b, KV[d][1]], [Sn_b])
                        tt('dve', qfb[d][0][r0:r1], qT[r0:r1], qd[r0:r1, h * 2 + d, :].unsqueeze(1).broadcast_to([64, NCH, 128]),
                           ALU.mult, [qT_b, qd_b], [qfb[d][1]])
                    ro, ro_b = ros[hp]
                    for ti, (t0, w) in enumerate(TILES):
                        c0, n = t0 // 128, w // 128
                        at_, at_b = att[ai % 2]
                        gt_, gt_b = gts[ai % 2]
                        (ys, ys_b), (yc, yc_b), (sq, sq_b), (rs, rs_b), (yn, yn_b) = hn[ai % 2]
                        ai += 1
                        P.dma('sp', gt_[:, 0:w], gate_s[h][:, t0:t0 + w], reads=[gate_b], writes=[gt_b])
                        ps, psb = psum()
                        for cc in range(n):
                            mm(ps[:, cc * 128:(cc + 1) * 128], psb, kT[r0:r1, c0 + cc, :], qT[r0:r1, c0 + cc, :], True, True, [kT_b, qT_b])
                        tt('dve', at_[:, 0:n, :], ps[:, 0:w].rearrange("p (c t) -> p c t", c=n),
                           masks[:, h, :].unsqueeze(1).broadcast_to([128, n, 128]), ALU.mult, [psb, masks_b], [at_b])
                        ps2, ps2b = psum()
                        for cc in range(n):
                            c = c0 + cc
                            o_ = ps2[:, cc * 128:(cc + 1) * 128]
                            mm(o_, ps2b, vt[:, c, hp * 128:(hp + 1) * 128], at_[:, cc, :], True, False, [vt_b, at_b])
                            mm(o_, ps2b, Sbf[0][0][r0:r1, c, :], qfb[0][0][r0:r1, c, :], False, False, [Sbf[0][1], qfb[0][1]])
                            mm(o_, ps2b, Sbf[1][0][r0:r1, c, :], qfb[1][0][r0:r1, c, :], False, True, [Sbf[1][1], qfb[1][1]])
                        cp('act', ys[:, 0:w], ps2[:, 0:w], [ps2b], [ys_b])
                        ps3, ps3b = psum()
                        mm(ps3[:, 0:w], ps3b, ones_f[:], ys[:, 0:w], True, True, [ones_fb, ys_b])
                        stt(yc[:, 0:w], ps3[:, 0:w], -1.0 / 128, ys[:, 0:w], ALU.mult, ALU.add, [ps3b, ys_b], [yc_b])
                        act(sq[:, 0:w], yc[:, 0:w], AF.Square, [yc_b], [sq_b])
                        ps4, ps4b = psum()
                        mm(ps4[:, 0:w], ps4b, ones_f[:], sq[:, 0:w], True, True, [ones_fb, sq_b])
                        act(rs[:, 0:w], ps4[:, 0:w], AF.Sqrt, [ps4b, epsT_b], [rs_b], bias=eps_ap(1e-5 * 64.0), scale=1.0 / 128)
                        recip(rs[:, 0:w], rs[:, 0:w], [rs_b], [rs_b])
                        tt('dve', yn[:, 0:w], yc[:, 0:w], rs[:, 0:w], ALU.mult, [yc_b, rs_b], [yn_b])
                        stt(ro[:, t0:t0 + w], yn[:, 0:w], rng[:, h:h + 1], gt_[:, 0:w], ALU.mult, ALU.mult, [yn_b, rng_b, gt_b], [ro_b])
                    P.dma('sp', ret_s[h], ro[:], reads=[ro_b], writes=[ret_b])

        def phase_rwkv(bi, l):
            A.reset()
            rwmask, rwmask_b = cst['rwmask']
            m6, m6_b = cst['m6']
            mu, mu_b = A.alloc([128, 15], F32)
            mus, mus_b = A.alloc([128, 15, 7], F32)
            w0, w0_b = A.alloc([128, 2, 4], F32)
            a0, a0_b = A.alloc([128, 2, 4], F32)
            kkc, kkc_b = A.alloc([128, 4], F32)
            kac, kac_b = A.alloc([128, 4], F32)
            omk, omk_b = A.alloc([128, 4], F32)
            hrk, hrk_b = A.alloc([128, 4], F32)
            ngc, ngc_b = A.alloc([128, 4], F32)
            P.dma('sp', mu, I['rwkv_mu'][l].rearrange("(j p) -> p j", p=128), writes=[mu_b], slow=True)
            for d in range(2):
                P.dma('sp', w0[:, d, :], I['rwkv_w0'][l, d].rearrange("(j p) -> p j", p=128), writes=[w0_b], slow=True)
                P.dma('sp', a0[:, d, :], I['rwkv_a0'][l, d].rearrange("(j p) -> p j", p=128), writes=[a0_b], slow=True)
            P.dma('sp', kkc, I['rwkv_k_k'][l].rearrange("(j p) -> p j", p=128), writes=[kkc_b], slow=True)
            P.dma('sp', kac, I['rwkv_k_a'][l].rearrange("(j p) -> p j", p=128), writes=[kac_b], slow=True)
            P.dma('sp', hrk, I['rwkv_r_k'][l].rearrange("(j p) -> p j", p=128), writes=[hrk_b], slow=True)
            P.dma('sp', ngc, I['rwkv_norm_g'][l].rearrange("(j p) -> p j", p=128), writes=[ngc_b], slow=True)
            ts('pool', omk, kac, -1.0, 1.0, ALU.mult, ALU.add, [kac_b], [omk_b])
            ts('pool', hrk, hrk, 0.5, None, ALU.mult, None, [hrk_b], [hrk_b])
            ts('pool', mus[:, :, 0], mu, -1.0, 1.0, ALU.mult, ALU.add, [mu_b], [mus_b])
            for g in range(6):
                ts('pool', mus[:, :, 1 + g], mu, m6[:, g:g + 1], None, ALU.mult, None, [mu_b, m6_b], [mus_b])
            w2, w2_b = A.alloc([128, 512], BF16)
            a2, a2_b = A.alloc([128, 512], BF16)
            g2, g2_b = A.alloc([128, 512], BF16)
            P.dma('pool', w2, I['rwkv_w2'][l], writes=[w2_b])
            P.dma('pool', a2, I['rwkv_a2'][l], writes=[a2_b])
            P.dma('pool', g2, I['rwkv_g2'][l], writes=[g2_b])
            tw, tw_b = A.alloc([128, TT], BF16)
            za, za_b = A.alloc([128, TT], BF16)
            sg, sg_b = A.alloc([128, TT], BF16)
            base_off = A.off
            zin, zin_b = A.alloc([128, TT], F32)

            def zb(ci, dst, dst_b):
                P.dma('sp', zin, zrw_s[ci], reads=[zrw_b], writes=[zin_b])
                act(dst, zin, AF.Identity, [zin_b, mus_b], [dst_b], scale=mus[:, ci, 0:1])
                z3 = zin[:, 0:T].rearrange("p (r c) -> p r c", c=64)
                d3 = dst[:, 0:T].rearrange("p (r c) -> p r c", c=64)
                rb = [zin_b, mus_b, dst_b]
                stt(d3[:, :, 1:64], z3[:, :, 0:63], mus[:, ci, 1:2], d3[:, :, 1:64], ALU.mult, ALU.add, rb, [dst_b])
                stt(d3[:, :, 0:63], z3[:, :, 1:64], mus[:, ci, 2:3], d3[:, :, 0:63], ALU.mult, ALU.add, rb, [dst_b])
                stt(dst[:, 64:T], zin[:, 0:T - 64], mus[:, ci, 3:4], dst[:, 64:T], ALU.mult, ALU.add, rb, [dst_b])
                stt(dst[:, 0:T - 64], zin[:, 64:T], mus[:, ci, 4:5], dst[:, 0:T - 64], ALU.mult, ALU.add, rb, [dst_b])
                stt(dst[:, T + 1:TT], zin[:, T:TT - 1], mus[:, ci, 5:6], dst[:, T + 1:TT], ALU.mult, ALU.add, rb, [dst_b])
                stt(dst[:, T:TT - 1], zin[:, T + 1:TT], mus[:, ci, 6:7], dst[:, T:TT - 1], ALU.mult, ALU.add, rb, [dst_b])

            ztmp, ztmp_b = A.alloc([128, TT], F32)
            zb(12, ztmp, ztmp_b)
            act(tw, ztmp, AF.Tanh, [ztmp_b], [tw_b])
            zb(13, ztmp, ztmp_b)
            cp('act', za, ztmp, [ztmp_b], [za_b])
            zb(14, ztmp, ztmp_b)
            act(sg, ztmp, AF.Sigmoid, [ztmp_b], [sg_b])

            WSTOP = os.environ.get('WSTOP', '')
            if WSTOP == 'pro':
                return
            for j in range(4):
                P.barrier()
                A.off = base_off
                zin, zin_b = A.alloc([128, TT], F32)
                zr, zr_b = A.alloc([128, TT], F32)
                zk, zk_b = A.alloc([128, TT], F32)
                zv, zv_b = A.alloc([128, TT], F32)
                kkn, kkn_b = A.alloc([128, TT], F32)
                ksum, ksum_b = A.alloc([128, TT], F32)
                Lw, Lw_b = A.alloc([128, TT], F32)
                Ic, Ic_b = A.alloc([128, TT], F32)
                Aa, Aa_b = A.alloc([128, TT], F32)
                Tk, Tk_b = A.alloc([128, TT], F32)
                stg = [A.alloc([128, TT], BF16) for _ in range(2)]
                rt, rt_b = A.alloc([128, 512], F32)
                sn = [0]

                def emit(idx, d, fn):
                    so, so_b = stg[sn[0] % 2]
                    sn[0] += 1
                    fn(so, so_b)
                    P.dma('sp', rw_s[j, d, :, idx, :], so, reads=[so_b], writes=[rw_b])
                zb(j, zr, zr_b)
                zb(4 + j, zk, zk_b)
                zb(8 + j, zv, zv_b)
                X, X_b = zin, zin_b
                for d in range(2):
                    emit(6, d, lambda so, so_b: cp('act', so, zv, [zv_b], [so_b]))
                act(kkn, zk, AF.Identity, [zk_b, kkc_b], [kkn_b], scale=kkc[:, j:j + 1])
                act(X, kkn, AF.Square, [kkn_b], [X_b])
                for (t0, w) in TILES:
                    ps, psb = psum()
                    mm(ps[:, 0:w], psb, bones_f[:], X[:, t0:t0 + w], True, True, [bones_fb, X_b])
                    act(rt[:, 0:w], ps[:, 0:w], AF.Sqrt, [psb, epsT_b], [rt_b], bias=eps_ap(1e-12))
                    recip(rt[:, 0:w], rt[:, 0:w], [rt_b], [rt_b])
                    tt('dve', kkn[:, t0:t0 + w], kkn[:, t0:t0 + w], rt[:, 0:w], ALU.mult, [kkn_b, rt_b], [kkn_b])
                I3 = Ic.rearrange("p (c t) -> p c t", t=128)
                X3 = X.rearrange("p (c t) -> p c t", t=128)
                L3 = Lw.rearrange("p (c t) -> p c t", t=128)
                totb = I3[:, :, 127:128].broadcast_to([128, NCH, 128])
                for d in range(2):
                    r0, r1 = d * 64, d * 64 + 64
                    for (t0, w) in TILES:
                        ps, psb = psum()
                        mm(ps[:, 0:w], psb, w2[r0:r1, j * 128:(j + 1) * 128], tw[r0:r1, t0:t0 + w], True, True, [w2_b, tw_b])
                        act(Lw[:, t0:t0 + w], ps[:, 0:w], AF.Sigmoid, [psb, w0_b], [Lw_b], bias=w0[:, d, j:j + 1])
                        ps2, ps2b = psum()
                        mm(ps2[:, 0:w], ps2b, a2[r0:r1, j * 128:(j + 1) * 128], za[r0:r1, t0:t0 + w], True, True, [a2_b, za_b])
                        act(Aa[:, t0:t0 + w], ps2[:, 0:w], AF.Sigmoid, [ps2b, a0_b], [Aa_b], bias=a0[:, d, j:j + 1])
                    act(Lw, Lw, AF.Identity, [Lw_b], [Lw_b], scale=DECAY_C)
                    for c in range(NCH):
                        P.op('dve', lambda e, c=c: e.tensor_tensor_scan(out=Ic[:, c * 128:(c + 1) * 128], data0=ones_f[:],
                                                                        data1=Lw[:, c * 128:(c + 1) * 128], initial=0.0,
                                                                        op0=ALU.mult, op1=ALU.add),
                             reads=[ones_fb, Lw_b], writes=[Ic_b])
                    act(Tk, Aa, AF.Identity, [Aa_b, kac_b, omk_b], [Tk_b], bias=omk[:, j:j + 1], scale=kac[:, j:j + 1])
                    tt('dve', Tk, Tk, zk, ALU.mult, [Tk_b, zk_b], [Tk_b])
                    if d == 0:
                        cp('act', ksum, Tk, [Tk_b], [ksum_b])
                    else:
                        tt('dve', ksum, ksum, Tk, ALU.add, [ksum_b, Tk_b], [ksum_b])
                    tt('dve', Aa, Aa, kkn, ALU.mult, [Aa_b, kkn_b], [Aa_b])
                    act(gC[:, j, d, :], I3[:, :, 127], AF.Exp, [Ic_b], [gC_b])
                    mul = lambda a_, a_b: (lambda so, so_b: tt('dve', so, a_, X, ALU.mult, [a_b, X_b], [so_b]))
                    nmul = lambda a_, a_b: (lambda so, so_b: stt(so, a_, -1.0, X, ALU.mult, ALU.mult, [a_b, X_b], [so_b]))
                    if d == 0:
                        act(X, Ic, AF.Exp, [Ic_b], [X_b], scale=-1.0)
                        emit(1, d, mul(Tk, Tk_b))
                        emit(2, d, mul(Aa, Aa_b))
                        tt('dve', X3, totb, I3, ALU.subtract, [Ic_b], [X_b])
                        act(X, X, AF.Exp, [X_b], [X_b])
                        emit(4, d, mul(Tk, Tk_b))
                        emit(5, d, nmul(Aa, Aa_b))
                        act(X, Ic, AF.Exp, [Ic_b], [X_b])
                        emit(3, d, mul(zr, zr_b))
                        tt('dve', X, Ic, Lw, ALU.subtract, [Ic_b, Lw_b], [X_b])
                        act(X, X, AF.Exp, [X_b], [X_b])
                        emit(0, d, mul(kkn, kkn_b))
                    else:
                        tt('dve', X3, totb, I3, ALU.subtract, [Ic_b], [X_b])
                        act(X, X, AF.Exp, [X_b], [X_b])
                        emit(0, d, mul(kkn, kkn_b))
                        emit(3, d, mul(zr, zr_b))
                        tt('dve', Lw, Ic, Lw, ALU.subtract, [Ic_b, Lw_b], [Lw_b])
                        tt('dve', X3, L3, totb, ALU.subtract, [Ic_b, Lw_b], [X_b])
                        act(X, X, AF.Exp, [X_b], [X_b])
                        emit(1, d, mul(Tk, Tk_b))
                        emit(2, d, mul(Aa, Aa_b))
                        act(X, Lw, AF.Exp, [Lw_b], [X_b])
                        emit(4, d, mul(Tk, Tk_b))
                        emit(5, d, nmul(Aa, Aa_b))
                tt('dve', X, zr, ksum, ALU.mult, [zr_b, ksum_b], [X_b])
                act(X, X, AF.Identity, [X_b, hrk_b], [X_b], scale=hrk[:, j:j + 1])
                for (t0, w) in TILES:
                    ps, psb = psum()
                    mm(ps[:, 0:w], psb, bones_f[:], X[:, t0:t0 + w], True, True, [bones_fb, X_b])
                    tt('dve', Lw[:, t0:t0 + w], ps[:, 0:w], zv[:, t0:t0 + w], ALU.mult, [psb, zv_b], [Lw_b])
                    ps2, ps2b = psum()
                    mm(ps2[:, 0:w], ps2b, g2[:, j * 128:(j + 1) * 128], sg[:, t0:t0 + w], True, True, [g2_b, sg_b])
                    cp('act', Ic[:, t0:t0 + w], ps2[:, 0:w], [ps2b], [Ic_b])
                P.dma('sp', bon_s[j, 0], Lw, reads=[Lw_b], writes=[bon_b])
                P.dma('sp', bon_s[j, 1], Ic, reads=[Ic_b], writes=[bon_b])

                if WSTOP == 'w1':
                    return
                P.barrier()
                A.off = base_off
                yacc, _ = A.alloc([128, NCH, 128], F32)
                yacc_bs = [Buf() for _ in range(NCH)]
                w3_off = A.off
                GI = 2
                nslot = GI * 2
                NHI = int(os.environ.get('NHI', '6'))

                def alloc_set():
                    st = {}
                    st['ld'] = [A.alloc([128, 7, 128], BF16) for _ in range(nslot)]
                    st['tok'] = [A.alloc([128, 3, 128], BF16) for _ in range(nslot)]
                    st['Gn'] = [A.alloc([128, 2, 128], F32) for _ in range(nslot * 2)]
                    st['Gb'] = [A.alloc([128, 3, 128], BF16) for _ in range(nslot * 2)]
                    st['XZ'] = [[A.alloc([128, 2, 128], F32), st['Gn'][m_]] for m_ in range(nslot * 2)]
                    st['XZb'] = [[A.alloc([128, 2, 128], BF16) for _ in range(2)] for _ in range(nslot * 2)]
                    st['PT'] = [A.alloc([128, 128], F32) for _ in range(nslot * 2)]
                    st['PTb'] = [A.alloc([128, 128], BF16) for _ in range(nslot * 2)]
                    st['r0'] = [A.alloc([128, 128], BF16) for _ in range(nslot)]
                    st['U'] = st['r0']
                    return st
                sets = [alloc_set(), alloc_set()]
                Hf = [A.alloc([128, 128], F32) for _ in range(2)]
                Hb = [A.alloc([128, 128], BF16) for _ in range(2)]
                for d in range(2):
                    memset('pool', Hf[d][0], 0.0, [Hf[d][1]])
                    memset('pool', Hb[d][0], 0.0, [Hb[d][1]])
                ywritten = set()
                ngroups = NCH // GI

                def prep(g, st):
                    items = []
                    for i in range(g * GI, (g + 1) * GI):
                        for d in range(2):
                            c = (ORDER_F if d == 0 else ORDER_B)[i]
                            sl = (i - g * GI) * 2 + d
                            ld, ld_b = st['ld'][sl]
                            tok, tok_b = st['tok'][sl]
                            P.dma('sp', ld, rw_s[j, d, :, :, c * 128:(c + 1) * 128], reads=[rw_b], writes=[ld_b])
                            ps, psb = psum()
                            psv = ps[:].bitcast(BF16)
                            for n_, idx in enumerate((6, 4, 5)):
                                transp(psv[:, n_ * 128:(n_ + 1) * 128], psb, ld[:, idx, :], ident_b[:], [ld_b, ident_bb])
                            cp('act', tok, psv[:, 0:384].rearrange("p (a b) -> p a b", a=3), [psb], [tok_b])
                            mats = []
                            for hp in range(2):
                                r0, r1 = hp * 64, hp * 64 + 64
                                mi = sl * 2 + hp
                                Gn_, Gn_b = st['Gn'][mi]
                                Gb_, Gb_b = st['Gb'][mi]
                                Qt, Kt, Bt, Rt = ld[r0:r1, 0, :], ld[r0:r1, 1, :], ld[r0:r1, 2, :], ld[r0:r1, 3, :]
                                psA, psAb = psum()
                                psB, psBb = psum()
                                mm(psA[:, 0:128], psAb, Qt, Bt, True, True, [ld_b])
                                mm(psA[:, 128:256], psAb, Bt, Qt, True, True, [ld_b])
                                mm(psA[:, 256:384], psAb, Kt, Qt, True, True, [ld_b])
                                mm(psA[:, 384:512], psAb, Kt, Rt, True, True, [ld_b])
                                mm(psB[:, 0:128], psBb, Bt, Rt, True, True, [ld_b])
                                tt('dve', Gn_[:, 0:2, :], psA[:, 0:256].rearrange("p (a b) -> p a b", a=2), rwmask[:, d, 0:2, :], ALU.mult,
                                   [psAb, rwmask_b], [Gn_b])
                                tt('dve', Gb_[:, 0:2, :], psA[:, 256:512].rearrange("p (a b) -> p a b", a=2), rwmask[:, d, 2:4, :], ALU.mult,
                                   [psAb, rwmask_b], [Gb_b])
                                tt('dve', Gb_[:, 2, :], psB[:, 0:128], rwmask[:, d, 4, :], ALU.mult, [psBb, rwmask_b], [Gb_b])
                                tt('pool', st['PT'][mi][0], ident_f[:], Gn_[:, 1, :], ALU.subtract, [ident_fb, Gn_b], [st['PT'][mi][1]])
                                mats.append(mi)
                            items.append((i, d, c, sl, mats))
                    return items

                def inv_slots(st, items):
                    allm = [mi for it in items for mi in it[4]]
                    state = {'cur': {mi: (st['Gn'][mi][0][:, 1, :], st['Gn'][mi][0][:, 0, :], st['Gn'][mi][1]) for mi in allm}, 'nxt': None}
                    slots = []
                    for lev in range(6):
                        last = lev == 5
                        hi = lev < NHI
                        nxt_hi = (lev + 1) < NHI

                        def sq(lev=lev, last=last, hi=hi, nxt_hi=nxt_hi):
                            nxt = {}
                            for n_i, mi in enumerate(allm):
                                Xm, Zm, XZb_ = state['cur'][mi]
                                ps, psb = psum()
                                mm(ps[:, 0:128], psb, Xm, Zm, True, True, [XZb_])
                                if not last:
                                    mm(ps[:, 128:256], psb, Zm, Xm, True, True, [XZb_])
                                k_ = 1 if last else 2
                                src = ps[:, 0:k_ * 128].rearrange("p (a b) -> p a b", a=k_)
                                e1, e2 = ('act', 'dve') if n_i % 2 else ('dve', 'act')
                                if hi:
                                    nf, nf_b = st['XZ'][mi][lev % 2]
                                    cp(e1, nf[:, 0:k_, :], src, [psb], [nf_b])
                                    zP = (nf[:, 0, :], nf_b)
                                    if nxt_hi:
                                        nxt[mi] = (nf[:, 1, :], nf[:, 0, :], nf_b, zP)
                                    else:
                                        nb, nb_b = st['XZb'][mi][lev % 2]
                                        cp('pool', nb[:, 0:k_, :], nf[:, 0:k_, :], [nf_b], [nb_b])
                                        nxt[mi] = (nb[:, 1, :], nb[:, 0, :], nb_b, zP)
                                else:
                                    nb, nb_b = st['XZb'][mi][lev % 2]
                                    cp(e1, nb[:, 0:k_, :], src, [psb], [nb_b])
                                    nxt[mi] = (nb[:, 1, :], nb[:, 0, :], nb_b, (nb[:, 0, :], nb_b))
                            state['nxt'] = nxt

                        def pu(lev=lev, hi=hi, nxt_hi=nxt_hi):
                            nxt = state['nxt']
                            for mi in allm:
                                Z2, Z2_b = nxt[mi][3]
                                pf, pf_b = st['PT'][mi]
                                pb, pb_b = st['PTb'][mi]
                                ps, psb = psum()
                                if hi:
                                    mm(ps[:, 0:128], psb, Z2, pf, True, True, [Z2_b, pf_b])
                                    if nxt_hi:
                                        tt('dve', pf, ps[:, 0:128], pf, ALU.add, [psb, pf_b], [pf_b])
                                    else:
                                        tt('dve', pb, ps[:, 0:128], pf, ALU.add, [psb, pf_b], [pb_b])
                                else:
                                    mm(ps[:, 0:128], psb, Z2, pb, True, True, [Z2_b, pb_b])
                                    tt('dve', pb, ps[:, 0:128], pb, ALU.add, [psb, pb_b], [pb_b])
                            state['cur'] = {mi: nxt[mi][0:3] for mi in allm}
                        slots.append(sq)
                        slots.append(pu)
                    return slots

                def seq_stages(st, items, d):
                    stages = []
                    for (i, d_, c, sl, mats) in items:
                        if d_ != d:
                            continue
                        ld, ld_b = st['ld'][sl]
                        tok, tok_b = st['tok'][sl]
                        Hb_, Hb_b = Hb[d]
                        Hf_, Hf_b = Hf[d]
                        rb_, rb_b = st['r0'][sl]
                        U_, U_b = st['U'][sl]

                        def s1(ld=ld, ld_b=ld_b, tok=tok, tok_b=tok_b, Hb_=Hb_, Hb_b=Hb_b, rb_=rb_, rb_b=rb_b, mats=mats):
                            for hp in range(2):
                                r0, r1 = hp * 64, hp * 64 + 64
                                G_, G_b = st['Gb'][mats[hp]]
                                ps, psb = psum()
                                mm(ps[:, 0:64], psb, ld[r0:r1, 0, :], Hb_[r0:r1, r0:r1], True, False, [ld_b, Hb_b])
                                mm(ps[:, 0:64], psb, G_[:, 0, :], tok[:, 0, r0:r1], False, True, [G_b, tok_b])
                                cp('act', rb_[:, r0:r1], ps[:, 0:64], [psb], [rb_b])

                        def s2(rb_=rb_, rb_b=rb_b, U_=U_, U_b=U_b, mats=mats):
                            ps, psb = psum()
                            for hp in range(2):
                                r0, r1 = hp * 64, hp * 64 + 64
                                pt_, pt_b = st['PTb'][mats[hp]]
                                mm(ps[:, r0:r1], psb, pt_, rb_[:, r0:r1], True, True, [pt_b, rb_b])
                            cp('act', U_, ps[:, 0:128], [psb], [U_b])

                        def s3(ld=ld, ld_b=ld_b, tok=tok, tok_b=tok_b, Hb_=Hb_, Hb_b=Hb_b, Hf_=Hf_, Hf_b=Hf_b, U_=U_, U_b=U_b,
                               mats=mats, c=c, d=d):
                            for hp in range(2):
                                r0, r1 = hp * 64, hp * 64 + 64
                                G_, G_b = st['Gb'][mats[hp]]
                                ps, psb = psum()
                                mm(ps[:, 0:64], psb, ld[r0:r1, 3, :], Hb_[r0:r1, r0:r1], True, False, [ld_b, Hb_b])
                                mm(ps[:, 0:64], psb, G_[:, 1, :], tok[:, 0, r0:r1], False, False, [G_b, tok_b])
                                mm(ps[:, 0:64], psb, G_[:, 2, :], U_[:, r0:r1], False, True, [G_b, U_b])
                                if c not in ywritten:
                                    cp('dve', yacc[:, c, r0:r1], ps[:, 0:64], [psb], [yacc_bs[c]])
                                else:
                                    tt('dve', yacc[:, c, r0:r1], ps[:, 0:64], yacc[:, c, r0:r1], ALU.add, [psb, yacc_bs[c]], [yacc_bs[c]])
                            ywritten.add(c)
                            ps, psb = psum()
                            mm(ps[:, 0:128], psb, tok[:, 1, :], tok[:, 0, :], True, False, [tok_b])
                            mm(ps[:, 0:128], psb, tok[:, 2, :], U_, False, True, [tok_b, U_b])
                            stt(Hf_, Hf_, gC[:, j, d, c:c + 1], ps[:, 0:128], ALU.mult, ALU.add, [Hf_b, gC_b, psb], [Hf_b])
                            cp('act', Hb_, Hf_, [Hf_b], [Hb_b])
                        stages += [s1, s2, s3]
                    return stages

                items_cur = prep(0, sets[0])
                for f_ in inv_slots(sets[0], items_cur):
                    f_()
                for g in range(ngroups):
                    st = sets[g % 2]
                    if g + 1 < ngroups:
                        items_nxt = prep(g + 1, sets[(g + 1) % 2])
                        slots = inv_slots(sets[(g + 1) % 2], items_nxt)
                    else:
                        items_nxt, slots = None, []
                    sf = seq_stages(st, items_cur, 0)
                    sb_ = seq_stages(st, items_cur, 1)
                    for s_ in range(max(len(slots), len(sf))):
                        if s_ < len(slots):
                            slots[s_]()
                        if s_ < len(sf):
                            sf[s_]()
                            sb_[s_]()
                    items_cur = items_nxt

                if WSTOP == 'w2':
                    return
                P.barrier()
                A.off = w3_off
                mn, mn_b = A.alloc([128, 36], F32)
                vr, vr_b = A.alloc([128, 36], F32)
                sqb, sqb_b = A.alloc([128, 36, 64], F32)
                ynb, ynb_b = A.alloc([128, NCH, 128], BF16)
                bon, bon_bb = A.alloc([128, TT], F32)
                gg, gg_b = A.alloc([128, TT], F32)
                tmp, tmp_b = A.alloc([128, 1024], F32)
                ro, ro_b = A.alloc([128, TT], BF16)
                P.dma('sp', bon, bon_s[j, 0], reads=[bon_b], writes=[bon_bb])
                P.dma('sp', gg, bon_s[j, 1], reads=[bon_b], writes=[gg_b])
                y4 = yacc.rearrange("p c (h v) -> p (c h) v", h=2)
                yacc_b = Buf()
                P.op('dve', lambda e: e.tensor_reduce(out=mn, in_=y4, axis=AX.X, op=ALU.add), reads=yacc_bs, writes=[mn_b, yacc_b])
                ts('pool', mn, mn, 1.0 / 64, None, ALU.mult, None, [mn_b], [mn_b])
                tt('dve', y4, y4, mn.unsqueeze(2).broadcast_to([128, 36, 64]), ALU.subtract, [yacc_b, mn_b], [yacc_b])
                act(sqb, y4, AF.Square, [yacc_b], [sqb_b])
                P.op('dve', lambda e: e.tensor_reduce(out=vr, in_=sqb, axis=AX.X, op=ALU.add), reads=[sqb_b], writes=[vr_b])
                act(vr, vr, AF.Sqrt, [vr_b, epsT_b], [vr_b], bias=eps_ap(64e-5), scale=1.0 / 64)
                recip(vr, vr, [vr_b], [vr_b])
                tt('dve', ynb.rearrange("p c (h v) -> p (c h) v", h=2), y4, vr.unsqueeze(2).broadcast_to([128, 36, 64]), ALU.mult,
                   [yacc_b, vr_b], [ynb_b])
                for c0 in range(0, NCH, 8):
                    n = min(8, NCH - c0)
                    ps, psb = psum()
                    psv = ps[:].bitcast(BF16)
                    for cc in range(n):
                        transp(psv[:, cc * 128:(cc + 1) * 128], psb, ynb[:, c0 + cc, :], ident_b[:], [ynb_b, ident_bb])
                    cs = slice(c0 * 128, (c0 + n) * 128)
                    stt(tmp[:, 0:n * 128], psv[:, 0:n * 128], ngc[:, j:j + 1], bon[:, cs], ALU.mult, ALU.add, [psb, ngc_b, bon_bb], [tmp_b])
                    tt('dve', ro[:, cs], tmp[:, 0:n * 128], gg[:, cs], ALU.mult, [tmp_b, gg_b], [ro_b])
                P.dma('sp', rwkv_s[j], ro, reads=[ro_b], writes=[rwkv_b])

        def phase_merge(bi, l):
            A.reset()
            last = (l == nlayers - 1)
            tiles = TILES[:4] if last else TILES
            Wa, Wa_b = A.alloc([128, 8, 1024], BF16)
            Wb, Wb_b = A.alloc([128, 4, 1024], BF16)
            Wo, Wo_b = A.alloc([128, 8, 1024], BF16)
            for hf in range(2):
                P.dma('pool', Wa[:, :, hf * 512:(hf + 1) * 512], I['w_branch_a'][l, :, hf * 512:(hf + 1) * 512].rearrange("(k p) n -> p k n", p=128), writes=[Wa_b])
                P.dma('pool', Wb[:, :, hf * 512:(hf + 1) * 512], I['w_branch_b'][l, :, hf * 512:(hf + 1) * 512].rearrange("(k p) n -> p k n", p=128), writes=[Wb_b])
                P.dma('pool', Wo[:, :, hf * 512:(hf + 1) * 512], I['w_out'][l, :, hf * 512:(hf + 1) * 512].rearrange("(k p) n -> p k n", p=128), writes=[Wo_b])
            bufs = []
            for _ in range(2):
                bufs.append((A.alloc([128, 8, 512], BF16), A.alloc([128, 4, 512], BF16), A.alloc([128, 8, 512], BF16), A.alloc([128, 8, 512], BF16)))
            mT, mT_b = A.alloc([128, 8, 512], BF16)
            m1, m1_b = A.alloc([128, 512], F32)
            m2, m2_b = A.alloc([128, 512], F32)
            for ti, (t0, w) in enumerate(tiles):
                (rt_, rt_b), (wt_, wt_b), (ga, ga_b), (gb, gb_b) = bufs[ti % 2]
                r = 2 if t0 >= T else bi
                P.dma('sp', rt_[:, :, 0:w], ret_s[:, :, t0:t0 + w].rearrange("h p t -> p h t"), reads=[ret_b], writes=[rt_b])
                P.dma('sp', wt_[:, :, 0:w], rwkv_s[:, :, t0:t0 + w].rearrange("h p t -> p h t"), reads=[rwkv_b], writes=[wt_b])
                P.dma('sp', ga[:, :, 0:w], gate_s[8:16, :, t0:t0 + w].rearrange("h p t -> p h t"), reads=[gate_b], writes=[ga_b])
                P.dma('sp', gb[:, :, 0:w], gate_s[16:24, :, t0:t0 + w].rearrange("h p t -> p h t"), reads=[gate_b], writes=[gb_b])
                for jo in range(8):
                    ps, psb = psum()
                    for k in range(8):
                        mm(ps[:, 0:w], psb, Wa[:, k, jo * 128:(jo + 1) * 128], rt_[:, k, 0:w], k == 0, k == 7, [Wa_b, rt_b])
                    tt('dve', m1[:, 0:w], ps[:, 0:w], ga[:, jo, 0:w], ALU.mult, [psb, ga_b], [m1_b])
                    ps2, ps2b = psum()
                    for k in range(4):
                        mm(ps2[:, 0:w], ps2b, Wb[:, k, jo * 128:(jo + 1) * 128], wt_[:, k, 0:w], k == 0, k == 3, [Wb_b, wt_b])
                    tt('dve', m2[:, 0:w], ps2[:, 0:w], gb[:, jo, 0:w], ALU.mult, [ps2b, gb_b], [m2_b])
                    tt('dve', mT[:, jo, 0:w], m1[:, 0:w], m2[:, 0:w], ALU.add, [m1_b, m2_b], [mT_b])
                for jo in range(8):
                    ps, psb = psum()
                    for k in range(8):
                        mm(ps[:, 0:w], psb, Wo[:, k, jo * 128:(jo + 1) * 128], mT[:, k, 0:w], k == 0, k == 7, [Wo_b, mT_b])
                    stt(xT[:, jo, t0:t0 + w], ps[:, 0:w], modT[:, l, 16 + jo, r:r + 1], xT[:, jo, t0:t0 + w], ALU.mult, ALU.add,
                        [psb, modT_b, xT_b], [xT_b])

        def phase_ffn(bi, l):
            A.reset()
            last = (l == nlayers - 1)
            sups = [(0, 1024), (1024, 1024)] + ([] if last else [(2048, 256)])
            h2, h2_b = A.alloc([128, 8, 1024], BF16)
            hid, hid_b = A.alloc([128, 22, 1024], BF16)
            sq, sq_b = A.alloc([128, 8, 512], F32)
            rs, rs_b = A.alloc([128, 512], F32)
            tmp, _ = A.alloc([128, 2, 512], F32)
            tmp_b = [Buf(), Buf()]
            wa = [A.alloc([128, 8, 128], BF16) for _ in range(3)]
            wg = [A.alloc([128, 8, 128], BF16) for _ in range(3)]
            w2c = [A.alloc([128, 22, 128], BF16) for _ in range(2)]
            sa = [A.alloc([128, 512], F32) for _ in range(2)]
            si = 0
            for (T0, W) in sups:
                subt = [(t0, min(512, T0 + W - t0)) for t0 in range(T0, T0 + W, 512)]
                norm_mod(lambda k, t0, w: h2[:, k, t0 - T0:t0 - T0 + w], h2_b,
                         lambda k, r: (modA[:, l, 1, k, (bi if r is None else r):(bi if r is None else r) + 1], [modA_b]),
                         lambda k, r: (modT[:, l, 24 + k, (bi if r is None else r):(bi if r is None else r) + 1], [modT_b]),
                         sq, sq_b, rs, rs_b, tmp, tmp_b, 1e-6, subt)
                for jh in range(22):
                    wa_, wa_b = wa[jh % 3]
                    wg_, wg_b = wg[jh % 3]
                    P.dma('pool', wa_, I['ffn_w13'][l, :, jh * 128:(jh + 1) * 128].rearrange("(k p) n -> p k n", p=128), writes=[wa_b])
                    P.dma('pool', wg_, I['ffn_w13'][l, :, FH + jh * 128:FH + (jh + 1) * 128].rearrange("(k p) n -> p k n", p=128), writes=[wg_b])
                    for (t0, w) in subt:
                        o0 = t0 - T0
                        ps, psb = psum()
                        for k in range(8):
                            mm(ps[:, 0:w], psb, wa_[:, k, :], h2[:, k, o0:o0 + w], k == 0, k == 7, [wa_b, h2_b])
                        ps2, ps2b = psum()
                        for k in range(8):
                            mm(ps2[:, 0:w], ps2b, wg_[:, k, :], h2[:, k, o0:o0 + w], k == 0, k == 7, [wg_b, h2_b])
                        sa_, sa_b = sa[si % 2]
                        si += 1
                        act(sa_[:, 0:w], ps[:, 0:w], AF.Silu, [psb], [sa_b])
                        tt('dve', hid[:, jh, o0:o0 + w], ps2[:, 0:w], sa_[:, 0:w], ALU.mult, [ps2b, sa_b], [hid_b])
                for jo in range(8):
                    w2_, w2_b = w2c[jo % 2]
                    for hf in range(2):
                        P.dma('pool', w2_[:, hf * 11:(hf + 1) * 11, :],
                              I['ffn_w2'][l, hf * 1408:(hf + 1) * 1408, jo * 128:(jo + 1) * 128].rearrange("(k p) n -> p k n", p=128), writes=[w2_b])
                    for (t0, w) in subt:
                        o0 = t0 - T0
                        r = 2 if t0 >= T else bi
                        ps, psb = psum()
                        for k in range(22):
                            mm(ps[:, 0:w], psb, w2_[:, k, :], hid[:, k, o0:o0 + w], k == 0, k == 21, [w2_b, hid_b])
                        stt(xT[:, jo, t0:t0 + w], ps[:, 0:w], modT[:, l, 40 + jo, r:r + 1], xT[:, jo, t0:t0 + w], ALU.mult, ALU.add,
                            [psb, modT_b, xT_b], [xT_b])

        def phase_final(bi):
            A.reset()
            yf, yf_b = A.alloc([128, 8, 512], F32)
            sq, sq_b = A.alloc([128, 8, 512], F32)
            rs, rs_b = A.alloc([128, 512], F32)
            tmp, _ = A.alloc([128, 2, 512], F32)
            tmp_b = [Buf(), Buf()]
            ost = [A.alloc([128, D], F32) for _ in range(2)]
            oi = 0
            for (t0, w) in TILES[:4]:
                norm_mod(lambda k, t0_, w_: yf[:, k, 0:w_], yf_b,
                         lambda k, r: (fng[:, k:k + 1], [fng_b]),
                         lambda k, r: (zcol[:, 0:1], [zcol_b]),
                         sq, sq_b, rs, rs_b, tmp, tmp_b, 1e-6, [(t0, w)])
                for cc in range(4):
                    o_, o_b = ost[oi % 2]
                    oi += 1
                    for hf in range(2):
                        ps, psb = psum()
                        for kk in range(4):
                            k = hf * 4 + kk
                            transp(ps[:, kk * 128:(kk + 1) * 128], psb, yf[:, k, cc * 128:(cc + 1) * 128], ident_f[:], [yf_b, ident_fb])
                        cp('act' if hf else 'dve', o_[:, hf * 512:(hf + 1) * 512], ps[:], [psb], [o_b])
                    P.dma('sp', out[bi, t0 + cc * 128:t0 + (cc + 1) * 128, :], o_, reads=[o_b], writes=[out_b])
        for bi in range(nbatch):
            A.reset()
            stg = [A.alloc([128, D], F32) for _ in range(2)]
            for c in range(NCH):
                s_, s_b = stg[c % 2]
                src_ = I['x'][bi, c * 128:(c + 1) * 128, :] if c < 16 else I['ctx'][bi, (c - 16) * 128:(c - 15) * 128, :]
                P.dma('sp', s_[:], src_, writes=[s_b])
                for hf in range(2):
                    ps, psb = psum()
                    for kk in range(4):
                        k = hf * 4 + kk
                        transp(ps[:, kk * 128:(kk + 1) * 128], psb, s_[:, k * 128:(k + 1) * 128], ident_f[:], [s_b, ident_fb])
                    cp('act' if hf else 'dve', xT[:, hf * 4:(hf + 1) * 4, c * 128:(c + 1) * 128],
                       ps[:].rearrange("p (a b) -> p a b", a=4), [psb], [xT_b])
            P.barrier()

            for l in range(nlayers):
                A.reset()
                hT, hT_b = A.alloc([128, 8, TT], BF16)
                sq, sq_b = A.alloc([128, 8, 512], F32)
                rs, rs_b = A.alloc([128, 512], F32)
                tmp, _ = A.alloc([128, 2, 512], F32)
                tmp_b = [Buf(), Buf()]
                norm_mod(lambda k, t0, w: hT[:, k, t0:t0 + w], hT_b,
                         lambda k, r: (modA[:, l, 0, k, (bi if r is None else r):(bi if r is None else r) + 1], [modA_b]),
                         lambda k, r: (modT[:, l, 0 + k, (bi if r is None else r):(bi if r is None else r) + 1], [modT_b]),
                         sq, sq_b, rs, rs_b, tmp, tmp_b, 1e-6, TILES)
                if 'dbg_h' in debug:
                    for k in range(8):
                        P.dma('sp', dbg_h[k], hT[:, k, :], reads=[hT_b], writes=[dbg_hb])
                P.barrier()
                A.off = 8 * TT // 2
                rc, rc_b = A.alloc([128, TT], F32)
                rsn, rsn_b = A.alloc([128, TT], F32)
                P.dma('sp', rc[:], C['ropecos'], writes=[rc_b])
                P.dma('sp', rsn[:], C['ropesin'], writes=[rsn_b])
                wch = [A.alloc([128, 8, 128], BF16) for _ in range(4)]
                wn = [0]

                def loadw(src2d):
                    w_, w_b = wch[wn[0] % 4]
                    wn[0] += 1
                    P.dma('pool', w_, src2d.rearrange("(k p) n -> p k n", p=128), writes=[w_b])
                    return w_, w_b
                hsrc = lambda k, t0, w: hT[:, k, t0:t0 + w]
                stgb = [A.alloc([128, TT], BF16) for _ in range(2)]
                stgf = [A.alloc([128, TT], F32) for _ in range(2)]
                t1, t1_b = A.alloc([128, 512], F32)
                t2, t2_b = A.alloc([128, 512], F32)
                sn = [0]
                for qk in range(2):
                    for j in range(4):
                        c0 = qk * 512 + j * 128
                        w_, w_b = loadw(I['w_in'][l, :, c0:c0 + 128])
                        wr_, wr_b = loadw(I['w_rot'][l, :, c0:c0 + 128])
                        so, so_b = stgb[sn[0] % 2]
                        sn[0] += 1
                        for (t0, w) in TILES:
                            ps, psb = psum()
                            ps2, ps2b = psum()
                            for k in range(8):
                                mm(ps[:, 0:w], psb, w_[:, k, :], hsrc(k, t0, w), k == 0, k == 7, [w_b, hT_b])
                            for k in range(8):
                                mm(ps2[:, 0:w], ps2b, wr_[:, k, :], hsrc(k, t0, w), k == 0, k == 7, [wr_b, hT_b])
                            tt('dve', t1[:, 0:w], ps[:, 0:w], rc[:, t0:t0 + w], ALU.mult, [psb, rc_b], [t1_b])
                            tt('dve', t2[:, 0:w], ps2[:, 0:w], rsn[:, t0:t0 + w], ALU.mult, [ps2b, rsn_b], [t2_b])
                            tt('dve', so[:, t0:t0 + w], t1[:, 0:w], t2[:, 0:w], ALU.add, [t1_b, t2_b], [so_b])
                        P.dma('sp', qk_s[qk * 4 + j], so[:], reads=[so_b], writes=[qk_b])
                for g in range(3):
                    base = [2048, 4992, 6016][g]
                    fn = AF.Silu if g == 0 else AF.Sigmoid
                    for j in range(8):
                        w_, w_b = loadw(I['w_in'][l, :, base + j * 128:base + (j + 1) * 128])
                        so, so_b = stgb[sn[0] % 2]
                        sn[0] += 1
                        dense_fm(w_, w_b, 8, hsrc, hT_b, TILES,
                                 lambda ps, psb, t0, w, so=so, so_b=so_b, fn=fn: act(so[:, t0:t0 + w], ps[:, 0:w], fn, [psb], [so_b]))
                        P.dma('sp', gate_s[g * 8 + j], so[:], reads=[so_b], writes=[gate_b])
                for j in range(15):
                    w_, w_b = loadw(I['w_in'][l, :, 3072 + j * 128:3072 + (j + 1) * 128])
                    so, so_b = stgf[j % 2]
                    dense_fm(w_, w_b, 8, hsrc, hT_b, TILES,
                             lambda ps, psb, t0, w, so=so, so_b=so_b: cp('act', so[:, t0:t0 + w], ps[:, 0:w], [psb], [so_b]))
                    P.dma('sp', zrw_s[j], so[:], reads=[so_b], writes=[zrw_b])
                P.barrier()
                A.off = 8 * TT // 2
                wv = [A.alloc([128, 8, 512], BF16) for _ in range(2)]
                for hf in range(2):
                    P.dma('pool', wv[hf][0], I['w_in'][l, :, 1024 + hf * 512:1024 + (hf + 1) * 512].rearrange("(k p) n -> p k n", p=128),
                          writes=[wv[hf][1]])
                vst = [A.alloc([128, 1024], BF16) for _ in range(2)]
                for c in range(NCH):
                    vo, vo_b = vst[c % 2]
                    for hf in range(2):
                        ps, psb = psum()
                        for k in range(8):
                            mm(ps[:], psb, hT[:, k, c * 128:(c + 1) * 128], wv[hf][0][:, k, :], k == 0, k == 7, [hT_b, wv[hf][1]])
                        cp('act' if hf else 'dve', vo[:, hf * 512:(hf + 1) * 512], ps[:], [psb], [vo_b])
                    P.dma('sp', v_s[c], vo[:], reads=[vo_b], writes=[v_b])
                P.barrier()
                if stop_after == 'A':
                    break

                phase_ret(bi, l)
                P.barrier()
                if stop_after == 'R':
                    break
                phase_rwkv(bi, l)
                P.barrier()
                if stop_after == 'W':
                    break
                phase_merge(bi, l)
                P.barrier()
                if stop_after == 'G':
                    break
                phase_ffn(bi, l)
                P.barrier()
            if 'dbg_x' in debug:
                for k in range(8):
                    P.dma('sp', dbg_x[k], xT[:, k, :], reads=[xT_b], writes=[dbg_xb])
            if stop_after is None:
                phase_final(bi)
            P.barrier()

        P.barrier()
        with nc.Block() as block:
            @block.sync
            def _(e):
                for f in P.stream['sp']:
                    f(e)

            @block.tensor
            def _(e):
                for f in P.stream['pe']:
                    f(e)

            @block.scalar
            def _(e):
                for f in P.stream['act']:
                    f(e)

            @block.vector
            def _(e):
                for f in P.stream['dve']:
                    f(e)

            @block.gpsimd
            def _(e):
                for f in P.stream['pool']:
                    f(e)
    return nc


def prep_inputs(inputs):
    consts = _host_consts()
    f = lambda a: np.ascontiguousarray(np.asarray(a, dtype=np.float32))
    w_in = f(inputs['w_in'])
    perm = np.zeros(1024, np.int64)
    for cidx in range(1024):
        h, i = divmod(cidx % 512, 64)
        ii = i % 32
        partner = i + 16 if ii < 16 else i - 16
        perm[cidx] = (cidx // 512) * 512 + h * 64 + partner
    w_rot = np.ascontiguousarray(w_in[:, :, perm])
    shared = {
        'c_ctx': f(inputs['c_ctx']), 'mod_w': f(inputs['mod_w']), 'mod_b': f(inputs['mod_b']),
        'norm1_g': f(inputs['norm1_g']), 'norm2_g': f(inputs['norm2_g']), 'w_in': w_in, 'w_rot': w_rot,
        'ret_decay': f(inputs['ret_decay']).reshape(NL, 16), 'ret_norm_g': f(inputs['ret_norm_g']),
        'rwkv_mu': f(inputs['rwkv_mu']), 'rwkv_w0': f(inputs['rwkv_w0']),
        'rwkv_w2': f(inputs['rwkv_w2']).reshape(NL, 128, 512), 'rwkv_a0': f(inputs['rwkv_a0']),
        'rwkv_a2': f(inputs['rwkv_a2']).reshape(NL, 128, 512), 'rwkv_g2': f(inputs['rwkv_g2']),
        'rwkv_k_k': f(inputs['rwkv_k_k']), 'rwkv_k_a': f(inputs['rwkv_k_a']),
        'rwkv_r_k': f(inputs['rwkv_r_k']).reshape(NL, 512), 'rwkv_norm_g': f(inputs['rwkv_norm_g']),
        'w_branch_a': f(inputs['w_branch_a']), 'w_branch_b': f(inputs['w_branch_b']), 'w_out': f(inputs['w_out']),
        'ffn_w13': f(inputs['ffn_w13']), 'ffn_w2': f(inputs['ffn_w2']), 'final_norm_g': f(inputs['final_norm_g']),
    }
    shared.update(consts)
    x = f(inputs['x']); c = f(inputs['c']); ctx = f(inputs['ctx'])
    in_maps = []
    for i in range(8):
        m = dict(shared)
        m['x'] = x[i * NB:(i + 1) * NB]
        m['c'] = c[i * NB:(i + 1) * NB]
        m['ctx'] = ctx[i * NB:(i + 1) * NB]
        in_maps.append(m)
    return in_maps


def kernel(**inputs):
    in_maps = prep_inputs(inputs)
    nc = build()
    res = run_bass_kernel_spmd(nc, in_maps, core_ids=list(range(8)))
    return np.concatenate([np.asarray(r['out'], dtype=np.float32) for r in res.results], axis=0)
```

```python
import contextlib
import os
import numpy as np
import ml_dtypes
import concourse.bass as bass
import concourse.mybir as mybir
from concourse.bass_utils import run_bass_kernel_spmd

F32 = mybir.dt.float32
F32R = mybir.dt.float32r
BF16 = mybir.dt.bfloat16
AF = mybir.ActivationFunctionType
ALU = mybir.AluOpType
AX = mybir.AxisListType

D = 1024
T = 2048
TC = 256
TT = 2304
NCH = 18
NL = 2
NB = 2
FH = 2816
TILES = [(0, 512), (512, 512), (1024, 512), (1536, 512), (2048, 256)]
ORDER_F = [16, 17] + list(range(16))
ORDER_B = [17, 16] + list(range(15, -1, -1))
DECAY_C = -0.6065306597126334
EPOCH = 20000
ENGS = ('pe', 'act', 'dve', 'pool', 'sp')


class Buf:
    __slots__ = ('w', 'r')

    def __init__(self):
        self.w = None
        self.r = {}


class Prog:
    def __init__(self, nc, stack):
        self.nc = nc
        self.stack = stack
        self.stream = {e: [] for e in ENGS}
        self.n = {e: 0 for e in ENGS}
        self.seen = {e: {} for e in ENGS}
        self.sems = {}
        self.dcount = {}
        self.dnext = {'sp': 0, 'pool': 0}
        self.nslots = {'sp': 12, 'pool': 6}

    def _sem(self, key):
        if key not in self.sems:
            self.sems[key] = self.stack.enter_context(self.nc.semaphore("s_" + "_".join(str(k) for k in key)))
        return self.sems[key]

    def _deps(self, eng, reads, writes):
        deps = {}

        def add(key, val):
            if deps.get(key, 0) < val:
                deps[key] = val
        for b in reads:
            if b.w is not None:
                add(*b.w)
        for b in writes:
            if b.w is not None:
                add(*b.w)
            for k, v in b.r.items():
                add(k, v)
        waits = []
        for key, val in deps.items():
            if key[0] == 'e' and key[1] == 'pe' and eng == 'pe':
                continue
            if self.seen[eng].get(key, 0) >= val:
                continue
            self.seen[eng][key] = val
            waits.append((self._sem(key), val))
        return waits

    def _mark(self, tok, reads, writes):
        key, val = tok
        for b in reads:
            if b.r.get(key, 0) < val:
                b.r[key] = val
        for b in writes:
            b.w = tok
            b.r = {}

    def op(self, eng, fn, reads=(), writes=()):
        waits = self._deps(eng, reads, writes)
        i = self.n[eng]
        self.n[eng] += 1
        key = ('e', eng, i // EPOCH)
        val = i % EPOCH + 1
        mysem = self._sem(key)

        def run(e, waits=waits, fn=fn, mysem=mysem):
            for s, v in waits:
                e.wait_ge(s, v)
            fn(e).then_inc(mysem, 1)
        self.stream[eng].append(run)
        self._mark((key, val), reads, writes)

    def dma(self, q, out, in_, reads=(), writes=(), slow=False):
        idx = self.dnext[q]
        self.dnext[q] = (idx + 1) % self.nslots[q]
        key = ('d', q, idx)
        cnt = self.dcount.get(key, 0)
        waits = self._deps(q, reads, writes)
        sem = self._sem(key)
        if cnt > 0 and self.seen[q].get(key, 0) < 16 * cnt:
            self.seen[q][key] = 16 * cnt
            waits.append((sem, 16 * cnt))
        self.dcount[key] = cnt + 1

        def run(e, waits=waits, out=out, in_=in_, sem=sem, slow=slow):
            for s, v in waits:
                e.wait_ge(s, v)
            if slow:
                e.dma_start(out=out, in_=in_, allow_slow_non_contiguous=True).then_inc(sem, 16)
            else:
                e.dma_start(out=out, in_=in_).then_inc(sem, 16)
        self.stream[q].append(run)
        self._mark((key, 16 * (cnt + 1)), reads, writes)

    def barrier(self):
        toks = []
        for e in ('pe', 'act', 'dve', 'pool'):
            if self.n[e] > 0:
                i = self.n[e] - 1
                toks.append((e, ('e', e, i // EPOCH), i % EPOCH + 1))
        for key, cnt in self.dcount.items():
            toks.append((None, key, 16 * cnt))
        for eng in ENGS:
            waits = []
            for src, key, val in toks:
                if src == eng:
                    continue
                if self.seen[eng].get(key, 0) >= val:
                    continue
                self.seen[eng][key] = val
                waits.append((self._sem(key), val))
            if waits:
                def run(e, waits=waits):
                    for s, v in waits:
                        e.wait_ge(s, v)
                self.stream[eng].append(run)


def _host_consts():
    c = {}
    c['ident_f'] = np.eye(128, dtype=np.float32)
    c['ident_b'] = np.eye(128, dtype=np.float32).astype(ml_dtypes.bfloat16)
    t = np.arange(T)
    row = (t // 64).astype(np.float32)
    col = (t % 64).astype(np.float32)
    inv = (10000.0 ** (-np.arange(16, dtype=np.float32) / 16)).astype(np.float32)
    cos = np.ones((128, TT), np.float32)
    sin = np.zeros((128, TT), np.float32)
    for p in range(128):
        i = p % 64
        pos = row if i < 32 else col
        ii = i % 32
        f = ii % 16
        ang = (pos * inv[f]).astype(np.float32)
        cos[p, :T] = np.cos(ang)
        sin[p, :T] = -np.sin(ang) if ii < 16 else np.sin(ang)
    c['ropecos'] = cos
    c['ropesin'] = sin
    m6 = np.zeros((128, 6), np.float32)
    for p in range(128):
        m6[p, p % 4] = 1.0
        m6[p, 4 + p % 2] = 1.0
    c['m6'] = m6
    rm = np.ones((128, TT), np.float32)
    rm[:, ::128] = 0.0
    c['rmask'] = rm
    a = np.arange(128)[:, None]
    b = np.arange(128)[None, :]
    Lm = (a > b).astype(np.float32)
    Um = (a < b).astype(np.float32)
    UE = (a <= b).astype(np.float32)
    rw = np.zeros((128, 2, 5, 128), np.float32)
    rw[:, 0] = np.stack([Lm, Um, Um, UE, -UE], 1)
    rw[:, 1] = np.stack([Um, Lm, Lm, Lm, -Lm], 1)
    c['rwmask'] = rw
    retD = np.zeros((128, 2, 128), np.float32)
    retM = np.zeros((128, 2, 128), np.float32)
    retD[:, 0] = np.maximum(b - a, 0)
    retM[:, 0] = (a <= b)
    retD[:, 1] = np.maximum(a - b, 0)
    retM[:, 1] = (a > b)
    c['retD'] = retD
    c['retM'] = retM
    qdt = np.zeros((128, 2, 128), np.float32)
    qdt[:, 0] = (b + 1)
    qdt[:, 1] = (128 - b)
    c['qdt'] = qdt
    kdt = np.zeros((128, 2), np.float32)
    kdt[:, 0] = 127 - np.arange(128)
    kdt[:, 1] = np.arange(128)
    c['kdt'] = kdt
    return c


CONST_SHAPES = {
    'ident_f': ([128, 128], F32), 'ident_b': ([128, 128], BF16), 'ropecos': ([128, TT], F32),
    'ropesin': ([128, TT], F32), 'm6': ([128, 6], F32), 'rmask': ([128, TT], F32),
    'rwmask': ([128, 2, 5, 128], F32), 'retD': ([128, 2, 128], F32), 'retM': ([128, 2, 128], F32),
    'qdt': ([128, 2, 128], F32), 'kdt': ([128, 2], F32),
}

IN_SHAPES = {
    'x': [NB, T, D], 'c': [NB, D], 'ctx': [NB, TC, D], 'c_ctx': [D], 'mod_w': [NL, D, 6 * D],
    'mod_b': [NL, 6 * D], 'norm1_g': [NL, D], 'norm2_g': [NL, D], 'w_in': [NL, D, 7040],
    'w_rot': [NL, D, 1024],
    'ret_decay': [NL, 16], 'ret_norm_g': [NL, D], 'rwkv_mu': [NL, 1920], 'rwkv_w0': [NL, 2, 512],
    'rwkv_w2': [NL, 128, 512], 'rwkv_a0': [NL, 2, 512], 'rwkv_a2': [NL, 128, 512], 'rwkv_g2': [NL, 128, 512],
    'rwkv_k_k': [NL, 512], 'rwkv_k_a': [NL, 512], 'rwkv_r_k': [NL, 512], 'rwkv_norm_g': [NL, 512],
    'w_branch_a': [NL, D, D], 'w_branch_b': [NL, 512, D], 'w_out': [NL, D, D], 'ffn_w13': [NL, D, 2 * FH],
    'ffn_w2': [NL, FH, D], 'final_norm_g': [D],
}


def build(debug=None, nlayers=NL, nbatch=NB, stop_after=None):
    debug = debug or []
    nc = bass.Bass("TRN2", target_bir_lowering=False)
    stack = contextlib.ExitStack()
    with stack:
        P = Prog(nc, stack)
        I = {k: nc.dram_tensor(k, s, F32, kind="ExternalInput").ap() for k, s in IN_SHAPES.items()}
        C = {k: nc.dram_tensor(k, s, dt, kind="ExternalInput").ap() for k, (s, dt) in CONST_SHAPES.items()}
        out = nc.dram_tensor("out", [NB, T, D], F32, kind="ExternalOutput").ap()
        out_b = Buf()

        def scratch(name, shape, dt):
            kind = "ExternalOutput" if name in debug else "Internal"
            return nc.dram_tensor(name, shape, dt, kind=kind).ap(), Buf()
        qk_s, qk_b = scratch("qk_s", [8, 128, TT], BF16)
        v_s, v_b = scratch("v_s", [NCH, 128, 1024], BF16)
        gate_s, gate_b = scratch("gate_s", [24, 128, TT], BF16)
        zrw_s, zrw_b = scratch("zrw_s", [15, 128, TT], F32)
        ret_s, ret_b = scratch("ret_s", [8, 128, TT], BF16)
        rw_s, rw_b = scratch("rw_s", [4, 2, 128, 7, TT], BF16)
        bon_s, bon_b = scratch("bon_s", [4, 2, 128, TT], F32)
        rwkv_s, rwkv_b = scratch("rwkv_s", [4, 128, TT], BF16)
        dbg_h, dbg_hb = scratch("dbg_h", [8, 128, TT], BF16)
        dbg_x, dbg_xb = scratch("dbg_x", [8, 128, TT], F32)

        def sb(name, shape, dt):
            return stack.enter_context(nc.sbuf_tensor(name, shape, dt)), Buf()
        xT, xT_b = sb("xT", [128, 8, TT], F32)
        AW = 31616
        arena, _ = sb("arena", [128, AW], F32)
        cst = {}
        for k in ('ident_f', 'ident_b', 'm6', 'rwmask', 'retD', 'retM', 'qdt', 'kdt'):
            cst[k] = sb("c_" + k, CONST_SHAPES[k][0], CONST_SHAPES[k][1])
        ones_f, ones_fb = sb("ones_f", [128, 128], F32)
        bones_f, bones_fb = sb("bones_f", [128, 128], F32)
        modT, modT_b = sb("modT", [128, NL, 48, 3], F32)
        modA, modA_b = sb("modA", [128, NL, 2, 8, 3], F32)
        gC, gC_b = sb("gC", [128, 4, 2, NCH], F32)
        pst = []
        for i in range(8):
            t_ = stack.enter_context(nc.psum_tensor("ps%d" % i, [128, 512], F32))
            pst.append((t_, Buf()))
        pi = [0]

        def psum():
            i = pi[0]
            pi[0] = (i + 1) % 8
            return pst[i]

        class Arena:
            def __init__(self):
                self.off = 0

            def reset(self):
                self.off = 0

            def alloc(self, shape, dt):
                n = int(np.prod(shape[1:]))
                words = n if dt in (F32, F32R) else (n + 1) // 2
                assert self.off + words <= AW, (self.off, words, AW)
                ap = arena[:, self.off:self.off + words]
                self.off += words
                if dt == F32R:
                    ap = ap.bitcast(F32R)
                if dt == BF16:
                    ap = ap.bitcast(BF16)
                    if n % 2:
                        ap = ap[:, 0:n]
                if len(shape) == 3:
                    ap = ap.rearrange("p (a b) -> p a b", a=shape[1])
                elif len(shape) == 4:
                    ap = ap.rearrange("p (a b c) -> p a b c", a=shape[1], b=shape[2])
                return ap, Buf()
        A = Arena()

        def mm(ps, psb, lhsT, rhs, start, stop, reads):
            P.op('pe', lambda e: e.matmul(ps, lhsT=lhsT, rhs=rhs, start=start, stop=stop),
                 reads=reads, writes=[psb])

        def transp(ps, psb, in_, ident, reads):
            P.op('pe', lambda e: e.transpose(out=ps, in_=in_, identity=ident), reads=reads, writes=[psb])

        def act(out_, in_, func, reads, writes, bias=0.0, scale=1.0):
            P.op('act', lambda e: e.activation(out=out_, in_=in_, func=func, bias=bias, scale=scale),
                 reads=reads, writes=writes)

        def tt(eng, out_, in0, in1, op, reads, writes):
            P.op(eng, lambda e: e.tensor_tensor(out=out_, in0=in0, in1=in1, op=op), reads=reads, writes=writes)

        def ts(eng, out_, in0, s1, s2, op0, op1, reads, writes):
            if op1 is None:
                P.op(eng, lambda e: e.tensor_scalar(out=out_, in0=in0, scalar1=s1, scalar2=None, op0=op0),
                     reads=reads, writes=writes)
            else:
                P.op(eng, lambda e: e.tensor_scalar(out=out_, in0=in0, scalar1=s1, scalar2=s2, op0=op0, op1=op1),
                     reads=reads, writes=writes)

        def stt(out_, in0, scalar, in1, op0, op1, reads, writes):
            P.op('dve', lambda e: e.scalar_tensor_tensor(out=out_, in0=in0, scalar=scalar, in1=in1, op0=op0, op1=op1),
                 reads=reads, writes=writes)

        def cp(eng, out_, in_, reads, writes):
            if eng == 'act':
                P.op('act', lambda e: e.copy(out=out_, in_=in_), reads=reads, writes=writes)
            else:
                P.op(eng, lambda e: e.tensor_copy(out=out_, in_=in_), reads=reads, writes=writes)

        def recip(out_, in_, reads, writes):
            P.op('dve', lambda e: e.reciprocal(out=out_, in_=in_), reads=reads, writes=writes)

        def memset(eng, ap, val, writes):
            P.op(eng, lambda e: e.memset(ap, val), writes=writes)

        for k in cst:
            P.dma('sp', cst[k][0][:], C[k], writes=[cst[k][1]])
        ident_f, ident_fb = cst['ident_f']
        ident_b, ident_bb = cst['ident_b']
        memset('pool', ones_f[:], 1.0, [ones_fb])
        memset('pool', bones_f[:], 0.0, [bones_fb])
        memset('pool', bones_f[0:64, 0:64], 1.0, [bones_fb])
        memset('pool', bones_f[64:128, 64:128], 1.0, [bones_fb])

        A.reset()
        c3, c3_b = A.alloc([128, 8, 3], F32)
        s3, s3_b = A.alloc([128, 8, 3], F32)
        mb, mb_b = A.alloc([128, NL, 48], F32)
        ng, ng_b = A.alloc([128, NL, 2, 8], F32)
        for r in range(NB):
            P.dma('sp', c3[:, :, r], I['c'][r].rearrange("(k p) -> p k", p=128), writes=[c3_b], slow=True)
        P.dma('sp', c3[:, :, 2], I['c_ctx'].rearrange("(k p) -> p k", p=128), writes=[c3_b], slow=True)
        for l in range(NL):
            P.dma('sp', mb[:, l, :], I['mod_b'][l].rearrange("(j p) -> p j", p=128), writes=[mb_b], slow=True)
            P.dma('sp', ng[:, l, 0, :], I['norm1_g'][l].rearrange("(k p) -> p k", p=128), writes=[ng_b], slow=True)
            P.dma('sp', ng[:, l, 1, :], I['norm2_g'][l].rearrange("(k p) -> p k", p=128), writes=[ng_b], slow=True)
        act(s3, c3, AF.Silu, [c3_b], [s3_b])
        wst = [A.alloc([128, 8, 512], F32) for _ in range(2)]
        wi = 0
        for l in range(nlayers):
            for cc in range(12):
                w_, w_b = wst[wi % 2]
                wi += 1
                P.dma('sp', w_, I['mod_w'][l, :, cc * 512:(cc + 1) * 512].rearrange("(k p) n -> p k n", p=128), writes=[w_b])
                for jj in range(4):
                    j = cc * 4 + jj
                    ps, psb = psum()
                    for k in range(8):
                        mm(ps[:, 0:3], psb, w_[:, k, jj * 128:(jj + 1) * 128], s3[:, k, :], k == 0, k == 7, [w_b, s3_b])
                    act(modT[:, l, j, :], ps[:, 0:3], AF.Identity, [psb, mb_b], [modT_b], bias=mb[:, l, j:j + 1])
            for n_ in range(2):
                j0 = 8 if n_ == 0 else 32
                ts('dve', modA[:, l, n_, :, :], modT[:, l, j0:j0 + 8, :], 1.0, None, ALU.add, None, [modT_b], [modA_b])
                tt('dve', modA[:, l, n_, :, :], modA[:, l, n_, :, :], ng[:, l, n_, :].unsqueeze(2).broadcast_to([128, 8, 3]),
                   ALU.mult, [modA_b, ng_b], [modA_b])
        P.barrier()

        def norm_mod(dst, dst_b, Afn, shfn, sq, sq_b, rs, rs_b, tmp, tmp_b, eps, tiles):
            for (t0, w) in tiles:
                for k in range(8):
                    act(sq[:, k, 0:w], xT[:, k, t0:t0 + w], AF.Square, [xT_b], [sq_b])
                ps, psb = psum()
                for k in range(8):
                    mm(ps[:, 0:w], psb, ones_f[:], sq[:, k, 0:w], k == 0, k == 7, [ones_fb, sq_b])
                act(rs[:, 0:w], ps[:, 0:w], AF.Sqrt, [psb], [rs_b], bias=eps_ap(eps), scale=1.0 / D)
                recip(rs[:, 0:w], rs[:, 0:w], [rs_b], [rs_b])
                r = 2 if t0 >= T else None
                for k in range(8):
                    tt('dve', tmp[:, k % 2, 0:w], xT[:, k, t0:t0 + w], rs[:, 0:w], ALU.mult,
                       [xT_b, rs_b], [tmp_b[k % 2]])
                    a_ap, a_bufs = Afn(k, r)
                    s_ap, s_bufs = shfn(k, r)
                    act(dst(k, t0, w), tmp[:, k % 2, 0:w], AF.Identity, [tmp_b[k % 2]] + a_bufs + s_bufs, [dst_b],
                        bias=s_ap, scale=a_ap)

        epsT, epsT_b = sb("epsT", [128, 4], F32)
        memset('pool', epsT[:, 0:1], 1e-6, [epsT_b])
        memset('pool', epsT[:, 1:2], 1e-5 * 64.0, [epsT_b])
        memset('pool', epsT[:, 2:3], 64e-5, [epsT_b])
        memset('pool', epsT[:, 3:4], 1e-12, [epsT_b])
        EPSI = {1e-6: 0, 1e-5 * 64.0: 1, 64e-5: 2, 1e-12: 3}

        def eps_ap(eps):
            i = EPSI[eps]
            return epsT[:, i:i + 1]

        fng, fng_b = sb("fng", [128, 8], F32)
        P.dma('sp', fng[:], I['final_norm_g'].rearrange("(k p) -> p k", p=128), writes=[fng_b], slow=True)
        zcol, zcol_b = sb("zcol", [128, 1], F32)
        memset('pool', zcol[:], 0.0, [zcol_b])

        def dense_fm(w_ap, w_b, kc, src, src_b, tiles, evac):
            for ti, (t0, w) in enumerate(tiles):
                ps, psb = psum()
                for k in range(kc):
                    mm(ps[:, 0:w], psb, w_ap[:, k, :], src(k, t0, w), k == 0, k == kc - 1, [w_b, src_b])
                evac(ps, psb, t0, w)

        def phase_ret(bi, l):
            A.reset()
            lgt, lgt_b = A.alloc([128, 16], F32)
            gcr, gcr_b = A.alloc([128, 16], F32)
            rng, rng_b = A.alloc([128, 8], F32)
            P.dma('sp', lgt, I['ret_decay'][l].partition_broadcast(128), writes=[lgt_b], slow=True)
            P.dma('sp', rng, I['ret_norm_g'][l].rearrange("(h p) -> p h", p=128), writes=[rng_b], slow=True)
            act(lgt, lgt, AF.Exp, [lgt_b], [lgt_b])
            ts('pool', lgt, lgt, -1.0, None, ALU.mult, None, [lgt_b], [lgt_b])
            act(gcr, lgt, AF.Exp, [lgt_b], [gcr_b], scale=128.0)
            retD, retD_b = cst['retD']
            retM, retM_b = cst['retM']
            qdt, qdt_b = cst['qdt']
            kdt, kdt_b = cst['kdt']
            masks, masks_b = A.alloc([128, 8, 128], F32)
            qd, qd_b = A.alloc([128, 16, 128], F32)
            kd, kd_b = A.alloc([128, 16], F32)
            e2, e2_b = A.alloc([128, 128], F32)
            for h in range(8):
                lf = lgt[:, h:h + 1]
                lb = lgt[:, 8 + h:9 + h]
                act(masks[:, h, :], retD[:, 0, :], AF.Exp, [retD_b, lgt_b], [masks_b], scale=lf)
                tt('pool', masks[:, h, :], masks[:, h, :], retM[:, 0, :], ALU.mult, [masks_b, retM_b], [masks_b])
                act(e2, retD[:, 1, :], AF.Exp, [retD_b, lgt_b], [e2_b], scale=lb)
                tt('pool', e2, e2, retM[:, 1, :], ALU.mult, [e2_b, retM_b], [e2_b])
                tt('pool', masks[:, h, :], masks[:, h, :], e2, ALU.add, [masks_b, e2_b], [masks_b])
                act(qd[:, h * 2, :], qdt[:, 0, :], AF.Exp, [qdt_b, lgt_b], [qd_b], scale=lf)
                act(qd[:, h * 2 + 1, :], qdt[:, 1, :], AF.Exp, [qdt_b, lgt_b], [qd_b], scale=lb)
                act(kd[:, h * 2:h * 2 + 1], kdt[:, 0:1], AF.Exp, [kdt_b, lgt_b], [kd_b], scale=lf)
                act(kd[:, h * 2 + 1:h * 2 + 2], kdt[:, 1:2], AF.Exp, [kdt_b, lgt_b], [kd_b], scale=lb)
            base_off = A.off
            for j in range(4):
                P.barrier()
                A.off = base_off
                qT, qT_b = A.alloc([128, NCH, 128], BF16)
                kT, kT_b = A.alloc([128, NCH, 128], BF16)
                vt, vt_b = A.alloc([128, NCH, 256], BF16)
                P.dma('sp', qT, qk_s[j].rearrange("p (c t) -> p c t", c=NCH), reads=[qk_b], writes=[qT_b])
                P.dma('sp', kT, qk_s[4 + j].rearrange("p (c t) -> p c t", c=NCH), reads=[qk_b], writes=[kT_b])
                P.dma('sp', vt, v_s[:, :, j * 256:(j + 1) * 256].rearrange("c p e -> p c e"), reads=[v_b], writes=[vt_b])
                kdp, kdp_b = A.alloc([128, 2, 128], F32)
                for d in range(2):
                    for hp in range(2):
                        h = 2 * j + hp
                        ts('pool', kdp[:, d, hp * 64:(hp + 1) * 64], ones_f[:, 0:64], kd[:, h * 2 + d:h * 2 + d + 1], None,
                           ALU.mult, None, [ones_fb, kd_b], [kdp_b])
                ktd = [A.alloc([128, NCH, 128], BF16) for _ in range(2)]
                for c0 in range(0, NCH, 8):
                    n = min(8, NCH - c0)
                    ps, psb = psum()
                    psv = ps[:].bitcast(BF16)
                    for cc in range(n):
                        transp(psv[:, cc * 128:(cc + 1) * 128], psb, kT[:, c0 + cc, :], ident_b[:], [kT_b, ident_bb])
                    for d in range(2):
                        tt('dve', ktd[d][0][:, c0:c0 + n, :], psv[:, 0:n * 128].rearrange("p (c t) -> p c t", c=n),
                           kdp[:, d, :].unsqueeze(1).broadcast_to([128, n, 128]), ALU.mult, [psb, kdp_b], [ktd[d][1]])
                KV = [A.alloc([128, NCH, 128], F32) for _ in range(2)]
                Sbf = [A.alloc([128, NCH, 128], BF16) for _ in range(2)]
                Srun = [[A.alloc([128, 128], F32) for _ in range(2)] for _ in range(2)]
                qfb = [A.alloc([128, NCH, 128], BF16) for _ in range(2)]
                NCS = 4
                cs_ = []
                for _ in range(NCS):
                    cs_.append({'att': A.alloc([128, 4, 128], BF16), 'gt': A.alloc([128, 512], BF16), 'y': A.alloc([128, 512], F32),
                                'sq': A.alloc([128, 512], F32), 'rs': A.alloc([128, 512], F32)})
                ros = [A.alloc([128, TT], BF16) for _ in range(2)]
                for hp in range(2):
                    h = 2 * j + hp
                    r0, r1 = hp * 64, hp * 64 + 64
                    for d in range(2):
                        for (t0, w) in TILES:
                            c0, n = t0 // 128, w // 128
                            ps, psb = psum()
                            for cc in range(n):
                                mm(ps[:, cc * 128:(cc + 1) * 128], psb, ktd[d][0][:, c0 + cc, :], vt[:, c0 + cc, hp * 128:(hp + 1) * 128],
                                   True, True, [ktd[d][1], vt_b])
                            cp('act' if d else 'dve', KV[d][0][r0:r1, c0:c0 + n, :], ps[r0:r1, 0:w].rearrange("p (c t) -> p c t", c=n), [psb], [KV[d][1]])
                        memset('pool', Srun[d][0][0][:], 0.0, [Srun[d][0][1]])
                        for ci_, c in enumerate(ORDER_F if d == 0 else ORDER_B):
                            S_, S_b = Srun[d][ci_ % 2]
                            Sn_, Sn_b = Srun[d][(ci_ + 1) % 2]
                            cp('act', Sbf[d][0][r0:r1, c, :], S_[r0:r1, :], [S_b], [Sbf[d][1]])
                            stt(Sn_[r0:r1, :], S_[r0:r1, :], gcr[r0:r1, d * 8 + h:d * 8 + h + 1], KV[d][0][r0:r1, c, :], ALU.mult, ALU.add,
                                [S_b, gcr_b, KV[d][1]], [Sn_b])
                        tt('dve', qfb[d][0][r0:r1], qT[r0:r1], qd[r0:r1, h * 2 + d, :].unsqueeze(1).broadcast_to([64, NCH, 128]),
                           ALU.mult, [qT_b, qd_b], [qfb[d][1]])
                chains = [(hp, t0, w) for (t0, w) in TILES for hp in range(2)]
                for g0 in range(0, len(chains), NCS):
                    grp = chains[g0:g0 + NCS]
                    loc = []
                    for si, (hp, t0, w) in enumerate(grp):
                        h = 2 * j + hp
                        r0, r1 = hp * 64, hp * 64 + 64
                        c0, n = t0 // 128, w // 128
                        sl = cs_[si]
                        at_, at_b = sl['att']
                        gt_, gt_b = sl['gt']
                        P.dma('sp', gt_[:, 0:w], gate_s[h][:, t0:t0 + w], reads=[gate_b], writes=[gt_b])
                        ps, psb = psum()
                        for cc in range(n):
                            mm(ps[:, cc * 128:(cc + 1) * 128], psb, kT[r0:r1, c0 + cc, :], qT[r0:r1, c0 + cc, :], True, True, [kT_b, qT_b])
                        tt('dve', at_[:, 0:n, :], ps[:, 0:w].rearrange("p (c t) -> p c t", c=n),
                           masks[:, h, :].unsqueeze(1).broadcast_to([128, n, 128]), ALU.mult, [psb, masks_b], [at_b])
                        ps2, ps2b = psum()
                        for cc in range(n):
                            c = c0 + cc
                            o_ = ps2[:, cc * 128:(cc + 1) * 128]
                            mm(o_, ps2b, vt[:, c, hp * 128:(hp + 1) * 128], at_[:, cc, :], True, False, [vt_b, at_b])
                            mm(o_, ps2b, Sbf[0][0][r0:r1, c, :], qfb[0][0][r0:r1, c, :], False, False, [Sbf[0][1], qfb[0][1]])
                            mm(o_, ps2b, Sbf[1][0][r0:r1, c, :], qfb[1][0][r0:r1, c, :], False, True, [Sbf[1][1], qfb[1][1]])
                        loc.append((ps2, ps2b))
                    for si, (hp, t0, w) in enumerate(grp):
                        ys, ys_b = cs_[si]['y']
                        cp('act', ys[:, 0:w], loc[si][0][:, 0:w], [loc[si][1]], [ys_b])
                    loc3 = []
                    for si, (hp, t0, w) in enumerate(grp):
                        ys, ys_b = cs_[si]['y']
                        ps3, ps3b = psum()
                        mm(ps3[:, 0:w], ps3b, ones_f[:], ys[:, 0:w], True, True, [ones_fb, ys_b])
                        loc3.append((ps3, ps3b))
                    for si, (hp, t0, w) in enumerate(grp):
                        ys, ys_b = cs_[si]['y']
                        stt(ys[:, 0:w], loc3[si][0][:, 0:w], -1.0 / 128, ys[:, 0:w], ALU.mult, ALU.add, [loc3[si][1], ys_b], [ys_b])
                    for si, (hp, t0, w) in enumerate(grp):
                        ys, ys_b = cs_[si]['y']
                        sq, sq_b = cs_[si]['sq']
                        act(sq[:, 0:w], ys[:, 0:w], AF.Square, [ys_b], [sq_b])
                    loc4 = []
                    for si, (hp, t0, w) in enumerate(grp):
                        sq, sq_b = cs_[si]['sq']
                        ps4, ps4b = psum()
                        mm(ps4[:, 0:w], ps4b, ones_f[:], sq[:, 0:w], True, True, [ones_fb, sq_b])
                        loc4.append((ps4, ps4b))
                    for si, (hp, t0, w) in enumerate(grp):
                        rs, rs_b = cs_[si]['rs']
                        act(rs[:, 0:w], loc4[si][0][:, 0:w], AF.Sqrt, [loc4[si][1], epsT_b], [rs_b], bias=eps_ap(1e-5 * 64.0), scale=1.0 / 128)
                    for si, (hp, t0, w) in enumerate(grp):
                        rs, rs_b = cs_[si]['rs']
                        recip(rs[:, 0:w], rs[:, 0:w], [rs_b], [rs_b])
                    for si, (hp, t0, w) in enumerate(grp):
                        ys, ys_b = cs_[si]['y']
                        rs, rs_b = cs_[si]['rs']
                        tt('dve', ys[:, 0:w], ys[:, 0:w], rs[:, 0:w], ALU.mult, [ys_b, rs_b], [ys_b])
                    for si, (hp, t0, w) in enumerate(grp):
                        h = 2 * j + hp
                        ys, ys_b = cs_[si]['y']
                        gt_, gt_b = cs_[si]['gt']
                        ro, ro_b = ros[hp]
                        stt(ro[:, t0:t0 + w], ys[:, 0:w], rng[:, h:h + 1], gt_[:, 0:w], ALU.mult, ALU.mult, [ys_b, rng_b, gt_b], [ro_b])
                for hp in range(2):
                    P.dma('sp', ret_s[2 * j + hp], ros[hp][0][:], reads=[ros[hp][1]], writes=[ret_b])

        def phase_rwkv(bi, l):
            A.reset()
            rwmask, rwmask_b = cst['rwmask']
            m6, m6_b = cst['m6']
            mu, mu_b = A.alloc([128, 15], F32)
            mus, mus_b = A.alloc([128, 15, 7], F32)
            w0, w0_b = A.alloc([128, 2, 4], F32)
            a0, a0_b = A.alloc([128, 2, 4], F32)
            kkc, kkc_b = A.alloc([128, 4], F32)
            kac, kac_b = A.alloc([128, 4], F32)
            omk, omk_b = A.alloc([128, 4], F32)
            hrk, hrk_b = A.alloc([128, 4], F32)
            ngc, ngc_b = A.alloc([128, 4], F32)
            P.dma('sp', mu, I['rwkv_mu'][l].rearrange("(j p) -> p j", p=128), writes=[mu_b], slow=True)
            for d in range(2):
                P.dma('sp', w0[:, d, :], I['rwkv_w0'][l, d].rearrange("(j p) -> p j", p=128), writes=[w0_b], slow=True)
                P.dma('sp', a0[:, d, :], I['rwkv_a0'][l, d].rearrange("(j p) -> p j", p=128), writes=[a0_b], slow=True)
            P.dma('sp', kkc, I['rwkv_k_k'][l].rearrange("(j p) -> p j", p=128), writes=[kkc_b], slow=True)
            P.dma('sp', kac, I['rwkv_k_a'][l].rearrange("(j p) -> p j", p=128), writes=[kac_b], slow=True)
            P.dma('sp', hrk, I['rwkv_r_k'][l].rearrange("(j p) -> p j", p=128), writes=[hrk_b], slow=True)
            P.dma('sp', ngc, I['rwkv_norm_g'][l].rearrange("(j p) -> p j", p=128), writes=[ngc_b], slow=True)
            ts('pool', omk, kac, -1.0, 1.0, ALU.mult, ALU.add, [kac_b], [omk_b])
            ts('pool', hrk, hrk, 0.5, None, ALU.mult, None, [hrk_b], [hrk_b])
            ts('pool', mus[:, :, 0], mu, -1.0, 1.0, ALU.mult, ALU.add, [mu_b], [mus_b])
            for g in range(6):
                ts('pool', mus[:, :, 1 + g], mu, m6[:, g:g + 1], None, ALU.mult, None, [mu_b, m6_b], [mus_b])
            w2, w2_b = A.alloc([128, 512], BF16)
            a2, a2_b = A.alloc([128, 512], BF16)
            g2, g2_b = A.alloc([128, 512], BF16)
            P.dma('pool', w2, I['rwkv_w2'][l], writes=[w2_b])
            P.dma('pool', a2, I['rwkv_a2'][l], writes=[a2_b])
            P.dma('pool', g2, I['rwkv_g2'][l], writes=[g2_b])
            tw, tw_b = A.alloc([128, TT], BF16)
            za, za_b = A.alloc([128, TT], BF16)
            sg, sg_b = A.alloc([128, TT], BF16)
            base_off = A.off
            zin, zin_b = A.alloc([128, TT], F32)

            def zb(ci, dst, dst_b):
                P.dma('sp', zin, zrw_s[ci], reads=[zrw_b], writes=[zin_b])
                act(dst, zin, AF.Identity, [zin_b, mus_b], [dst_b], scale=mus[:, ci, 0:1])
                z3 = zin[:, 0:T].rearrange("p (r c) -> p r c", c=64)
                d3 = dst[:, 0:T].rearrange("p (r c) -> p r c", c=64)
                rb = [zin_b, mus_b, dst_b]
                stt(d3[:, :, 1:64], z3[:, :, 0:63], mus[:, ci, 1:2], d3[:, :, 1:64], ALU.mult, ALU.add, rb, [dst_b])
                stt(d3[:, :, 0:63], z3[:, :, 1:64], mus[:, ci, 2:3], d3[:, :, 0:63], ALU.mult, ALU.add, rb, [dst_b])
                stt(dst[:, 64:T], zin[:, 0:T - 64], mus[:, ci, 3:4], dst[:, 64:T], ALU.mult, ALU.add, rb, [dst_b])
                stt(dst[:, 0:T - 64], zin[:, 64:T], mus[:, ci, 4:5], dst[:, 0:T - 64], ALU.mult, ALU.add, rb, [dst_b])
                stt(dst[:, T + 1:TT], zin[:, T:TT - 1], mus[:, ci, 5:6], dst[:, T + 1:TT], ALU.mult, ALU.add, rb, [dst_b])
                stt(dst[:, T:TT - 1], zin[:, T + 1:TT], mus[:, ci, 6:7], dst[:, T:TT - 1], ALU.mult, ALU.add, rb, [dst_b])

            ztmp, ztmp_b = A.alloc([128, TT], F32)
            zb(12, ztmp, ztmp_b)
            act(tw, ztmp, AF.Tanh, [ztmp_b], [tw_b])
            zb(13, ztmp, ztmp_b)
            cp('act', za, ztmp, [ztmp_b], [za_b])
            zb(14, ztmp, ztmp_b)
            act(sg, ztmp, AF.Sigmoid, [ztmp_b], [sg_b])

            WSTOP = os.environ.get('WSTOP', '')
            if WSTOP == 'pro':
                return
            for j in range(4):
                P.barrier()
                A.off = base_off
                zin, zin_b = A.alloc([128, TT], F32)
                zr, zr_b = A.alloc([128, TT], F32)
                zk, zk_b = A.alloc([128, TT], F32)
                zv, zv_b = A.alloc([128, TT], F32)
                kkn, kkn_b = A.alloc([128, TT], F32)
                ksum, ksum_b = A.alloc([128, TT], F32)
                Lw, Lw_b = A.alloc([128, TT], F32)
                Ic, Ic_b = A.alloc([128, TT], F32)
                Aa, Aa_b = A.alloc([128, TT], F32)
                Tk, Tk_b = A.alloc([128, TT], F32)
                stg = [A.alloc([128, TT], BF16) for _ in range(2)]
                rt, rt_b = A.alloc([128, 512], F32)
                sn = [0]

                def emit(idx, d, fn):
                    so, so_b = stg[sn[0] % 2]
                    sn[0] += 1
                    fn(so, so_b)
                    P.dma('sp', rw_s[j, d, :, idx, :], so, reads=[so_b], writes=[rw_b])
                zb(j, zr, zr_b)
                zb(4 + j, zk, zk_b)
                zb(8 + j, zv, zv_b)
                X, X_b = zin, zin_b
                for d in range(2):
                    emit(6, d, lambda so, so_b: cp('act', so, zv, [zv_b], [so_b]))
                act(kkn, zk, AF.Identity, [zk_b, kkc_b], [kkn_b], scale=kkc[:, j:j + 1])
                act(X, kkn, AF.Square, [kkn_b], [X_b])
                for (t0, w) in TILES:
                    ps, psb = psum()
                    mm(ps[:, 0:w], psb, bones_f[:], X[:, t0:t0 + w], True, True, [bones_fb, X_b])
                    act(rt[:, 0:w], ps[:, 0:w], AF.Sqrt, [psb, epsT_b], [rt_b], bias=eps_ap(1e-12))
                    recip(rt[:, 0:w], rt[:, 0:w], [rt_b], [rt_b])
                    tt('dve', kkn[:, t0:t0 + w], kkn[:, t0:t0 + w], rt[:, 0:w], ALU.mult, [kkn_b, rt_b], [kkn_b])
                I3 = Ic.rearrange("p (c t) -> p c t", t=128)
                X3 = X.rearrange("p (c t) -> p c t", t=128)
                L3 = Lw.rearrange("p (c t) -> p c t", t=128)
                totb = I3[:, :, 127:128].broadcast_to([128, NCH, 128])
                for d in range(2):
                    r0, r1 = d * 64, d * 64 + 64
                    for (t0, w) in TILES:
                        ps, psb = psum()
                        mm(ps[:, 0:w], psb, w2[r0:r1, j * 128:(j + 1) * 128], tw[r0:r1, t0:t0 + w], True, True, [w2_b, tw_b])
                        act(Lw[:, t0:t0 + w], ps[:, 0:w], AF.Sigmoid, [psb, w0_b], [Lw_b], bias=w0[:, d, j:j + 1])
                        ps2, ps2b = psum()
                        mm(ps2[:, 0:w], ps2b, a2[r0:r1, j * 128:(j + 1) * 128], za[r0:r1, t0:t0 + w], True, True, [a2_b, za_b])
                        act(Aa[:, t0:t0 + w], ps2[:, 0:w], AF.Sigmoid, [ps2b, a0_b], [Aa_b], bias=a0[:, d, j:j + 1])
                    act(Lw, Lw, AF.Identity, [Lw_b], [Lw_b], scale=DECAY_C)
                    for c in range(NCH):
                        P.op('dve', lambda e, c=c: e.tensor_tensor_scan(out=Ic[:, c * 128:(c + 1) * 128], data0=ones_f[:],
                                                                        data1=Lw[:, c * 128:(c + 1) * 128], initial=0.0,
                                                                        op0=ALU.mult, op1=ALU.add),
                             reads=[ones_fb, Lw_b], writes=[Ic_b])
                    act(Tk, Aa, AF.Identity, [Aa_b, kac_b, omk_b], [Tk_b], bias=omk[:, j:j + 1], scale=kac[:, j:j + 1])
                    tt('dve', Tk, Tk, zk, ALU.mult, [Tk_b, zk_b], [Tk_b])
                    if d == 0:
                        cp('act', ksum, Tk, [Tk_b], [ksum_b])
                    else:
                        tt('dve', ksum, ksum, Tk, ALU.add, [ksum_b, Tk_b], [ksum_b])
                    tt('dve', Aa, Aa, kkn, ALU.mult, [Aa_b, kkn_b], [Aa_b])
                    act(gC[:, j, d, :], I3[:, :, 127], AF.Exp, [Ic_b], [gC_b])
                    mul = lambda a_, a_b: (lambda so, so_b: tt('dve', so, a_, X, ALU.mult, [a_b, X_b], [so_b]))
                    nmul = lambda a_, a_b: (lambda so, so_b: stt(so, a_, -1.0, X, ALU.mult, ALU.mult, [a_b, X_b], [so_b]))
                    if d == 0:
                        act(X, Ic, AF.Exp, [Ic_b], [X_b], scale=-1.0)
                        emit(1, d, mul(Tk, Tk_b))
                        emit(2, d, mul(Aa, Aa_b))
                        tt('dve', X3, totb, I3, ALU.subtract, [Ic_b], [X_b])
                        act(X, X, AF.Exp, [X_b], [X_b])
                        emit(4, d, mul(Tk, Tk_b))
                        emit(5, d, nmul(Aa, Aa_b))
                        act(X, Ic, AF.Exp, [Ic_b], [X_b])
                        emit(3, d, mul(zr, zr_b))
                        tt('dve', X, Ic, Lw, ALU.subtract, [Ic_b, Lw_b], [X_b])
                        act(X, X, AF.Exp, [X_b], [X_b])
                        emit(0, d, mul(kkn, kkn_b))
                    else:
                        tt('dve', X3, totb, I3, ALU.subtract, [Ic_b], [X_b])
                        act(X, X, AF.Exp, [X_b], [X_b])
                        emit(0, d, mul(kkn, kkn_b))
                        emit(3, d, mul(zr, zr_b))
                        tt('dve', Lw, Ic, Lw, ALU.subtract, [Ic_b, Lw_b], [Lw_b])
                        tt('dve', X3, L3, totb, ALU.subtract, [Ic_b, Lw_b], [X_b])
                        act(X, X, AF.Exp, [X_b], [X_b])
                        emit(1, d, mul(Tk, Tk_b))
                        emit(2, d, mul(Aa, Aa_b))
                        act(X, Lw, AF.Exp, [Lw_b], [X_b])
                        emit(4, d, mul(Tk, Tk_b))
                        emit(5, d, nmul(Aa, Aa_b))
                tt('dve', X, zr, ksum, ALU.mult, [zr_b, ksum_b], [X_b])
                act(X, X, AF.Identity, [X_b, hrk_b], [X_b], scale=hrk[:, j:j + 1])
                for (t0, w) in TILES:
                    ps, psb = psum()
                    mm(ps[:, 0:w], psb, bones_f[:], X[:, t0:t0 + w], True, True, [bones_fb, X_b])
                    tt('dve', Lw[:, t0:t0 + w], ps[:, 0:w], zv[:, t0:t0 + w], ALU.mult, [psb, zv_b], [Lw_b])
                    ps2, ps2b = psum()
                    mm(ps2[:, 0:w], ps2b, g2[:, j * 128:(j + 1) * 128], sg[:, t0:t0 + w], True, True, [g2_b, sg_b])
                    cp('act', Ic[:, t0:t0 + w], ps2[:, 0:w], [ps2b], [Ic_b])
                P.dma('sp', bon_s[j, 0], Lw, reads=[Lw_b], writes=[bon_b])
                P.dma('sp', bon_s[j, 1], Ic, reads=[Ic_b], writes=[bon_b])

                if WSTOP == 'w1':
                    return
                P.barrier()
                A.off = base_off
                yacc, _ = A.alloc([128, NCH, 128], F32)
                yacc_bs = [Buf() for _ in range(NCH)]
                w3_off = A.off
                GI = 2
                nslot = GI * 2
                NHI = int(os.environ.get('NHI', '6'))

                def alloc_set():
                    st = {}
                    st['ld'] = [A.alloc([128, 7, 128], BF16) for _ in range(nslot)]
                    st['tok'] = [A.alloc([128, 3, 128], BF16) for _ in range(nslot)]
                    st['Gn'] = [A.alloc([128, 2, 128], F32) for _ in range(nslot * 2)]
                    st['Gb'] = [A.alloc([128, 3, 128], BF16) for _ in range(nslot * 2)]
                    st['XZ'] = [[A.alloc([128, 2, 128], F32), st['Gn'][m_]] for m_ in range(nslot * 2)]
                    st['XZb'] = [[A.alloc([128, 2, 128], BF16) for _ in range(2)] for _ in range(nslot * 2)]
                    st['PT'] = [A.alloc([128, 128], F32) for _ in range(nslot * 2)]
                    st['PTb'] = [A.alloc([128, 128], BF16) for _ in range(nslot * 2)]
                    st['r0'] = [A.alloc([128, 128], BF16) for _ in range(nslot)]
                    st['U'] = st['r0']
                    return st
                sets = [alloc_set(), alloc_set()]
                Hf = [A.alloc([128, 128], F32) for _ in range(2)]
                Hb = [A.alloc([128, 128], BF16) for _ in range(2)]
                for d in range(2):
                    memset('pool', Hf[d][0], 0.0, [Hf[d][1]])
                    memset('pool', Hb[d][0], 0.0, [Hb[d][1]])
                ywritten = set()
                ngroups = NCH // GI

                def prep(g, st):
                    items = []
                    for i in range(g * GI, (g + 1) * GI):
                        for d in range(2):
                            c = (ORDER_F if d == 0 else ORDER_B)[i]
                            sl = (i - g * GI) * 2 + d
                            ld, ld_b = st['ld'][sl]
                            tok, tok_b = st['tok'][sl]
                            P.dma('sp', ld, rw_s[j, d, :, :, c * 128:(c + 1) * 128], reads=[rw_b], writes=[ld_b])
                            ps, psb = psum()
                            psv = ps[:].bitcast(BF16)
                            for n_, idx in enumerate((6, 4, 5)):
                                transp(psv[:, n_ * 128:(n_ + 1) * 128], psb, ld[:, idx, :], ident_b[:], [ld_b, ident_bb])
                            cp('act', tok, psv[:, 0:384].rearrange("p (a b) -> p a b", a=3), [psb], [tok_b])
                            mats = []
                            for hp in range(2):
                                r0, r1 = hp * 64, hp * 64 + 64
                                mi = sl * 2 + hp
                                Gn_, Gn_b = st['Gn'][mi]
                                Gb_, Gb_b = st['Gb'][mi]
                                Qt, Kt, Bt, Rt = ld[r0:r1, 0, :], ld[r0:r1, 1, :], ld[r0:r1, 2, :], ld[r0:r1, 3, :]
                                psA, psAb = psum()
                                psB, psBb = psum()
                                mm(psA[:, 0:128], psAb, Qt, Bt, True, True, [ld_b])
                                mm(psA[:, 128:256], psAb, Bt, Qt, True, True, [ld_b])
                                mm(psA[:, 256:384], psAb, Kt, Qt, True, True, [ld_b])
                                mm(psA[:, 384:512], psAb, Kt, Rt, True, True, [ld_b])
                                mm(psB[:, 0:128], psBb, Bt, Rt, True, True, [ld_b])
                                tt('dve', Gn_[:, 0:2, :], psA[:, 0:256].rearrange("p (a b) -> p a b", a=2), rwmask[:, d, 0:2, :], ALU.mult,
                                   [psAb, rwmask_b], [Gn_b])
                                tt('dve', Gb_[:, 0:2, :], psA[:, 256:512].rearrange("p (a b) -> p a b", a=2), rwmask[:, d, 2:4, :], ALU.mult,
                                   [psAb, rwmask_b], [Gb_b])
                                tt('dve', Gb_[:, 2, :], psB[:, 0:128], rwmask[:, d, 4, :], ALU.mult, [psBb, rwmask_b], [Gb_b])
                                tt('pool', st['PT'][mi][0], ident_f[:], Gn_[:, 1, :], ALU.subtract, [ident_fb, Gn_b], [st['PT'][mi][1]])
                                mats.append(mi)
                            items.append((i, d, c, sl, mats))
                    return items

                def inv_slots(st, items):
                    allm = [mi for it in items for mi in it[4]]
                    state = {'cur': {mi: (st['Gn'][mi][0][:, 1, :], st['Gn'][mi][0][:, 0, :], st['Gn'][mi][1]) for mi in allm}, 'nxt': None}
                    slots = []
                    for lev in range(6):
                        last = lev == 5
                        hi = lev < NHI
                        nxt_hi = (lev + 1) < NHI

                        def sq(lev=lev, last=last, hi=hi, nxt_hi=nxt_hi):
                            nxt = {}
                            for n_i, mi in enumerate(allm):
                                Xm, Zm, XZb_ = state['cur'][mi]
                                ps, psb = psum()
                                mm(ps[:, 0:128], psb, Xm, Zm, True, True, [XZb_])
                                if not last:
                                    mm(ps[:, 128:256], psb, Zm, Xm, True, True, [XZb_])
                                k_ = 1 if last else 2
                                src = ps[:, 0:k_ * 128].rearrange("p (a b) -> p a b", a=k_)
                                e1, e2 = ('dve', 'act') if n_i % 4 == 0 else ('act', 'dve')
                                if hi:
                                    nf, nf_b = st['XZ'][mi][lev % 2]
                                    cp(e1, nf[:, 0:k_, :], src, [psb], [nf_b])
                                    zP = (nf[:, 0, :], nf_b)
                                    if nxt_hi:
                                        nxt[mi] = (nf[:, 1, :], nf[:, 0, :], nf_b, zP)
                                    else:
                                        nb, nb_b = st['XZb'][mi][lev % 2]
                                        cp('pool', nb[:, 0:k_, :], nf[:, 0:k_, :], [nf_b], [nb_b])
                                        nxt[mi] = (nb[:, 1, :], nb[:, 0, :], nb_b, zP)
                                else:
                                    nb, nb_b = st['XZb'][mi][lev % 2]
                                    cp(e1, nb[:, 0:k_, :], src, [psb], [nb_b])
                                    nxt[mi] = (nb[:, 1, :], nb[:, 0, :], nb_b, (nb[:, 0, :], nb_b))
                            state['nxt'] = nxt

                        def pu(lev=lev, hi=hi, nxt_hi=nxt_hi):
                            nxt = state['nxt']
                            for mi in allm:
                                Z2, Z2_b = nxt[mi][3]
                                pf, pf_b = st['PT'][mi]
                                pb, pb_b = st['PTb'][mi]
                                ps, psb = psum()
                                if hi:
                                    mm(ps[:, 0:128], psb, Z2, pf, True, True, [Z2_b, pf_b])
                                    if nxt_hi:
                                        tt('dve', pf, ps[:, 0:128], pf, ALU.add, [psb, pf_b], [pf_b])
                                    else:
                                        tt('dve', pb, ps[:, 0:128], pf, ALU.add, [psb, pf_b], [pb_b])
                                else:
                                    mm(ps[:, 0:128], psb, Z2, pb, True, True, [Z2_b, pb_b])
                                    tt('dve', pb, ps[:, 0:128], pb, ALU.add, [psb, pb_b], [pb_b])
                            state['cur'] = {mi: nxt[mi][0:3] for mi in allm}
                        slots.append(sq)
                        slots.append(pu)
                    return slots

                def seq_stages(st, items, d):
                    stages = []
                    for (i, d_, c, sl, mats) in items:
                        if d_ != d:
                            continue
                        ld, ld_b = st['ld'][sl]
                        tok, tok_b = st['tok'][sl]
                        Hb_, Hb_b = Hb[d]
                        Hf_, Hf_b = Hf[d]
                        rb_, rb_b = st['r0'][sl]
                        U_, U_b = st['U'][sl]

                        def s1(ld=ld, ld_b=ld_b, tok=tok, tok_b=tok_b, Hb_=Hb_, Hb_b=Hb_b, rb_=rb_, rb_b=rb_b, mats=mats):
                            for hp in range(2):
                                r0, r1 = hp * 64, hp * 64 + 64
                                G_, G_b = st['Gb'][mats[hp]]
                                ps, psb = psum()
                                mm(ps[:, 0:64], psb, ld[r0:r1, 0, :], Hb_[r0:r1, r0:r1], True, False, [ld_b, Hb_b])
                                mm(ps[:, 0:64], psb, G_[:, 0, :], tok[:, 0, r0:r1], False, True, [G_b, tok_b])
                                cp('act', rb_[:, r0:r1], ps[:, 0:64], [psb], [rb_b])

                        def s2(rb_=rb_, rb_b=rb_b, U_=U_, U_b=U_b, mats=mats):
                            ps, psb = psum()
                            for hp in range(2):
                                r0, r1 = hp * 64, hp * 64 + 64
                                pt_, pt_b = st['PTb'][mats[hp]]
                                mm(ps[:, r0:r1], psb, pt_, rb_[:, r0:r1], True, True, [pt_b, rb_b])
                            cp('act', U_, ps[:, 0:128], [psb], [U_b])

                        def s3(ld=ld, ld_b=ld_b, tok=tok, tok_b=tok_b, Hb_=Hb_, Hb_b=Hb_b, Hf_=Hf_, Hf_b=Hf_b, U_=U_, U_b=U_b,
                               mats=mats, c=c, d=d):
                            for hp in range(2):
                                r0, r1 = hp * 64, hp * 64 + 64
                                G_, G_b = st['Gb'][mats[hp]]
                                ps, psb = psum()
                                mm(ps[:, 0:64], psb, ld[r0:r1, 3, :], Hb_[r0:r1, r0:r1], True, False, [ld_b, Hb_b])
                                mm(ps[:, 0:64], psb, G_[:, 1, :], tok[:, 0, r0:r1], False, False, [G_b, tok_b])
                                mm(ps[:, 0:64], psb, G_[:, 2, :], U_[:, r0:r1], False, True, [G_b, U_b])
                                if c not in ywritten:
                                    cp('dve', yacc[:, c, r0:r1], ps[:, 0:64], [psb], [yacc_bs[c]])
                                else:
                                    tt('dve', yacc[:, c, r0:r1], ps[:, 0:64], yacc[:, c, r0:r1], ALU.add, [psb, yacc_bs[c]], [yacc_bs[c]])
                            ywritten.add(c)
                            ps, psb = psum()
                            mm(ps[:, 0:128], psb, tok[:, 1, :], tok[:, 0, :], True, False, [tok_b])
                            mm(ps[:, 0:128], psb, tok[:, 2, :], U_, False, True, [tok_b, U_b])
                            stt(Hf_, Hf_, gC[:, j, d, c:c + 1], ps[:, 0:128], ALU.mult, ALU.add, [Hf_b, gC_b, psb], [Hf_b])
                            cp('act', Hb_, Hf_, [Hf_b], [Hb_b])
                        stages += [s1, s2, s3]
                    return stages

                items_cur = prep(0, sets[0])
                for f_ in inv_slots(sets[0], items_cur):
                    f_()
                for g in range(ngroups):
                    st = sets[g % 2]
                    if g + 1 < ngroups:
                        items_nxt = prep(g + 1, sets[(g + 1) % 2])
                        slots = inv_slots(sets[(g + 1) % 2], items_nxt)
                    else:
                        items_nxt, slots = None, []
                    sf = seq_stages(st, items_cur, 0)
                    sb_ = seq_stages(st, items_cur, 1)
                    for s_ in range(max(len(slots), len(sf))):
                        if s_ < len(slots):
                            slots[s_]()
                        if s_ < len(sf):
                            sf[s_]()
                            sb_[s_]()
                    items_cur = items_nxt

                if WSTOP == 'w2':
                    return
                P.barrier()
                A.off = w3_off
                mn, mn_b = A.alloc([128, 36], F32)
                vr, vr_b = A.alloc([128, 36], F32)
                sqb, sqb_b = A.alloc([128, 36, 64], F32)
                ynb, ynb_b = A.alloc([128, NCH, 128], BF16)
                bon, bon_bb = A.alloc([128, TT], F32)
                gg, gg_b = A.alloc([128, TT], F32)
                tmp, tmp_b = A.alloc([128, 1024], F32)
                ro, ro_b = A.alloc([128, TT], BF16)
                P.dma('sp', bon, bon_s[j, 0], reads=[bon_b], writes=[bon_bb])
                P.dma('sp', gg, bon_s[j, 1], reads=[bon_b], writes=[gg_b])
                y4 = yacc.rearrange("p c (h v) -> p (c h) v", h=2)
                yacc_b = Buf()
                P.op('dve', lambda e: e.tensor_reduce(out=mn, in_=y4, axis=AX.X, op=ALU.add), reads=yacc_bs, writes=[mn_b, yacc_b])
                ts('pool', mn, mn, 1.0 / 64, None, ALU.mult, None, [mn_b], [mn_b])
                tt('dve', y4, y4, mn.unsqueeze(2).broadcast_to([128, 36, 64]), ALU.subtract, [yacc_b, mn_b], [yacc_b])
                act(sqb, y4, AF.Square, [yacc_b], [sqb_b])
                P.op('dve', lambda e: e.tensor_reduce(out=vr, in_=sqb, axis=AX.X, op=ALU.add), reads=[sqb_b], writes=[vr_b])
                act(vr, vr, AF.Sqrt, [vr_b, epsT_b], [vr_b], bias=eps_ap(64e-5), scale=1.0 / 64)
                recip(vr, vr, [vr_b], [vr_b])
                tt('dve', ynb.rearrange("p c (h v) -> p (c h) v", h=2), y4, vr.unsqueeze(2).broadcast_to([128, 36, 64]), ALU.mult,
                   [yacc_b, vr_b], [ynb_b])
                for c0 in range(0, NCH, 8):
                    n = min(8, NCH - c0)
                    ps, psb = psum()
                    psv = ps[:].bitcast(BF16)
                    for cc in range(n):
                        transp(psv[:, cc * 128:(cc + 1) * 128], psb, ynb[:, c0 + cc, :], ident_b[:], [ynb_b, ident_bb])
                    cs = slice(c0 * 128, (c0 + n) * 128)
                    stt(tmp[:, 0:n * 128], psv[:, 0:n * 128], ngc[:, j:j + 1], bon[:, cs], ALU.mult, ALU.add, [psb, ngc_b, bon_bb], [tmp_b])
                    tt('dve', ro[:, cs], tmp[:, 0:n * 128], gg[:, cs], ALU.mult, [tmp_b, gg_b], [ro_b])
                P.dma('sp', rwkv_s[j], ro, reads=[ro_b], writes=[rwkv_b])

        def phase_merge(bi, l):
            A.reset()
            last = (l == nlayers - 1)
            tiles = TILES[:4] if last else TILES
            Wa, Wa_b = A.alloc([128, 8, 1024], BF16)
            Wb, Wb_b = A.alloc([128, 4, 1024], BF16)
            Wo, Wo_b = A.alloc([128, 8, 1024], BF16)
            for hf in range(2):
                P.dma('pool', Wa[:, :, hf * 512:(hf + 1) * 512], I['w_branch_a'][l, :, hf * 512:(hf + 1) * 512].rearrange("(k p) n -> p k n", p=128), writes=[Wa_b])
                P.dma('pool', Wb[:, :, hf * 512:(hf + 1) * 512], I['w_branch_b'][l, :, hf * 512:(hf + 1) * 512].rearrange("(k p) n -> p k n", p=128), writes=[Wb_b])
                P.dma('pool', Wo[:, :, hf * 512:(hf + 1) * 512], I['w_out'][l, :, hf * 512:(hf + 1) * 512].rearrange("(k p) n -> p k n", p=128), writes=[Wo_b])
            bufs = []
            for _ in range(2):
                bufs.append((A.alloc([128, 8, 512], BF16), A.alloc([128, 4, 512], BF16), A.alloc([128, 8, 512], BF16), A.alloc([128, 8, 512], BF16)))
            mT, mT_b = A.alloc([128, 8, 512], BF16)
            m1, m1_b = A.alloc([128, 512], F32)
            m2, m2_b = A.alloc([128, 512], F32)
            for ti, (t0, w) in enumerate(tiles):
                (rt_, rt_b), (wt_, wt_b), (ga, ga_b), (gb, gb_b) = bufs[ti % 2]
                r = 2 if t0 >= T else bi
                P.dma('sp', rt_[:, :, 0:w], ret_s[:, :, t0:t0 + w].rearrange("h p t -> p h t"), reads=[ret_b], writes=[rt_b])
                P.dma('sp', wt_[:, :, 0:w], rwkv_s[:, :, t0:t0 + w].rearrange("h p t -> p h t"), reads=[rwkv_b], writes=[wt_b])
                P.dma('sp', ga[:, :, 0:w], gate_s[8:16, :, t0:t0 + w].rearrange("h p t -> p h t"), reads=[gate_b], writes=[ga_b])
                P.dma('sp', gb[:, :, 0:w], gate_s[16:24, :, t0:t0 + w].rearrange("h p t -> p h t"), reads=[gate_b], writes=[gb_b])
                for jo in range(8):
                    ps, psb = psum()
                    for k in range(8):
                        mm(ps[:, 0:w], psb, Wa[:, k, jo * 128:(jo + 1) * 128], rt_[:, k, 0:w], k == 0, k == 7, [Wa_b, rt_b])
                    tt('dve', m1[:, 0:w], ps[:, 0:w], ga[:, jo, 0:w], ALU.mult, [psb, ga_b], [m1_b])
                    ps2, ps2b = psum()
                    for k in range(4):
                        mm(ps2[:, 0:w], ps2b, Wb[:, k, jo * 128:(jo + 1) * 128], wt_[:, k, 0:w], k == 0, k == 3, [Wb_b, wt_b])
                    tt('dve', m2[:, 0:w], ps2[:, 0:w], gb[:, jo, 0:w], ALU.mult, [ps2b, gb_b], [m2_b])
                    tt('dve', mT[:, jo, 0:w], m1[:, 0:w], m2[:, 0:w], ALU.add, [m1_b, m2_b], [mT_b])
                for jo in range(8):
                    ps, psb = psum()
                    for k in range(8):
                        mm(ps[:, 0:w], psb, Wo[:, k, jo * 128:(jo + 1) * 128], mT[:, k, 0:w], k == 0, k == 7, [Wo_b, mT_b])
                    stt(xT[:, jo, t0:t0 + w], ps[:, 0:w], modT[:, l, 16 + jo, r:r + 1], xT[:, jo, t0:t0 + w], ALU.mult, ALU.add,
                        [psb, modT_b, xT_b], [xT_b])

        def phase_ffn(bi, l):
            A.reset()
            last = (l == nlayers - 1)
            sups = [(0, 1024), (1024, 1024)] + ([] if last else [(2048, 256)])
            h2, h2_b = A.alloc([128, 8, 1024], BF16)
            hid, hid_b = A.alloc([128, 22, 1024], BF16)
            sq, sq_b = A.alloc([128, 8, 512], F32)
            rs, rs_b = A.alloc([128, 512], F32)
            tmp, _ = A.alloc([128, 2, 512], F32)
            tmp_b = [Buf(), Buf()]
            wa = [A.alloc([128, 8, 128], BF16) for _ in range(3)]
            wg = [A.alloc([128, 8, 128], BF16) for _ in range(3)]
            w2c = [A.alloc([128, 22, 128], BF16) for _ in range(2)]
            sa = [A.alloc([128, 512], F32) for _ in range(2)]
            si = 0
            for (T0, W) in sups:
                subt = [(t0, min(512, T0 + W - t0)) for t0 in range(T0, T0 + W, 512)]
                norm_mod(lambda k, t0, w: h2[:, k, t0 - T0:t0 - T0 + w], h2_b,
                         lambda k, r: (modA[:, l, 1, k, (bi if r is None else r):(bi if r is None else r) + 1], [modA_b]),
                         lambda k, r: (modT[:, l, 24 + k, (bi if r is None else r):(bi if r is None else r) + 1], [modT_b]),
                         sq, sq_b, rs, rs_b, tmp, tmp_b, 1e-6, subt)
                for jh in range(22):
                    wa_, wa_b = wa[jh % 3]
                    wg_, wg_b = wg[jh % 3]
                    P.dma('pool', wa_, I['ffn_w13'][l, :, jh * 128:(jh + 1) * 128].rearrange("(k p) n -> p k n", p=128), writes=[wa_b])
                    P.dma('pool', wg_, I['ffn_w13'][l, :, FH + jh * 128:FH + (jh + 1) * 128].rearrange("(k p) n -> p k n", p=128), writes=[wg_b])
                    for (t0, w) in subt:
                        o0 = t0 - T0
                        ps, psb = psum()
                        for k in range(8):
                            mm(ps[:, 0:w], psb, wa_[:, k, :], h2[:, k, o0:o0 + w], k == 0, k == 7, [wa_b, h2_b])
                        ps2, ps2b = psum()
                        for k in range(8):
                            mm(ps2[:, 0:w], ps2b, wg_[:, k, :], h2[:, k, o0:o0 + w], k == 0, k == 7, [wg_b, h2_b])
                        sa_, sa_b = sa[si % 2]
                        si += 1
                        act(sa_[:, 0:w], ps[:, 0:w], AF.Silu, [psb], [sa_b])
                        tt('dve', hid[:, jh, o0:o0 + w], ps2[:, 0:w], sa_[:, 0:w], ALU.mult, [ps2b, sa_b], [hid_b])
                for jo in range(8):
                    w2_, w2_b = w2c[jo % 2]
                    for hf in range(2):
                        P.dma('pool', w2_[:, hf * 11:(hf + 1) * 11, :],
                              I['ffn_w2'][l, hf * 1408:(hf + 1) * 1408, jo * 128:(jo + 1) * 128].rearrange("(k p) n -> p k n", p=128), writes=[w2_b])
                    for (t0, w) in subt:
                        o0 = t0 - T0
                        r = 2 if t0 >= T else bi
                        ps, psb = psum()
                        for k in range(22):
                            mm(ps[:, 0:w], psb, w2_[:, k, :], hid[:, k, o0:o0 + w], k == 0, k == 21, [w2_b, hid_b])
                        stt(xT[:, jo, t0:t0 + w], ps[:, 0:w], modT[:, l, 40 + jo, r:r + 1], xT[:, jo, t0:t0 + w], ALU.mult, ALU.add,
                            [psb, modT_b, xT_b], [xT_b])

        def phase_final(bi):
            A.reset()
            yf, yf_b = A.alloc([128, 8, 512], F32)
            sq, sq_b = A.alloc([128, 8, 512], F32)
            rs, rs_b = A.alloc([128, 512], F32)
            tmp, _ = A.alloc([128, 2, 512], F32)
            tmp_b = [Buf(), Buf()]
            ost = [A.alloc([128, D], F32) for _ in range(2)]
            oi = 0
            for (t0, w) in TILES[:4]:
                norm_mod(lambda k, t0_, w_: yf[:, k, 0:w_], yf_b,
                         lambda k, r: (fng[:, k:k + 1], [fng_b]),
                         lambda k, r: (zcol[:, 0:1], [zcol_b]),
                         sq, sq_b, rs, rs_b, tmp, tmp_b, 1e-6, [(t0, w)])
                for cc in range(4):
                    o_, o_b = ost[oi % 2]
                    oi += 1
                    for hf in range(2):
                        ps, psb = psum()
                        for kk in range(4):
                            k = hf * 4 + kk
                            transp(ps[:, kk * 128:(kk + 1) * 128], psb, yf[:, k, cc * 128:(cc + 1) * 128], ident_f[:], [yf_b, ident_fb])
                        cp('act' if hf else 'dve', o_[:, hf * 512:(hf + 1) * 512], ps[:], [psb], [o_b])
                    P.dma('sp', out[bi, t0 + cc * 128:t0 + (cc + 1) * 128, :], o_, reads=[o_b], writes=[out_b])
        for bi in range(nbatch):
            A.reset()
            stg = [A.alloc([128, D], F32) for _ in range(2)]
            for c in range(NCH):
                s_, s_b = stg[c % 2]
                src_ = I['x'][bi, c * 128:(c + 1) * 128, :] if c < 16 else I['ctx'][bi, (c - 16) * 128:(c - 15) * 128, :]
                P.dma('sp', s_[:], src_, writes=[s_b])
                for hf in range(2):
                    ps, psb = psum()
                    for kk in range(4):
                        k = hf * 4 + kk
                        transp(ps[:, kk * 128:(kk + 1) * 128], psb, s_[:, k * 128:(k + 1) * 128], ident_f[:], [s_b, ident_fb])
                    cp('act' if hf else 'dve', xT[:, hf * 4:(hf + 1) * 4, c * 128:(c + 1) * 128],
                       ps[:].rearrange("p (a b) -> p a b", a=4), [psb], [xT_b])
            P.barrier()

            for l in range(nlayers):
                A.reset()
                hT, hT_b = A.alloc([128, 8, TT], BF16)
                sq, sq_b = A.alloc([128, 8, 512], F32)
                rs, rs_b = A.alloc([128, 512], F32)
                tmp, _ = A.alloc([128, 2, 512], F32)
                tmp_b = [Buf(), Buf()]
                norm_mod(lambda k, t0, w: hT[:, k, t0:t0 + w], hT_b,
                         lambda k, r: (modA[:, l, 0, k, (bi if r is None else r):(bi if r is None else r) + 1], [modA_b]),
                         lambda k, r: (modT[:, l, 0 + k, (bi if r is None else r):(bi if r is None else r) + 1], [modT_b]),
                         sq, sq_b, rs, rs_b, tmp, tmp_b, 1e-6, TILES)
                if 'dbg_h' in debug:
                    for k in range(8):
                        P.dma('sp', dbg_h[k], hT[:, k, :], reads=[hT_b], writes=[dbg_hb])
                P.barrier()
                A.off = 8 * TT // 2
                rc, rc_b = A.alloc([128, TT], F32)
                rsn, rsn_b = A.alloc([128, TT], F32)
                P.dma('sp', rc[:], C['ropecos'], writes=[rc_b])
                P.dma('sp', rsn[:], C['ropesin'], writes=[rsn_b])
                wch = [A.alloc([128, 8, 128], BF16) for _ in range(4)]
                wn = [0]

                def loadw(src2d):
                    w_, w_b = wch[wn[0] % 4]
                    wn[0] += 1
                    P.dma('pool', w_, src2d.rearrange("(k p) n -> p k n", p=128), writes=[w_b])
                    return w_, w_b
                hsrc = lambda k, t0, w: hT[:, k, t0:t0 + w]
                stgb = [A.alloc([128, TT], BF16) for _ in range(2)]
                stgf = [A.alloc([128, TT], F32) for _ in range(2)]
                t1, t1_b = A.alloc([128, 512], F32)
                t2, t2_b = A.alloc([128, 512], F32)
                sn = [0]
                for qk in range(2):
                    for j in range(4):
                        c0 = qk * 512 + j * 128
                        w_, w_b = loadw(I['w_in'][l, :, c0:c0 + 128])
                        wr_, wr_b = loadw(I['w_rot'][l, :, c0:c0 + 128])
                        so, so_b = stgb[sn[0] % 2]
                        sn[0] += 1
                        for (t0, w) in TILES:
                            ps, psb = psum()
                            ps2, ps2b = psum()
                            for k in range(8):
                                mm(ps[:, 0:w], psb, w_[:, k, :], hsrc(k, t0, w), k == 0, k == 7, [w_b, hT_b])
                            for k in range(8):
                                mm(ps2[:, 0:w], ps2b, wr_[:, k, :], hsrc(k, t0, w), k == 0, k == 7, [wr_b, hT_b])
                            tt('dve', t1[:, 0:w], ps[:, 0:w], rc[:, t0:t0 + w], ALU.mult, [psb, rc_b], [t1_b])
                            tt('dve', t2[:, 0:w], ps2[:, 0:w], rsn[:, t0:t0 + w], ALU.mult, [ps2b, rsn_b], [t2_b])
                            tt('dve', so[:, t0:t0 + w], t1[:, 0:w], t2[:, 0:w], ALU.add, [t1_b, t2_b], [so_b])
                        P.dma('sp', qk_s[qk * 4 + j], so[:], reads=[so_b], writes=[qk_b])
                for g in range(3):
                    base = [2048, 4992, 6016][g]
                    fn = AF.Silu if g == 0 else AF.Sigmoid
                    for j in range(8):
                        w_, w_b = loadw(I['w_in'][l, :, base + j * 128:base + (j + 1) * 128])
                        so, so_b = stgb[sn[0] % 2]
                        sn[0] += 1
                        dense_fm(w_, w_b, 8, hsrc, hT_b, TILES,
                                 lambda ps, psb, t0, w, so=so, so_b=so_b, fn=fn: act(so[:, t0:t0 + w], ps[:, 0:w], fn, [psb], [so_b]))
                        P.dma('sp', gate_s[g * 8 + j], so[:], reads=[so_b], writes=[gate_b])
                for j in range(15):
                    w_, w_b = loadw(I['w_in'][l, :, 3072 + j * 128:3072 + (j + 1) * 128])
                    so, so_b = stgf[j % 2]
                    dense_fm(w_, w_b, 8, hsrc, hT_b, TILES,
                             lambda ps, psb, t0, w, so=so, so_b=so_b: cp('act', so[:, t0:t0 + w], ps[:, 0:w], [psb], [so_b]))
                    P.dma('sp', zrw_s[j], so[:], reads=[so_b], writes=[zrw_b])
                P.barrier()
                A.off = 8 * TT // 2
                wv = [A.alloc([128, 8, 512], BF16) for _ in range(2)]
                for hf in range(2):
                    P.dma('pool', wv[hf][0], I['w_in'][l, :, 1024 + hf * 512:1024 + (hf + 1) * 512].rearrange("(k p) n -> p k n", p=128),
                          writes=[wv[hf][1]])
                vst = [A.alloc([128, 1024], BF16) for _ in range(2)]
                for c in range(NCH):
                    vo, vo_b = vst[c % 2]
                    for hf in range(2):
                        ps, psb = psum()
                        for k in range(8):
                            mm(ps[:], psb, hT[:, k, c * 128:(c + 1) * 128], wv[hf][0][:, k, :], k == 0, k == 7, [hT_b, wv[hf][1]])
                        cp('act' if hf else 'dve', vo[:, hf * 512:(hf + 1) * 512], ps[:], [psb], [vo_b])
                    P.dma('sp', v_s[c], vo[:], reads=[vo_b], writes=[v_b])
                P.barrier()
                if stop_after == 'A':
                    break

                phase_ret(bi, l)
                P.barrier()
                if stop_after == 'R':
                    break
                phase_rwkv(bi, l)
                P.barrier()
                if stop_after == 'W':
                    break
                phase_merge(bi, l)
                P.barrier()
                if stop_after == 'G':
                    break
                phase_ffn(bi, l)
                P.barrier()
            if 'dbg_x' in debug:
                for k in range(8):
                    P.dma('sp', dbg_x[k], xT[:, k, :], reads=[xT_b], writes=[dbg_xb])
            if stop_after is None:
                phase_final(bi)
            P.barrier()

        P.barrier()
        with nc.Block() as block:
            @block.sync
            def _(e):
                for f in P.stream['sp']:
                    f(e)

            @block.tensor
            def _(e):
                for f in P.stream['pe']:
                    f(e)

            @block.scalar
            def _(e):
                for f in P.stream['act']:
                    f(e)

            @block.vector
            def _(e):
                for f in P.stream['dve']:
                    f(e)

            @block.gpsimd
            def _(e):
                for f in P.stream['pool']:
                    f(e)
    return nc


def prep_inputs(inputs):
    consts = _host_consts()
    f = lambda a: np.ascontiguousarray(np.asarray(a, dtype=np.float32))
    w_in = f(inputs['w_in'])
    perm = np.zeros(1024, np.int64)
    for cidx in range(1024):
        h, i = divmod(cidx % 512, 64)
        ii = i % 32
        partner = i + 16 if ii < 16 else i - 16
        perm[cidx] = (cidx // 512) * 512 + h * 64 + partner
    w_rot = np.ascontiguousarray(w_in[:, :, perm])
    shared = {
        'c_ctx': f(inputs['c_ctx']), 'mod_w': f(inputs['mod_w']), 'mod_b': f(inputs['mod_b']),
        'norm1_g': f(inputs['norm1_g']), 'norm2_g': f(inputs['norm2_g']), 'w_in': w_in, 'w_rot': w_rot,
        'ret_decay': f(inputs['ret_decay']).reshape(NL, 16), 'ret_norm_g': f(inputs['ret_norm_g']),
        'rwkv_mu': f(inputs['rwkv_mu']), 'rwkv_w0': f(inputs['rwkv_w0']),
        'rwkv_w2': f(inputs['rwkv_w2']).reshape(NL, 128, 512), 'rwkv_a0': f(inputs['rwkv_a0']),
        'rwkv_a2': f(inputs['rwkv_a2']).reshape(NL, 128, 512), 'rwkv_g2': f(inputs['rwkv_g2']),
        'rwkv_k_k': f(inputs['rwkv_k_k']), 'rwkv_k_a': f(inputs['rwkv_k_a']),
        'rwkv_r_k': f(inputs['rwkv_r_k']).reshape(NL, 512), 'rwkv_norm_g': f(inputs['rwkv_norm_g']),
        'w_branch_a': f(inputs['w_branch_a']), 'w_branch_b': f(inputs['w_branch_b']), 'w_out': f(inputs['w_out']),
        'ffn_w13': f(inputs['ffn_w13']), 'ffn_w2': f(inputs['ffn_w2']), 'final_norm_g': f(inputs['final_norm_g']),
    }
    shared.update(consts)
    x = f(inputs['x']); c = f(inputs['c']); ctx = f(inputs['ctx'])
    in_maps = []
    for i in range(8):
        m = dict(shared)
        m['x'] = x[i * NB:(i + 1) * NB]
        m['c'] = c[i * NB:(i + 1) * NB]
        m['ctx'] = ctx[i * NB:(i + 1) * NB]
        in_maps.append(m)
    return in_maps


def kernel(**inputs):
    in_maps = prep_inputs(inputs)
    nc = build()
    res = run_bass_kernel_spmd(nc, in_maps, core_ids=list(range(8)))
    return np.concatenate([np.asarray(r['out'], dtype=np.float32) for r in res.results], axis=0)
```

```python
import contextlib
import os
import numpy as np
import ml_dtypes
import concourse.bass as bass
import concourse.mybir as mybir
from concourse.bass_utils import run_bass_kernel_spmd

F32 = mybir.dt.float32
F32R = mybir.dt.float32r
BF16 = mybir.dt.bfloat16
AF = mybir.ActivationFunctionType
ALU = mybir.AluOpType
AX = mybir.AxisListType

D = 1024
T = 2048
TC = 256
TT = 2304
NCH = 18
NL = 2
NB = 2
FH = 2816
TILES = [(0, 512), (512, 512), (1024, 512), (1536, 512), (2048, 256)]
ORDER_F = [16, 17] + list(range(16))
ORDER_B = [17, 16] + list(range(15, -1, -1))
DECAY_C = -0.6065306597126334
EPOCH = 20000
ENGS = ('pe', 'act', 'dve', 'pool', 'sp')


class Buf:
    __slots__ = ('w', 'r')

    def __init__(self):
        self.w = None
        self.r = {}


class Prog:
    def __init__(self, nc, stack):
        self.nc = nc
        self.stack = stack
        self.stream = {e: [] for e in ENGS}
        self.n = {e: 0 for e in ENGS}
        self.seen = {e: {} for e in ENGS}
        self.sems = {}
        self.dcount = {}
        self.dnext = {'sp': 0, 'pool': 0}
        self.nslots = {'sp': 12, 'pool': 6}

    def _sem(self, key):
        if key not in self.sems:
            self.sems[key] = self.stack.enter_context(self.nc.semaphore("s_" + "_".join(str(k) for k in key)))
        return self.sems[key]

    def _deps(self, eng, reads, writes):
        deps = {}

        def add(key, val):
            if deps.get(key, 0) < val:
                deps[key] = val
        for b in reads:
            if b.w is not None:
                add(*b.w)
        for b in writes:
            if b.w is not None:
                add(*b.w)
            for k, v in b.r.items():
                add(k, v)
        waits = []
        for key, val in deps.items():
            if key[0] == 'e' and key[1] == 'pe' and eng == 'pe':
                continue
            if self.seen[eng].get(key, 0) >= val:
                continue
            self.seen[eng][key] = val
            waits.append((self._sem(key), val))
        return waits

    def _mark(self, tok, reads, writes):
        key, val = tok
        for b in reads:
            if b.r.get(key, 0) < val:
                b.r[key] = val
        for b in writes:
            b.w = tok
            b.r = {}

    def op(self, eng, fn, reads=(), writes=()):
        waits = self._deps(eng, reads, writes)
        i = self.n[eng]
        self.n[eng] += 1
        key = ('e', eng, i // EPOCH)
        val = i % EPOCH + 1
        mysem = self._sem(key)

        def run(e, waits=waits, fn=fn, mysem=mysem):
            for s, v in waits:
                e.wait_ge(s, v)
            fn(e).then_inc(mysem, 1)
        self.stream[eng].append(run)
        self._mark((key, val), reads, writes)

    def dma(self, q, out, in_, reads=(), writes=(), slow=False):
        idx = self.dnext[q]
        self.dnext[q] = (idx + 1) % self.nslots[q]
        key = ('d', q, idx)
        cnt = self.dcount.get(key, 0)
        waits = self._deps(q, reads, writes)
        sem = self._sem(key)
        if cnt > 0 and self.seen[q].get(key, 0) < 16 * cnt:
            self.seen[q][key] = 16 * cnt
            waits.append((sem, 16 * cnt))
        self.dcount[key] = cnt + 1

        def run(e, waits=waits, out=out, in_=in_, sem=sem, slow=slow):
            for s, v in waits:
                e.wait_ge(s, v)
            if slow:
                e.dma_start(out=out, in_=in_, allow_slow_non_contiguous=True).then_inc(sem, 16)
            else:
                e.dma_start(out=out, in_=in_).then_inc(sem, 16)
        self.stream[q].append(run)
        self._mark((key, 16 * (cnt + 1)), reads, writes)

    def barrier(self):
        toks = []
        for e in ('pe', 'act', 'dve', 'pool'):
            if self.n[e] > 0:
                i = self.n[e] - 1
                toks.append((e, ('e', e, i // EPOCH), i % EPOCH + 1))
        for key, cnt in self.dcount.items():
            toks.append((None, key, 16 * cnt))
        for eng in ENGS:
            waits = []
            for src, key, val in toks:
                if src == eng:
                    continue
                if self.seen[eng].get(key, 0) >= val:
                    continue
                self.seen[eng][key] = val
                waits.append((self._sem(key), val))
            if waits:
                def run(e, waits=waits):
                    for s, v in waits:
                        e.wait_ge(s, v)
                self.stream[eng].append(run)


def _host_consts():
    c = {}
    c['ident_f'] = np.eye(128, dtype=np.float32)
    c['ident_b'] = np.eye(128, dtype=np.float32).astype(ml_dtypes.bfloat16)
    t = np.arange(T)
    row = (t // 64).astype(np.float32)
    col = (t % 64).astype(np.float32)
    inv = (10000.0 ** (-np.arange(16, dtype=np.float32) / 16)).astype(np.float32)
    cos = np.ones((128, TT), np.float32)
    sin = np.zeros((128, TT), np.float32)
    for p in range(128):
        i = p % 64
        pos = row if i < 32 else col
        ii = i % 32
        f = ii % 16
        ang = (pos * inv[f]).astype(np.float32)
        cos[p, :T] = np.cos(ang)
        sin[p, :T] = -np.sin(ang) if ii < 16 else np.sin(ang)
    c['ropecos'] = cos
    c['ropesin'] = sin
    m6 = np.zeros((128, 6), np.float32)
    for p in range(128):
        m6[p, p % 4] = 1.0
        m6[p, 4 + p % 2] = 1.0
    c['m6'] = m6
    rm = np.ones((128, TT), np.float32)
    rm[:, ::128] = 0.0
    c['rmask'] = rm
    a = np.arange(128)[:, None]
    b = np.arange(128)[None, :]
    Lm = (a > b).astype(np.float32)
    Um = (a < b).astype(np.float32)
    UE = (a <= b).astype(np.float32)
    rw = np.zeros((128, 2, 5, 128), np.float32)
    rw[:, 0] = np.stack([Lm, Um, Um, UE, -UE], 1)
    rw[:, 1] = np.stack([Um, Lm, Lm, Lm, -Lm], 1)
    c['rwmask'] = rw
    retD = np.zeros((128, 2, 128), np.float32)
    retM = np.zeros((128, 2, 128), np.float32)
    retD[:, 0] = np.maximum(b - a, 0)
    retM[:, 0] = (a <= b)
    retD[:, 1] = np.maximum(a - b, 0)
    retM[:, 1] = (a > b)
    c['retD'] = retD
    c['retM'] = retM
    qdt = np.zeros((128, 2, 128), np.float32)
    qdt[:, 0] = (b + 1)
    qdt[:, 1] = (128 - b)
    c['qdt'] = qdt
    kdt = np.zeros((128, 2), np.float32)
    kdt[:, 0] = 127 - np.arange(128)
    kdt[:, 1] = np.arange(128)
    c['kdt'] = kdt
    return c


CONST_SHAPES = {
    'ident_f': ([128, 128], F32), 'ident_b': ([128, 128], BF16), 'ropecos': ([128, TT], F32),
    'ropesin': ([128, TT], F32), 'm6': ([128, 6], F32), 'rmask': ([128, TT], F32),
    'rwmask': ([128, 2, 5, 128], F32), 'retD': ([128, 2, 128], F32), 'retM': ([128, 2, 128], F32),
    'qdt': ([128, 2, 128], F32), 'kdt': ([128, 2], F32),
}

IN_SHAPES = {
    'x': [NB, T, D], 'c': [NB, D], 'ctx': [NB, TC, D], 'c_ctx': [D], 'mod_w': [NL, D, 6 * D],
    'mod_b': [NL, 6 * D], 'norm1_g': [NL, D], 'norm2_g': [NL, D], 'w_in': [NL, D, 7040],
    'w_rot': [NL, D, 1024],
    'ret_decay': [NL, 16], 'ret_norm_g': [NL, D], 'rwkv_mu': [NL, 1920], 'rwkv_w0': [NL, 2, 512],
    'rwkv_w2': [NL, 128, 512], 'rwkv_a0': [NL, 2, 512], 'rwkv_a2': [NL, 128, 512], 'rwkv_g2': [NL, 128, 512],
    'rwkv_k_k': [NL, 512], 'rwkv_k_a': [NL, 512], 'rwkv_r_k': [NL, 512], 'rwkv_norm_g': [NL, 512],
    'w_branch_a': [NL, D, D], 'w_branch_b': [NL, 512, D], 'w_out': [NL, D, D], 'ffn_w13': [NL, D, 2 * FH],
    'ffn_w2': [NL, FH, D], 'final_norm_g': [D],
}


def build(debug=None, nlayers=NL, nbatch=NB, stop_after=None):
    debug = debug or []
    nc = bass.Bass("TRN2", target_bir_lowering=False)
    stack = contextlib.ExitStack()
    with stack:
        P = Prog(nc, stack)
        I = {k: nc.dram_tensor(k, s, F32, kind="ExternalInput").ap() for k, s in IN_SHAPES.items()}
        C = {k: nc.dram_tensor(k, s, dt, kind="ExternalInput").ap() for k, (s, dt) in CONST_SHAPES.items()}
        out = nc.dram_tensor("out", [NB, T, D], F32, kind="ExternalOutput").ap()
        out_b = Buf()

        def scratch(name, shape, dt):
            kind = "ExternalOutput" if name in debug else "Internal"
            return nc.dram_tensor(name, shape, dt, kind=kind).ap(), Buf()
        qk_s, qk_b = scratch("qk_s", [8, 128, TT], BF16)
        v_s, v_b = scratch("v_s", [NCH, 128, 1024], BF16)
        gate_s, gate_b = scratch("gate_s", [24, 128, TT], BF16)
        zrw_s, zrw_b = scratch("zrw_s", [15, 128, TT], F32)
        ret_s, ret_b = scratch("ret_s", [8, 128, TT], BF16)
        rw_s, rw_b = scratch("rw_s", [4, 2, 128, 7, TT], BF16)
        bon_s, bon_b = scratch("bon_s", [4, 2, 128, TT], F32)
        rwkv_s, rwkv_b = scratch("rwkv_s", [4, 128, TT], BF16)
        dbg_h, dbg_hb = scratch("dbg_h", [8, 128, TT], BF16)
        dbg_x, dbg_xb = scratch("dbg_x", [8, 128, TT], F32)

        def sb(name, shape, dt):
            return stack.enter_context(nc.sbuf_tensor(name, shape, dt)), Buf()
        xT, xT_b = sb("xT", [128, 8, TT], F32)
        AW = 31616
        arena, _ = sb("arena", [128, AW], F32)
        cst = {}
        for k in ('ident_f', 'ident_b', 'm6', 'rwmask', 'retD', 'retM', 'qdt', 'kdt'):
            cst[k] = sb("c_" + k, CONST_SHAPES[k][0], CONST_SHAPES[k][1])
        ones_f, ones_fb = sb("ones_f", [128, 128], F32)
        bones_f, bones_fb = sb("bones_f", [128, 128], F32)
        modT, modT_b = sb("modT", [128, NL, 48, 3], F32)
        modA, modA_b = sb("modA", [128, NL, 2, 8, 3], F32)
        gC, gC_b = sb("gC", [128, 4, 2, NCH], F32)
        pst = []
        for i in range(8):
            t_ = stack.enter_context(nc.psum_tensor("ps%d" % i, [128, 512], F32))
            pst.append((t_, Buf()))
        pi = [0]

        def psum():
            i = pi[0]
            pi[0] = (i + 1) % 8
            return pst[i]

        class Arena:
            def __init__(self):
                self.off = 0

            def reset(self):
                self.off = 0

            def alloc(self, shape, dt):
                n = int(np.prod(shape[1:]))
                words = n if dt in (F32, F32R) else (n + 1) // 2
                assert self.off + words <= AW, (self.off, words, AW)
                ap = arena[:, self.off:self.off + words]
                self.off += words
                if dt == F32R:
                    ap = ap.bitcast(F32R)
                if dt == BF16:
                    ap = ap.bitcast(BF16)
                    if n % 2:
                        ap = ap[:, 0:n]
                if len(shape) == 3:
                    ap = ap.rearrange("p (a b) -> p a b", a=shape[1])
                elif len(shape) == 4:
                    ap = ap.rearrange("p (a b c) -> p a b c", a=shape[1], b=shape[2])
                return ap, Buf()
        A = Arena()

        def mm(ps, psb, lhsT, rhs, start, stop, reads):
            P.op('pe', lambda e: e.matmul(ps, lhsT=lhsT, rhs=rhs, start=start, stop=stop),
                 reads=reads, writes=[psb])

        def transp(ps, psb, in_, ident, reads):
            P.op('pe', lambda e: e.transpose(out=ps, in_=in_, identity=ident), reads=reads, writes=[psb])

        def act(out_, in_, func, reads, writes, bias=0.0, scale=1.0):
            P.op('act', lambda e: e.activation(out=out_, in_=in_, func=func, bias=bias, scale=scale),
                 reads=reads, writes=writes)

        def tt(eng, out_, in0, in1, op, reads, writes):
            P.op(eng, lambda e: e.tensor_tensor(out=out_, in0=in0, in1=in1, op=op), reads=reads, writes=writes)

        def ts(eng, out_, in0, s1, s2, op0, op1, reads, writes):
            if op1 is None:
                P.op(eng, lambda e: e.tensor_scalar(out=out_, in0=in0, scalar1=s1, scalar2=None, op0=op0),
                     reads=reads, writes=writes)
            else:
                P.op(eng, lambda e: e.tensor_scalar(out=out_, in0=in0, scalar1=s1, scalar2=s2, op0=op0, op1=op1),
                     reads=reads, writes=writes)

        def stt(out_, in0, scalar, in1, op0, op1, reads, writes):
            P.op('dve', lambda e: e.scalar_tensor_tensor(out=out_, in0=in0, scalar=scalar, in1=in1, op0=op0, op1=op1),
                 reads=reads, writes=writes)

        def cp(eng, out_, in_, reads, writes):
            if eng == 'act':
                P.op('act', lambda e: e.copy(out=out_, in_=in_), reads=reads, writes=writes)
            else:
                P.op(eng, lambda e: e.tensor_copy(out=out_, in_=in_), reads=reads, writes=writes)

        def recip(out_, in_, reads, writes):
            P.op('dve', lambda e: e.reciprocal(out=out_, in_=in_), reads=reads, writes=writes)

        def memset(eng, ap, val, writes):
            P.op(eng, lambda e: e.memset(ap, val), writes=writes)

        for k in cst:
            P.dma('sp', cst[k][0][:], C[k], writes=[cst[k][1]])
        ident_f, ident_fb = cst['ident_f']
        ident_b, ident_bb = cst['ident_b']
        memset('pool', ones_f[:], 1.0, [ones_fb])
        memset('pool', bones_f[:], 0.0, [bones_fb])
        memset('pool', bones_f[0:64, 0:64], 1.0, [bones_fb])
        memset('pool', bones_f[64:128, 64:128], 1.0, [bones_fb])

        A.reset()
        c3, c3_b = A.alloc([128, 8, 3], F32)
        s3, s3_b = A.alloc([128, 8, 3], F32)
        mb, mb_b = A.alloc([128, NL, 48], F32)
        ng, ng_b = A.alloc([128, NL, 2, 8], F32)
        for r in range(NB):
            P.dma('sp', c3[:, :, r], I['c'][r].rearrange("(k p) -> p k", p=128), writes=[c3_b], slow=True)
        P.dma('sp', c3[:, :, 2], I['c_ctx'].rearrange("(k p) -> p k", p=128), writes=[c3_b], slow=True)
        for l in range(NL):
            P.dma('sp', mb[:, l, :], I['mod_b'][l].rearrange("(j p) -> p j", p=128), writes=[mb_b], slow=True)
            P.dma('sp', ng[:, l, 0, :], I['norm1_g'][l].rearrange("(k p) -> p k", p=128), writes=[ng_b], slow=True)
            P.dma('sp', ng[:, l, 1, :], I['norm2_g'][l].rearrange("(k p) -> p k", p=128), writes=[ng_b], slow=True)
        act(s3, c3, AF.Silu, [c3_b], [s3_b])
        wst = [A.alloc([128, 8, 512], F32) for _ in range(2)]
        wi = 0
        for l in range(nlayers):
            for cc in range(12):
                w_, w_b = wst[wi % 2]
                wi += 1
                P.dma('sp', w_, I['mod_w'][l, :, cc * 512:(cc + 1) * 512].rearrange("(k p) n -> p k n", p=128), writes=[w_b])
                for jj in range(4):
                    j = cc * 4 + jj
                    ps, psb = psum()
                    for k in range(8):
                        mm(ps[:, 0:3], psb, w_[:, k, jj * 128:(jj + 1) * 128], s3[:, k, :], k == 0, k == 7, [w_b, s3_b])
                    act(modT[:, l, j, :], ps[:, 0:3], AF.Identity, [psb, mb_b], [modT_b], bias=mb[:, l, j:j + 1])
            for n_ in range(2):
                j0 = 8 if n_ == 0 else 32
                ts('dve', modA[:, l, n_, :, :], modT[:, l, j0:j0 + 8, :], 1.0, None, ALU.add, None, [modT_b], [modA_b])
                tt('dve', modA[:, l, n_, :, :], modA[:, l, n_, :, :], ng[:, l, n_, :].unsqueeze(2).broadcast_to([128, 8, 3]),
                   ALU.mult, [modA_b, ng_b], [modA_b])
        P.barrier()

        def norm_mod(dst, dst_b, Afn, shfn, sq, sq_b, rs, rs_b, tmp, tmp_b, eps, tiles):
            for (t0, w) in tiles:
                for k in range(8):
                    act(sq[:, k, 0:w], xT[:, k, t0:t0 + w], AF.Square, [xT_b], [sq_b])
                ps, psb = psum()
                for k in range(8):
                    mm(ps[:, 0:w], psb, ones_f[:], sq[:, k, 0:w], k == 0, k == 7, [ones_fb, sq_b])
                act(rs[:, 0:w], ps[:, 0:w], AF.Sqrt, [psb], [rs_b], bias=eps_ap(eps), scale=1.0 / D)
                recip(rs[:, 0:w], rs[:, 0:w], [rs_b], [rs_b])
                r = 2 if t0 >= T else None
                for k in range(8):
                    tt('dve', tmp[:, k % 2, 0:w], xT[:, k, t0:t0 + w], rs[:, 0:w], ALU.mult,
                       [xT_b, rs_b], [tmp_b[k % 2]])
                    a_ap, a_bufs = Afn(k, r)
                    s_ap, s_bufs = shfn(k, r)
                    act(dst(k, t0, w), tmp[:, k % 2, 0:w], AF.Identity, [tmp_b[k % 2]] + a_bufs + s_bufs, [dst_b],
                        bias=s_ap, scale=a_ap)

        epsT, epsT_b = sb("epsT", [128, 4], F32)
        memset('pool', epsT[:, 0:1], 1e-6, [epsT_b])
        memset('pool', epsT[:, 1:2], 1e-5 * 64.0, [epsT_b])
        memset('pool', epsT[:, 2:3], 64e-5, [epsT_b])
        memset('pool', epsT[:, 3:4], 1e-12, [epsT_b])
        EPSI = {1e-6: 0, 1e-5 * 64.0: 1, 64e-5: 2, 1e-12: 3}

        def eps_ap(eps):
            i = EPSI[eps]
            return epsT[:, i:i + 1]

        fng, fng_b = sb("fng", [128, 8], F32)
        P.dma('sp', fng[:], I['final_norm_g'].rearrange("(k p) -> p k", p=128), writes=[fng_b], slow=True)
        zcol, zcol_b = sb("zcol", [128, 1], F32)
        memset('pool', zcol[:], 0.0, [zcol_b])

        def dense_fm(w_ap, w_b, kc, src, src_b, tiles, evac):
            for ti, (t0, w) in enumerate(tiles):
                ps, psb = psum()
                for k in range(kc):
                    mm(ps[:, 0:w], psb, w_ap[:, k, :], src(k, t0, w), k == 0, k == kc - 1, [w_b, src_b])
                evac(ps, psb, t0, w)

        def phase_ret(bi, l):
            A.reset()
            lgt, lgt_b = A.alloc([128, 16], F32)
            gcr, gcr_b = A.alloc([128, 16], F32)
            rng, rng_b = A.alloc([128, 8], F32)
            P.dma('sp', lgt, I['ret_decay'][l].partition_broadcast(128), writes=[lgt_b], slow=True)
            P.dma('sp', rng, I['ret_norm_g'][l].rearrange("(h p) -> p h", p=128), writes=[rng_b], slow=True)
            act(lgt, lgt, AF.Exp, [lgt_b], [lgt_b])
            ts('pool', lgt, lgt, -1.0, None, ALU.mult, None, [lgt_b], [lgt_b])
            act(gcr, lgt, AF.Exp, [lgt_b], [gcr_b], scale=128.0)
            retD, retD_b = cst['retD']
            retM, retM_b = cst['retM']
            qdt, qdt_b = cst['qdt']
            kdt, kdt_b = cst['kdt']
            masks, masks_b = A.alloc([128, 8, 128], F32)
            qd, qd_b = A.alloc([128, 16, 128], F32)
            kd, kd_b = A.alloc([128, 16], F32)
            e2, e2_b = A.alloc([128, 128], F32)
            for h in range(8):
                lf = lgt[:, h:h + 1]
                lb = lgt[:, 8 + h:9 + h]
                act(masks[:, h, :], retD[:, 0, :], AF.Exp, [retD_b, lgt_b], [masks_b], scale=lf)
                tt('pool', masks[:, h, :], masks[:, h, :], retM[:, 0, :], ALU.mult, [masks_b, retM_b], [masks_b])
                act(e2, retD[:, 1, :], AF.Exp, [retD_b, lgt_b], [e2_b], scale=lb)
                tt('pool', e2, e2, retM[:, 1, :], ALU.mult, [e2_b, retM_b], [e2_b])
                tt('pool', masks[:, h, :], masks[:, h, :], e2, ALU.add, [masks_b, e2_b], [masks_b])
                act(qd[:, h * 2, :], qdt[:, 0, :], AF.Exp, [qdt_b, lgt_b], [qd_b], scale=lf)
                act(qd[:, h * 2 + 1, :], qdt[:, 1, :], AF.Exp, [qdt_b, lgt_b], [qd_b], scale=lb)
                act(kd[:, h * 2:h * 2 + 1], kdt[:, 0:1], AF.Exp, [kdt_b, lgt_b], [kd_b], scale=lf)
                act(kd[:, h * 2 + 1:h * 2 + 2], kdt[:, 1:2], AF.Exp, [kdt_b, lgt_b], [kd_b], scale=lb)
            base_off = A.off
            for j in range(4):
                P.barrier()
                A.off = base_off
                qT, qT_b = A.alloc([128, NCH, 128], BF16)
                kT, kT_b = A.alloc([128, NCH, 128], BF16)
                vt, vt_b = A.alloc([128, NCH, 256], BF16)
                P.dma('sp', qT, qk_s[j].rearrange("p (c t) -> p c t", c=NCH), reads=[qk_b], writes=[qT_b])
                P.dma('sp', kT, qk_s[4 + j].rearrange("p (c t) -> p c t", c=NCH), reads=[qk_b], writes=[kT_b])
                P.dma('sp', vt, v_s[:, :, j * 256:(j + 1) * 256].rearrange("c p e -> p c e"), reads=[v_b], writes=[vt_b])
                kdp, kdp_b = A.alloc([128, 2, 128], F32)
                for d in range(2):
                    for hp in range(2):
                        h = 2 * j + hp
                        ts('pool', kdp[:, d, hp * 64:(hp + 1) * 64], ones_f[:, 0:64], kd[:, h * 2 + d:h * 2 + d + 1], None,
                           ALU.mult, None, [ones_fb, kd_b], [kdp_b])
                ktd = [A.alloc([128, NCH, 128], BF16) for _ in range(2)]
                for c0 in range(0, NCH, 8):
                    n = min(8, NCH - c0)
                    ps, psb = psum()
                    psv = ps[:].bitcast(BF16)
                    for cc in range(n):
                        transp(psv[:, cc * 128:(cc + 1) * 128], psb, kT[:, c0 + cc, :], ident_b[:], [kT_b, ident_bb])
                    for d in range(2):
                        tt('dve', ktd[d][0][:, c0:c0 + n, :], psv[:, 0:n * 128].rearrange("p (c t) -> p c t", c=n),
                           kdp[:, d, :].unsqueeze(1).broadcast_to([128, n, 128]), ALU.mult, [psb, kdp_b], [ktd[d][1]])
                KV = [A.alloc([128, NCH, 128], F32) for _ in range(2)]
                Sbf = [A.alloc([128, NCH, 128], BF16) for _ in range(2)]
                Srun = [[A.alloc([128, 128], F32) for _ in range(2)] for _ in range(2)]
                qfb = [A.alloc([128, NCH, 128], BF16) for _ in range(2)]
                NCS = 4
                cs_ = []
                for _ in range(NCS):
                    cs_.append({'att': A.alloc([128, 4, 128], BF16), 'gt': A.alloc([128, 512], BF16), 'y': A.alloc([128, 512], F32),
                                'sq': A.alloc([128, 512], F32), 'rs': A.alloc([128, 512], F32)})
                ros = [A.alloc([128, TT], BF16) for _ in range(2)]
                for hp in range(2):
                    h = 2 * j + hp
                    r0, r1 = hp * 64, hp * 64 + 64
                    for d in range(2):
                        for (t0, w) in TILES:
                            c0, n = t0 // 128, w // 128
                            ps, psb = psum()
                            for cc in range(n):
                                mm(ps[:, cc * 128:(cc + 1) * 128], psb, ktd[d][0][:, c0 + cc, :], vt[:, c0 + cc, hp * 128:(hp + 1) * 128],
                                   True, True, [ktd[d][1], vt_b])
                            cp('act' if d else 'dve', KV[d][0][r0:r1, c0:c0 + n, :], ps[r0:r1, 0:w].rearrange("p (c t) -> p c t", c=n), [psb], [KV[d][1]])
                        memset('pool', Srun[d][0][0][:], 0.0, [Srun[d][0][1]])
                        for ci_, c in enumerate(ORDER_F if d == 0 else ORDER_B):
                            S_, S_b = Srun[d][ci_ % 2]
                            Sn_, Sn_b = Srun[d][(ci_ + 1) % 2]
                            cp('act', Sbf[d][0][r0:r1, c, :], S_[r0:r1, :], [S_b], [Sbf[d][1]])
                            stt(Sn_[r0:r1, :], S_[r0:r1, :], gcr[r0:r1, d * 8 + h:d * 8 + h + 1], KV[d][0][r0:r1, c, :], ALU.mult, ALU.add,
                                [S_b, gcr_b, KV[d][1]], [Sn_b])
                        tt('dve', qfb[d][0][r0:r1], qT[r0:r1], qd[r0:r1, h * 2 + d, :].unsqueeze(1).broadcast_to([64, NCH, 128]),
                           ALU.mult, [qT_b, qd_b], [qfb[d][1]])
                chains = [(hp, t0, w) for (t0, w) in TILES for hp in range(2)]
                for g0 in range(0, len(chains), NCS):
                    grp = chains[g0:g0 + NCS]
                    loc = []
                    for si, (hp, t0, w) in enumerate(grp):
                        h = 2 * j + hp
                        r0, r1 = hp * 64, hp * 64 + 64
                        c0, n = t0 // 128, w // 128
                        sl = cs_[si]
                        at_, at_b = sl['att']
                        gt_, gt_b = sl['gt']
                        P.dma('sp', gt_[:, 0:w], gate_s[h][:, t0:t0 + w], reads=[gate_b], writes=[gt_b])
                        ps, psb = psum()
                        for cc in range(n):
                            mm(ps[:, cc * 128:(cc + 1) * 128], psb, kT[r0:r1, c0 + cc, :], qT[r0:r1, c0 + cc, :], True, True, [kT_b, qT_b])
                        tt('dve', at_[:, 0:n, :], ps[:, 0:w].rearrange("p (c t) -> p c t", c=n),
                           masks[:, h, :].unsqueeze(1).broadcast_to([128, n, 128]), ALU.mult, [psb, masks_b], [at_b])
                        ps2, ps2b = psum()
                        for cc in range(n):
                            c = c0 + cc
                            o_ = ps2[:, cc * 128:(cc + 1) * 128]
                            mm(o_, ps2b, vt[:, c, hp * 128:(hp + 1) * 128], at_[:, cc, :], True, False, [vt_b, at_b])
                            mm(o_, ps2b, Sbf[0][0][r0:r1, c, :], qfb[0][0][r0:r1, c, :], False, False, [Sbf[0][1], qfb[0][1]])
                            mm(o_, ps2b, Sbf[1][0][r0:r1, c, :], qfb[1][0][r0:r1, c, :], False, True, [Sbf[1][1], qfb[1][1]])
                        loc.append((ps2, ps2b))
                    for si, (hp, t0, w) in enumerate(grp):
                        ys, ys_b = cs_[si]['y']
                        cp('act', ys[:, 0:w], loc[si][0][:, 0:w], [loc[si][1]], [ys_b])
                    loc3 = []
                    for si, (hp, t0, w) in enumerate(grp):
                        ys, ys_b = cs_[si]['y']
                        ps3, ps3b = psum()
                        mm(ps3[:, 0:w], ps3b, ones_f[:], ys[:, 0:w], True, True, [ones_fb, ys_b])
                        loc3.append((ps3, ps3b))
                    for si, (hp, t0, w) in enumerate(grp):
                        ys, ys_b = cs_[si]['y']
                        stt(ys[:, 0:w], loc3[si][0][:, 0:w], -1.0 / 128, ys[:, 0:w], ALU.mult, ALU.add, [loc3[si][1], ys_b], [ys_b])
                    for si, (hp, t0, w) in enumerate(grp):
                        ys, ys_b = cs_[si]['y']
                        sq, sq_b = cs_[si]['sq']
                        act(sq[:, 0:w], ys[:, 0:w], AF.Square, [ys_b], [sq_b])
                    loc4 = []
                    for si, (hp, t0, w) in enumerate(grp):
                        sq, sq_b = cs_[si]['sq']
                        ps4, ps4b = psum()
                        mm(ps4[:, 0:w], ps4b, ones_f[:], sq[:, 0:w], True, True, [ones_fb, sq_b])
                        loc4.append((ps4, ps4b))
                    for si, (hp, t0, w) in enumerate(grp):
                        rs, rs_b = cs_[si]['rs']
                        act(rs[:, 0:w], loc4[si][0][:, 0:w], AF.Sqrt, [loc4[si][1], epsT_b], [rs_b], bias=eps_ap(1e-5 * 64.0), scale=1.0 / 128)
                    for si, (hp, t0, w) in enumerate(grp):
                        rs, rs_b = cs_[si]['rs']
                        recip(rs[:, 0:w], rs[:, 0:w], [rs_b], [rs_b])
                    for si, (hp, t0, w) in enumerate(grp):
                        ys, ys_b = cs_[si]['y']
                        rs, rs_b = cs_[si]['rs']
                        tt('dve', ys[:, 0:w], ys[:, 0:w], rs[:, 0:w], ALU.mult, [ys_b, rs_b], [ys_b])
                    for si, (hp, t0, w) in enumerate(grp):
                        h = 2 * j + hp
                        ys, ys_b = cs_[si]['y']
                        gt_, gt_b = cs_[si]['gt']
                        ro, ro_b = ros[hp]
                        stt(ro[:, t0:t0 + w], ys[:, 0:w], rng[:, h:h + 1], gt_[:, 0:w], ALU.mult, ALU.mult, [ys_b, rng_b, gt_b], [ro_b])
                for hp in range(2):
                    P.dma('sp', ret_s[2 * j + hp], ros[hp][0][:], reads=[ros[hp][1]], writes=[ret_b])

        def phase_rwkv(bi, l):
            A.reset()
            rwmask, rwmask_b = cst['rwmask']
            m6, m6_b = cst['m6']
            mu, mu_b = A.alloc([128, 15], F32)
            mus, mus_b = A.alloc([128, 15, 7], F32)
            w0, w0_b = A.alloc([128, 2, 4], F32)
            a0, a0_b = A.alloc([128, 2, 4], F32)
            kkc, kkc_b = A.alloc([128, 4], F32)
            kac, kac_b = A.alloc([128, 4], F32)
            omk, omk_b = A.alloc([128, 4], F32)
            hrk, hrk_b = A.alloc([128, 4], F32)
            ngc, ngc_b = A.alloc([128, 4], F32)
            P.dma('sp', mu, I['rwkv_mu'][l].rearrange("(j p) -> p j", p=128), writes=[mu_b], slow=True)
            for d in range(2):
                P.dma('sp', w0[:, d, :], I['rwkv_w0'][l, d].rearrange("(j p) -> p j", p=128), writes=[w0_b], slow=True)
                P.dma('sp', a0[:, d, :], I['rwkv_a0'][l, d].rearrange("(j p) -> p j", p=128), writes=[a0_b], slow=True)
            P.dma('sp', kkc, I['rwkv_k_k'][l].rearrange("(j p) -> p j", p=128), writes=[kkc_b], slow=True)
            P.dma('sp', kac, I['rwkv_k_a'][l].rearrange("(j p) -> p j", p=128), writes=[kac_b], slow=True)
            P.dma('sp', hrk, I['rwkv_r_k'][l].rearrange("(j p) -> p j", p=128), writes=[hrk_b], slow=True)
            P.dma('sp', ngc, I['rwkv_norm_g'][l].rearrange("(j p) -> p j", p=128), writes=[ngc_b], slow=True)
            ts('pool', omk, kac, -1.0, 1.0, ALU.mult, ALU.add, [kac_b], [omk_b])
            ts('pool', hrk, hrk, 0.5, None, ALU.mult, None, [hrk_b], [hrk_b])
            ts('pool', mus[:, :, 0], mu, -1.0, 1.0, ALU.mult, ALU.add, [mu_b], [mus_b])
            for g in range(6):
                ts('pool', mus[:, :, 1 + g], mu, m6[:, g:g + 1], None, ALU.mult, None, [mu_b, m6_b], [mus_b])
            w2, w2_b = A.alloc([128, 512], BF16)
            a2, a2_b = A.alloc([128, 512], BF16)
            g2, g2_b = A.alloc([128, 512], BF16)
            P.dma('pool', w2, I['rwkv_w2'][l], writes=[w2_b])
            P.dma('pool', a2, I['rwkv_a2'][l], writes=[a2_b])
            P.dma('pool', g2, I['rwkv_g2'][l], writes=[g2_b])
            tw, tw_b = A.alloc([128, TT], BF16)
            za, za_b = A.alloc([128, TT], BF16)
            sg, sg_b = A.alloc([128, TT], BF16)
            base_off = A.off
            zin, zin_b = A.alloc([128, TT], F32)

            def zb(ci, dst, dst_b):
                dst_bs = list(dst_b) if isinstance(dst_b, list) else [dst_b]
                P.dma('sp', zin, zrw_s[ci], reads=[zrw_b], writes=[zin_b])
                act(dst, zin, AF.Identity, [zin_b, mus_b], dst_bs, scale=mus[:, ci, 0:1])
                z3 = zin[:, 0:T].rearrange("p (r c) -> p r c", c=64)
                d3 = dst[:, 0:T].rearrange("p (r c) -> p r c", c=64)
                rb = [zin_b, mus_b] + dst_bs
                stt(d3[:, :, 1:64], z3[:, :, 0:63], mus[:, ci, 1:2], d3[:, :, 1:64], ALU.mult, ALU.add, rb, dst_bs)
                stt(d3[:, :, 0:63], z3[:, :, 1:64], mus[:, ci, 2:3], d3[:, :, 0:63], ALU.mult, ALU.add, rb, dst_bs)
                stt(dst[:, 64:T], zin[:, 0:T - 64], mus[:, ci, 3:4], dst[:, 64:T], ALU.mult, ALU.add, rb, dst_bs)
                stt(dst[:, 0:T - 64], zin[:, 64:T], mus[:, ci, 4:5], dst[:, 0:T - 64], ALU.mult, ALU.add, rb, dst_bs)
                stt(dst[:, T + 1:TT], zin[:, T:TT - 1], mus[:, ci, 5:6], dst[:, T + 1:TT], ALU.mult, ALU.add, rb, dst_bs)
                stt(dst[:, T:TT - 1], zin[:, T + 1:TT], mus[:, ci, 6:7], dst[:, T:TT - 1], ALU.mult, ALU.add, rb, dst_bs)

            ztmp, ztmp_b = A.alloc([128, TT], F32)
            zb(12, ztmp, ztmp_b)
            act(tw, ztmp, AF.Tanh, [ztmp_b], [tw_b])
            zb(13, ztmp, ztmp_b)
            cp('act', za, ztmp, [ztmp_b], [za_b])
            zb(14, ztmp, ztmp_b)
            act(sg, ztmp, AF.Sigmoid, [ztmp_b], [sg_b])

            WSTOP = os.environ.get('WSTOP', '')
            if WSTOP == 'pro':
                return
            for j in range(4):
                P.barrier()
                A.off = base_off
                zin, zin_b = A.alloc([128, TT], F32)
                HV = [(0, 1024), (1024, 1280)]
                HT = [TILES[0:2], TILES[2:5]]
                HC = [(0, 8), (8, 18)]

                def hb():
                    return [Buf(), Buf()]
                zr, _ = A.alloc([128, TT], F32)
                zk, _ = A.alloc([128, TT], F32)
                zv, _ = A.alloc([128, TT], F32)
                kkn, _ = A.alloc([128, TT], F32)
                ksum, _ = A.alloc([128, TT], F32)
                Lw, _ = A.alloc([128, TT], F32)
                Ic, _ = A.alloc([128, TT], F32)
                Aa, _ = A.alloc([128, TT], F32)
                Tk, _ = A.alloc([128, TT], F32)
                zr_b, zk_b, zv_b, kkn_b, ksum_b, Lw_b, Ic_b, Aa_b, Tk_b = [hb() for _ in range(9)]
                stg = [A.alloc([128, TT], BF16) for _ in range(2)]
                stgb = [hb(), hb()]
                rt, rt_b = A.alloc([128, 512], F32)
                sn = [0]

                def S(h):
                    return slice(HV[h][0], HV[h][0] + HV[h][1])

                def emit(idx, d, fnh):
                    k_ = sn[0] % 2
                    sn[0] += 1
                    for h in range(2):
                        so = stg[k_][0][:, S(h)]
                        sob = stgb[k_][h]
                        fnh(h, so, sob)
                        P.dma('sp', rw_s[j, d, :, idx, S(h)], so, reads=[sob], writes=[rw_b])
                zb(j, zr, zr_b)
                zb(4 + j, zk, zk_b)
                zb(8 + j, zv, zv_b)
                X = zin
                X_b = hb()
                P.barrier()
                for d in range(2):
                    emit(6, d, lambda h, so, sob: cp('act', so, zv[:, S(h)], [zv_b[h]], [sob]))
                for h in range(2):
                    act(kkn[:, S(h)], zk[:, S(h)], AF.Identity, [zk_b[h], kkc_b], [kkn_b[h]], scale=kkc[:, j:j + 1])
                for h in range(2):
                    act(X[:, S(h)], kkn[:, S(h)], AF.Square, [kkn_b[h]], [X_b[h]])
                for h in range(2):
                    for (t0, w) in HT[h]:
                        ps, psb = psum()
                        mm(ps[:, 0:w], psb, bones_f[:], X[:, t0:t0 + w], True, True, [bones_fb, X_b[h]])
                        act(rt[:, 0:w], ps[:, 0:w], AF.Sqrt, [psb, epsT_b], [rt_b], bias=eps_ap(1e-12))
                        recip(rt[:, 0:w], rt[:, 0:w], [rt_b], [rt_b])
                        tt('dve', kkn[:, t0:t0 + w], kkn[:, t0:t0 + w], rt[:, 0:w], ALU.mult, [kkn_b[h], rt_b], [kkn_b[h]])
                I3 = Ic.rearrange("p (c t) -> p c t", t=128)
                X3 = X.rearrange("p (c t) -> p c t", t=128)
                L3 = Lw.rearrange("p (c t) -> p c t", t=128)

                def C3(a3, h):
                    return a3[:, HC[h][0]:HC[h][1], :]

                def totb(h):
                    n_ = HC[h][1] - HC[h][0]
                    return I3[:, HC[h][0]:HC[h][1], 127:128].broadcast_to([128, n_, 128])
                for d in range(2):
                    r0, r1 = d * 64, d * 64 + 64
                    for h in range(2):
                        for (t0, w) in HT[h]:
                            ps, psb = psum()
                            mm(ps[:, 0:w], psb, w2[r0:r1, j * 128:(j + 1) * 128], tw[r0:r1, t0:t0 + w], True, True, [w2_b, tw_b])
                            act(Lw[:, t0:t0 + w], ps[:, 0:w], AF.Sigmoid, [psb, w0_b], [Lw_b[h]], bias=w0[:, d, j:j + 1])
                            ps2, ps2b = psum()
                            mm(ps2[:, 0:w], ps2b, a2[r0:r1, j * 128:(j + 1) * 128], za[r0:r1, t0:t0 + w], True, True, [a2_b, za_b])
                            act(Aa[:, t0:t0 + w], ps2[:, 0:w], AF.Sigmoid, [ps2b, a0_b], [Aa_b[h]], bias=a0[:, d, j:j + 1])
                    for h in range(2):
                        act(Lw[:, S(h)], Lw[:, S(h)], AF.Identity, [Lw_b[h]], [Lw_b[h]], scale=DECAY_C)
                    for h in range(2):
                        for c in range(HC[h][0], HC[h][1]):
                            P.op('dve', lambda e, c=c: e.tensor_tensor_scan(out=Ic[:, c * 128:(c + 1) * 128], data0=ones_f[:],
                                                                            data1=Lw[:, c * 128:(c + 1) * 128], initial=0.0,
                                                                            op0=ALU.mult, op1=ALU.add),
                                 reads=[ones_fb, Lw_b[h]], writes=[Ic_b[h]])
                    for h in range(2):
                        act(Tk[:, S(h)], Aa[:, S(h)], AF.Identity, [Aa_b[h], kac_b, omk_b], [Tk_b[h]], bias=omk[:, j:j + 1], scale=kac[:, j:j + 1])
                    for h in range(2):
                        tt('dve', Tk[:, S(h)], Tk[:, S(h)], zk[:, S(h)], ALU.mult, [Tk_b[h], zk_b[h]], [Tk_b[h]])
                    for h in range(2):
                        if d == 0:
                            cp('act', ksum[:, S(h)], Tk[:, S(h)], [Tk_b[h]], [ksum_b[h]])
                        else:
                            tt('dve', ksum[:, S(h)], ksum[:, S(h)], Tk[:, S(h)], ALU.add, [ksum_b[h], Tk_b[h]], [ksum_b[h]])
                    for h in range(2):
                        tt('dve', Aa[:, S(h)], Aa[:, S(h)], kkn[:, S(h)], ALU.mult, [Aa_b[h], kkn_b[h]], [Aa_b[h]])
                    for h in range(2):
                        act(gC[:, j, d, HC[h][0]:HC[h][1]], I3[:, HC[h][0]:HC[h][1], 127], AF.Exp, [Ic_b[h]], [gC_b])
                    mul = lambda a_, a_b: (lambda h, so, sob: tt('dve', so, a_[:, S(h)], X[:, S(h)], ALU.mult, [a_b[h], X_b[h]], [sob]))
                    nmul = lambda a_, a_b: (lambda h, so, sob: stt(so, a_[:, S(h)], -1.0, X[:, S(h)], ALU.mult, ALU.mult, [a_b[h], X_b[h]], [sob]))

                    def xexp(src, src_b, scale=1.0):
                        for h in range(2):
                            act(X[:, S(h)], src[:, S(h)], AF.Exp, [src_b[h]], [X_b[h]], scale=scale)

                    def xtot_minus_I():
                        for h in range(2):
                            tt('dve', C3(X3, h), totb(h), C3(I3, h), ALU.subtract, [Ic_b[h]], [X_b[h]])
                    if d == 0:
                        xexp(Ic, Ic_b, -1.0)
                        emit(1, d, mul(Tk, Tk_b))
                        emit(2, d, mul(Aa, Aa_b))
                        xtot_minus_I()
                        xexp(X, X_b)
                        emit(4, d, mul(Tk, Tk_b))
                        emit(5, d, nmul(Aa, Aa_b))
                        xexp(Ic, Ic_b)
                        emit(3, d, mul(zr, zr_b))
                        for h in range(2):
                            tt('dve', X[:, S(h)], Ic[:, S(h)], Lw[:, S(h)], ALU.subtract, [Ic_b[h], Lw_b[h]], [X_b[h]])
                        xexp(X, X_b)
                        emit(0, d, mul(kkn, kkn_b))
                    else:
                        xtot_minus_I()
                        xexp(X, X_b)
                        emit(0, d, mul(kkn, kkn_b))
                        emit(3, d, mul(zr, zr_b))
                        for h in range(2):
                            tt('dve', Lw[:, S(h)], Ic[:, S(h)], Lw[:, S(h)], ALU.subtract, [Ic_b[h], Lw_b[h]], [Lw_b[h]])
                        for h in range(2):
                            tt('dve', C3(X3, h), C3(L3, h), totb(h), ALU.subtract, [Ic_b[h], Lw_b[h]], [X_b[h]])
                        xexp(X, X_b)
                        emit(1, d, mul(Tk, Tk_b))
                        emit(2, d, mul(Aa, Aa_b))
                        xexp(Lw, Lw_b)
                        emit(4, d, mul(Tk, Tk_b))
                        emit(5, d, nmul(Aa, Aa_b))
                for h in range(2):
                    tt('dve', X[:, S(h)], zr[:, S(h)], ksum[:, S(h)], ALU.mult, [zr_b[h], ksum_b[h]], [X_b[h]])
                for h in range(2):
                    act(X[:, S(h)], X[:, S(h)], AF.Identity, [X_b[h], hrk_b], [X_b[h]], scale=hrk[:, j:j + 1])
                for h in range(2):
                    for (t0, w) in HT[h]:
                        ps, psb = psum()
                        mm(ps[:, 0:w], psb, bones_f[:], X[:, t0:t0 + w], True, True, [bones_fb, X_b[h]])
                        tt('dve', Lw[:, t0:t0 + w], ps[:, 0:w], zv[:, t0:t0 + w], ALU.mult, [psb, zv_b[h]], [Lw_b[h]])
                        ps2, ps2b = psum()
                        mm(ps2[:, 0:w], ps2b, g2[:, j * 128:(j + 1) * 128], sg[:, t0:t0 + w], True, True, [g2_b, sg_b])
                        cp('act', Ic[:, t0:t0 + w], ps2[:, 0:w], [ps2b], [Ic_b[h]])
                P.dma('sp', bon_s[j, 0], Lw, reads=Lw_b, writes=[bon_b])
                P.dma('sp', bon_s[j, 1], Ic, reads=Ic_b, writes=[bon_b])

                if WSTOP == 'w1':
                    return
                P.barrier()
                A.off = base_off
                yacc, _ = A.alloc([128, NCH, 128], F32)
                yacc_bs = [Buf() for _ in range(NCH)]
                w3_off = A.off
                GI = 2
                nslot = GI * 2
                NHI = int(os.environ.get('NHI', '6'))

                def alloc_set():
                    st = {}
                    st['ld'] = [A.alloc([128, 7, 128], BF16) for _ in range(nslot)]
                    st['tok'] = [A.alloc([128, 3, 128], BF16) for _ in range(nslot)]
                    st['Gn'] = [A.alloc([128, 2, 128], F32) for _ in range(nslot * 2)]
                    st['Gb'] = [A.alloc([128, 3, 128], BF16) for _ in range(nslot * 2)]
                    st['XZ'] = [[A.alloc([128, 2, 128], F32), st['Gn'][m_]] for m_ in range(nslot * 2)]
                    st['XZb'] = [[A.alloc([128, 2, 128], BF16) for _ in range(2)] for _ in range(nslot * 2)]
                    st['PT'] = [A.alloc([128, 128], F32) for _ in range(nslot * 2)]
                    st['PTb'] = [A.alloc([128, 128], BF16) for _ in range(nslot * 2)]
                    st['r0'] = [A.alloc([128, 128], BF16) for _ in range(nslot)]
                    st['U'] = st['r0']
                    return st
                sets = [alloc_set(), alloc_set()]
                Hf = [A.alloc([128, 128], F32) for _ in range(2)]
                Hb = [A.alloc([128, 128], BF16) for _ in range(2)]
                for d in range(2):
                    memset('pool', Hf[d][0], 0.0, [Hf[d][1]])
                    memset('pool', Hb[d][0], 0.0, [Hb[d][1]])
                ywritten = set()
                ngroups = NCH // GI

                def prep(g, st):
                    items = []
                    for i in range(g * GI, (g + 1) * GI):
                        for d in range(2):
                            c = (ORDER_F if d == 0 else ORDER_B)[i]
                            sl = (i - g * GI) * 2 + d
                            ld, ld_b = st['ld'][sl]
                            tok, tok_b = st['tok'][sl]
                            P.dma('sp', ld, rw_s[j, d, :, :, c * 128:(c + 1) * 128], reads=[rw_b], writes=[ld_b])
                            ps, psb = psum()
                            psv = ps[:].bitcast(BF16)
                            for n_, idx in enumerate((6, 4, 5)):
                                transp(psv[:, n_ * 128:(n_ + 1) * 128], psb, ld[:, idx, :], ident_b[:], [ld_b, ident_bb])
                            cp('act', tok, psv[:, 0:384].rearrange("p (a b) -> p a b", a=3), [psb], [tok_b])
                            mats = []
                            for hp in range(2):
                                r0, r1 = hp * 64, hp * 64 + 64
                                mi = sl * 2 + hp
                                Gn_, Gn_b = st['Gn'][mi]
                                Gb_, Gb_b = st['Gb'][mi]
                                Qt, Kt, Bt, Rt = ld[r0:r1, 0, :], ld[r0:r1, 1, :], ld[r0:r1, 2, :], ld[r0:r1, 3, :]
                                psA, psAb = psum()
                                psB, psBb = psum()
                                mm(psA[:, 0:128], psAb, Qt, Bt, True, True, [ld_b])
                                mm(psA[:, 128:256], psAb, Bt, Qt, True, True, [ld_b])
                                mm(psA[:, 256:384], psAb, Kt, Qt, True, True, [ld_b])
                                mm(psA[:, 384:512], psAb, Kt, Rt, True, True, [ld_b])
                                mm(psB[:, 0:128], psBb, Bt, Rt, True, True, [ld_b])
                                tt('dve', Gn_[:, 0:2, :], psA[:, 0:256].rearrange("p (a b) -> p a b", a=2), rwmask[:, d, 0:2, :], ALU.mult,
                                   [psAb, rwmask_b], [Gn_b])
                                tt('dve', Gb_[:, 0:2, :], psA[:, 256:512].rearrange("p (a b) -> p a b", a=2), rwmask[:, d, 2:4, :], ALU.mult,
                                   [psAb, rwmask_b], [Gb_b])
                                tt('dve', Gb_[:, 2, :], psB[:, 0:128], rwmask[:, d, 4, :], ALU.mult, [psBb, rwmask_b], [Gb_b])
                                tt('pool', st['PT'][mi][0], ident_f[:], Gn_[:, 1, :], ALU.subtract, [ident_fb, Gn_b], [st['PT'][mi][1]])
                                mats.append(mi)
                            items.append((i, d, c, sl, mats))
                    return items

                def inv_slots(st, items):
                    allm = [mi for it in items for mi in it[4]]
                    state = {'cur': {mi: (st['Gn'][mi][0][:, 1, :], st['Gn'][mi][0][:, 0, :], st['Gn'][mi][1]) for mi in allm}, 'nxt': None}
                    slots = []
                    for lev in range(6):
                        last = lev == 5
                        hi = lev < NHI
                        nxt_hi = (lev + 1) < NHI

                        def sq(lev=lev, last=last, hi=hi, nxt_hi=nxt_hi):
                            nxt = {}
                            for n_i, mi in enumerate(allm):
                                Xm, Zm, XZb_ = state['cur'][mi]
                                ps, psb = psum()
                                mm(ps[:, 0:128], psb, Xm, Zm, True, True, [XZb_])
                                if not last:
                                    mm(ps[:, 128:256], psb, Zm, Xm, True, True, [XZb_])
                                k_ = 1 if last else 2
                                src = ps[:, 0:k_ * 128].rearrange("p (a b) -> p a b", a=k_)
                                e1, e2 = ('dve', 'act') if n_i % 4 == 0 else ('act', 'dve')
                                if hi:
                                    nf, nf_b = st['XZ'][mi][lev % 2]
                                    cp(e1, nf[:, 0:k_, :], src, [psb], [nf_b])
                                    zP = (nf[:, 0, :], nf_b)
                                    if nxt_hi:
                                        nxt[mi] = (nf[:, 1, :], nf[:, 0, :], nf_b, zP)
                                    else:
                                        nb, nb_b = st['XZb'][mi][lev % 2]
                                        cp('pool', nb[:, 0:k_, :], nf[:, 0:k_, :], [nf_b], [nb_b])
                                        nxt[mi] = (nb[:, 1, :], nb[:, 0, :], nb_b, zP)
                                else:
                                    nb, nb_b = st['XZb'][mi][lev % 2]
                                    cp(e1, nb[:, 0:k_, :], src, [psb], [nb_b])
                                    nxt[mi] = (nb[:, 1, :], nb[:, 0, :], nb_b, (nb[:, 0, :], nb_b))
                            state['nxt'] = nxt

                        def pu(lev=lev, hi=hi, nxt_hi=nxt_hi):
                            nxt = state['nxt']
                            for mi in allm:
                                Z2, Z2_b = nxt[mi][3]
                                pf, pf_b = st['PT'][mi]
                                pb, pb_b = st['PTb'][mi]
                                ps, psb = psum()
                                if hi:
                                    mm(ps[:, 0:128], psb, Z2, pf, True, True, [Z2_b, pf_b])
                                    if nxt_hi:
                                        tt('dve', pf, ps[:, 0:128], pf, ALU.add, [psb, pf_b], [pf_b])
                                    else:
                                        tt('dve', pb, ps[:, 0:128], pf, ALU.add, [psb, pf_b], [pb_b])
                                else:
                                    mm(ps[:, 0:128], psb, Z2, pb, True, True, [Z2_b, pb_b])
                                    tt('dve', pb, ps[:, 0:128], pb, ALU.add, [psb, pb_b], [pb_b])
                            state['cur'] = {mi: nxt[mi][0:3] for mi in allm}
                        slots.append(sq)
                        slots.append(pu)
                    return slots

                def seq_stages(st, items, d):
                    stages = []
                    for (i, d_, c, sl, mats) in items:
                        if d_ != d:
                            continue
                        ld, ld_b = st['ld'][sl]
                        tok, tok_b = st['tok'][sl]
                        Hb_, Hb_b = Hb[d]
                        Hf_, Hf_b = Hf[d]
                        rb_, rb_b = st['r0'][sl]
                        U_, U_b = st['U'][sl]

                        def s1(ld=ld, ld_b=ld_b, tok=tok, tok_b=tok_b, Hb_=Hb_, Hb_b=Hb_b, rb_=rb_, rb_b=rb_b, mats=mats):
                            for hp in range(2):
                                r0, r1 = hp * 64, hp * 64 + 64
                                G_, G_b = st['Gb'][mats[hp]]
                                ps, psb = psum()
                                mm(ps[:, 0:64], psb, ld[r0:r1, 0, :], Hb_[r0:r1, r0:r1], True, False, [ld_b, Hb_b])
                                mm(ps[:, 0:64], psb, G_[:, 0, :], tok[:, 0, r0:r1], False, True, [G_b, tok_b])
                                cp('act', rb_[:, r0:r1], ps[:, 0:64], [psb], [rb_b])

                        def s2(rb_=rb_, rb_b=rb_b, U_=U_, U_b=U_b, mats=mats):
                            ps, psb = psum()
                            for hp in range(2):
                                r0, r1 = hp * 64, hp * 64 + 64
                                pt_, pt_b = st['PTb'][mats[hp]]
                                mm(ps[:, r0:r1], psb, pt_, rb_[:, r0:r1], True, True, [pt_b, rb_b])
                            cp('act', U_, ps[:, 0:128], [psb], [U_b])

                        def s3(ld=ld, ld_b=ld_b, tok=tok, tok_b=tok_b, Hb_=Hb_, Hb_b=Hb_b, Hf_=Hf_, Hf_b=Hf_b, U_=U_, U_b=U_b,
                               mats=mats, c=c, d=d):
                            for hp in range(2):
                                r0, r1 = hp * 64, hp * 64 + 64
                                G_, G_b = st['Gb'][mats[hp]]
                                ps, psb = psum()
                                mm(ps[:, 0:64], psb, ld[r0:r1, 3, :], Hb_[r0:r1, r0:r1], True, False, [ld_b, Hb_b])
                                mm(ps[:, 0:64], psb, G_[:, 1, :], tok[:, 0, r0:r1], False, False, [G_b, tok_b])
                                mm(ps[:, 0:64], psb, G_[:, 2, :], U_[:, r0:r1], False, True, [G_b, U_b])
                                if c not in ywritten:
                                    cp('dve', yacc[:, c, r0:r1], ps[:, 0:64], [psb], [yacc_bs[c]])
                                else:
                                    tt('dve', yacc[:, c, r0:r1], ps[:, 0:64], yacc[:, c, r0:r1], ALU.add, [psb, yacc_bs[c]], [yacc_bs[c]])
                            ywritten.add(c)
                            ps, psb = psum()
                            mm(ps[:, 0:128], psb, tok[:, 1, :], tok[:, 0, :], True, False, [tok_b])
                            mm(ps[:, 0:128], psb, tok[:, 2, :], U_, False, True, [tok_b, U_b])
                            stt(Hf_, Hf_, gC[:, j, d, c:c + 1], ps[:, 0:128], ALU.mult, ALU.add, [Hf_b, gC_b, psb], [Hf_b])
                            cp('act', Hb_, Hf_, [Hf_b], [Hb_b])
                        stages += [s1, s2, s3]
                    return stages

                items_cur = prep(0, sets[0])
                for f_ in inv_slots(sets[0], items_cur):
                    f_()
                for g in range(ngroups):
                    st = sets[g % 2]
                    if g + 1 < ngroups:
                        items_nxt = prep(g + 1, sets[(g + 1) % 2])
                        slots = inv_slots(sets[(g + 1) % 2], items_nxt)
                    else:
                        items_nxt, slots = None, []
                    sf = seq_stages(st, items_cur, 0)
                    sb_ = seq_stages(st, items_cur, 1)
                    for s_ in range(max(len(slots), len(sf))):
                        if s_ < len(slots):
                            slots[s_]()
                        if s_ < len(sf):
                            sf[s_]()
                            sb_[s_]()
                    items_cur = items_nxt

                if WSTOP == 'w2':
                    return
                P.barrier()
                A.off = w3_off
                mn, mn_b = A.alloc([128, 36], F32)
                vr, vr_b = A.alloc([128, 36], F32)
                sqb, sqb_b = A.alloc([128, 36, 64], F32)
                ynb, ynb_b = A.alloc([128, NCH, 128], BF16)
                bon, bon_bb = A.alloc([128, TT], F32)
                gg, gg_b = A.alloc([128, TT], F32)
                tmp, tmp_b = A.alloc([128, 1024], F32)
                ro, ro_b = A.alloc([128, TT], BF16)
                P.dma('sp', bon, bon_s[j, 0], reads=[bon_b], writes=[bon_bb])
                P.dma('sp', gg, bon_s[j, 1], reads=[bon_b], writes=[gg_b])
                y4 = yacc.rearrange("p c (h v) -> p (c h) v", h=2)
                yacc_b = Buf()
                P.op('dve', lambda e: e.tensor_reduce(out=mn, in_=y4, axis=AX.X, op=ALU.add), reads=yacc_bs, writes=[mn_b, yacc_b])
                ts('pool', mn, mn, 1.0 / 64, None, ALU.mult, None, [mn_b], [mn_b])
                tt('dve', y4, y4, mn.unsqueeze(2).broadcast_to([128, 36, 64]), ALU.subtract, [yacc_b, mn_b], [yacc_b])
                act(sqb, y4, AF.Square, [yacc_b], [sqb_b])
                P.op('dve', lambda e: e.tensor_reduce(out=vr, in_=sqb, axis=AX.X, op=ALU.add), reads=[sqb_b], writes=[vr_b])
                act(vr, vr, AF.Sqrt, [vr_b, epsT_b], [vr_b], bias=eps_ap(64e-5), scale=1.0 / 64)
                recip(vr, vr, [vr_b], [vr_b])
                tt('dve', ynb.rearrange("p c (h v) -> p (c h) v", h=2), y4, vr.unsqueeze(2).broadcast_to([128, 36, 64]), ALU.mult,
                   [yacc_b, vr_b], [ynb_b])
                for c0 in range(0, NCH, 8):
                    n = min(8, NCH - c0)
                    ps, psb = psum()
                    psv = ps[:].bitcast(BF16)
                    for cc in range(n):
                        transp(psv[:, cc * 128:(cc + 1) * 128], psb, ynb[:, c0 + cc, :], ident_b[:], [ynb_b, ident_bb])
                    cs = slice(c0 * 128, (c0 + n) * 128)
                    stt(tmp[:, 0:n * 128], psv[:, 0:n * 128], ngc[:, j:j + 1], bon[:, cs], ALU.mult, ALU.add, [psb, ngc_b, bon_bb], [tmp_b])
                    tt('dve', ro[:, cs], tmp[:, 0:n * 128], gg[:, cs], ALU.mult, [tmp_b, gg_b], [ro_b])
                P.dma('sp', rwkv_s[j], ro, reads=[ro_b], writes=[rwkv_b])

        def phase_merge(bi, l):
            A.reset()
            last = (l == nlayers - 1)
            tiles = TILES[:4] if last else TILES
            Wa, Wa_b = A.alloc([128, 8, 1024], BF16)
            Wb, Wb_b = A.alloc([128, 4, 1024], BF16)
            Wo, Wo_b = A.alloc([128, 8, 1024], BF16)
            for hf in range(2):
                P.dma('pool', Wa[:, :, hf * 512:(hf + 1) * 512], I['w_branch_a'][l, :, hf * 512:(hf + 1) * 512].rearrange("(k p) n -> p k n", p=128), writes=[Wa_b])
                P.dma('pool', Wb[:, :, hf * 512:(hf + 1) * 512], I['w_branch_b'][l, :, hf * 512:(hf + 1) * 512].rearrange("(k p) n -> p k n", p=128), writes=[Wb_b])
                P.dma('pool', Wo[:, :, hf * 512:(hf + 1) * 512], I['w_out'][l, :, hf * 512:(hf + 1) * 512].rearrange("(k p) n -> p k n", p=128), writes=[Wo_b])
            bufs = []
            for _ in range(2):
                bufs.append((A.alloc([128, 8, 512], BF16), A.alloc([128, 4, 512], BF16), A.alloc([128, 8, 512], BF16), A.alloc([128, 8, 512], BF16)))
            mT, mT_b = A.alloc([128, 8, 512], BF16)
            m1, m1_b = A.alloc([128, 512], F32)
            m2, m2_b = A.alloc([128, 512], F32)
            for ti, (t0, w) in enumerate(tiles):
                (rt_, rt_b), (wt_, wt_b), (ga, ga_b), (gb, gb_b) = bufs[ti % 2]
                r = 2 if t0 >= T else bi
                P.dma('sp', rt_[:, :, 0:w], ret_s[:, :, t0:t0 + w].rearrange("h p t -> p h t"), reads=[ret_b], writes=[rt_b])
                P.dma('sp', wt_[:, :, 0:w], rwkv_s[:, :, t0:t0 + w].rearrange("h p t -> p h t"), reads=[rwkv_b], writes=[wt_b])
                P.dma('sp', ga[:, :, 0:w], gate_s[8:16, :, t0:t0 + w].rearrange("h p t -> p h t"), reads=[gate_b], writes=[ga_b])
                P.dma('sp', gb[:, :, 0:w], gate_s[16:24, :, t0:t0 + w].rearrange("h p t -> p h t"), reads=[gate_b], writes=[gb_b])
                for jo in range(8):
                    ps, psb = psum()
                    for k in range(8):
                        mm(ps[:, 0:w], psb, Wa[:, k, jo * 128:(jo + 1) * 128], rt_[:, k, 0:w], k == 0, k == 7, [Wa_b, rt_b])
                    tt('dve', m1[:, 0:w], ps[:, 0:w], ga[:, jo, 0:w], ALU.mult, [psb, ga_b], [m1_b])
                    ps2, ps2b = psum()
                    for k in range(4):
                        mm(ps2[:, 0:w], ps2b, Wb[:, k, jo * 128:(jo + 1) * 128], wt_[:, k, 0:w], k == 0, k == 3, [Wb_b, wt_b])
                    tt('dve', m2[:, 0:w], ps2[:, 0:w], gb[:, jo, 0:w], ALU.mult, [ps2b, gb_b], [m2_b])
                    tt('dve', mT[:, jo, 0:w], m1[:, 0:w], m2[:, 0:w], ALU.add, [m1_b, m2_b], [mT_b])
                for jo in range(8):
                    ps, psb = psum()
                    for k in range(8):
                        mm(ps[:, 0:w], psb, Wo[:, k, jo * 128:(jo + 1) * 128], mT[:, k, 0:w], k == 0, k == 7, [Wo_b, mT_b])
                    stt(xT[:, jo, t0:t0 + w], ps[:, 0:w], modT[:, l, 16 + jo, r:r + 1], xT[:, jo, t0:t0 + w], ALU.mult, ALU.add,
                        [psb, modT_b, xT_b], [xT_b])

        def phase_ffn(bi, l):
            A.reset()
            last = (l == nlayers - 1)
            sups = [(0, 1024), (1024, 1024)] + ([] if last else [(2048, 256)])
            h2, h2_b = A.alloc([128, 8, 1024], BF16)
            hid, hid_b = A.alloc([128, 22, 1024], BF16)
            sq, sq_b = A.alloc([128, 8, 512], F32)
            rs, rs_b = A.alloc([128, 512], F32)
            tmp, _ = A.alloc([128, 2, 512], F32)
            tmp_b = [Buf(), Buf()]
            wa = [A.alloc([128, 8, 128], BF16) for _ in range(3)]
            wg = [A.alloc([128, 8, 128], BF16) for _ in range(3)]
            w2c = [A.alloc([128, 22, 128], BF16) for _ in range(2)]
            sa = [A.alloc([128, 512], F32) for _ in range(2)]
            si = 0
            for (T0, W) in sups:
                subt = [(t0, min(512, T0 + W - t0)) for t0 in range(T0, T0 + W, 512)]
                norm_mod(lambda k, t0, w: h2[:, k, t0 - T0:t0 - T0 + w], h2_b,
                         lambda k, r: (modA[:, l, 1, k, (bi if r is None else r):(bi if r is None else r) + 1], [modA_b]),
                         lambda k, r: (modT[:, l, 24 + k, (bi if r is None else r):(bi if r is None else r) + 1], [modT_b]),
                         sq, sq_b, rs, rs_b, tmp, tmp_b, 1e-6, subt)
                for jh in range(22):
                    wa_, wa_b = wa[jh % 3]
                    wg_, wg_b = wg[jh % 3]
                    P.dma('pool', wa_, I['ffn_w13'][l, :, jh * 128:(jh + 1) * 128].rearrange("(k p) n -> p k n", p=128), writes=[wa_b])
                    P.dma('pool', wg_, I['ffn_w13'][l, :, FH + jh * 128:FH + (jh + 1) * 128].rearrange("(k p) n -> p k n", p=128), writes=[wg_b])
                    for (t0, w) in subt:
                        o0 = t0 - T0
                        ps, psb = psum()
                        for k in range(8):
                            mm(ps[:, 0:w], psb, wa_[:, k, :], h2[:, k, o0:o0 + w], k == 0, k == 7, [wa_b, h2_b])
                        ps2, ps2b = psum()
                        for k in range(8):
                            mm(ps2[:, 0:w], ps2b, wg_[:, k, :], h2[:, k, o0:o0 + w], k == 0, k == 7, [wg_b, h2_b])
                        sa_, sa_b = sa[si % 2]
                        si += 1
                        act(sa_[:, 0:w], ps[:, 0:w], AF.Silu, [psb], [sa_b])
                        tt('dve', hid[:, jh, o0:o0 + w], ps2[:, 0:w], sa_[:, 0:w], ALU.mult, [ps2b, sa_b], [hid_b])
                for jo in range(8):
                    w2_, w2_b = w2c[jo % 2]
                    for hf in range(2):
                        P.dma('pool', w2_[:, hf * 11:(hf + 1) * 11, :],
                              I['ffn_w2'][l, hf * 1408:(hf + 1) * 1408, jo * 128:(jo + 1) * 128].rearrange("(k p) n -> p k n", p=128), writes=[w2_b])
                    for (t0, w) in subt:
                        o0 = t0 - T0
                        r = 2 if t0 >= T else bi
                        ps, psb = psum()
                        for k in range(22):
                            mm(ps[:, 0:w], psb, w2_[:, k, :], hid[:, k, o0:o0 + w], k == 0, k == 21, [w2_b, hid_b])
                        stt(xT[:, jo, t0:t0 + w], ps[:, 0:w], modT[:, l, 40 + jo, r:r + 1], xT[:, jo, t0:t0 + w], ALU.mult, ALU.add,
                            [psb, modT_b, xT_b], [xT_b])

        def phase_final(bi):
            A.reset()
            yf, yf_b = A.alloc([128, 8, 512], F32)
            sq, sq_b = A.alloc([128, 8, 512], F32)
            rs, rs_b = A.alloc([128, 512], F32)
            tmp, _ = A.alloc([128, 2, 512], F32)
            tmp_b = [Buf(), Buf()]
            ost = [A.alloc([128, D], F32) for _ in range(2)]
            oi = 0
            for (t0, w) in TILES[:4]:
                norm_mod(lambda k, t0_, w_: yf[:, k, 0:w_], yf_b,
                         lambda k, r: (fng[:, k:k + 1], [fng_b]),
                         lambda k, r: (zcol[:, 0:1], [zcol_b]),
                         sq, sq_b, rs, rs_b, tmp, tmp_b, 1e-6, [(t0, w)])
                for cc in range(4):
                    o_, o_b = ost[oi % 2]
                    oi += 1
                    for hf in range(2):
                        ps, psb = psum()
                        for kk in range(4):
                            k = hf * 4 + kk
                            transp(ps[:, kk * 128:(kk + 1) * 128], psb, yf[:, k, cc * 128:(cc + 1) * 128], ident_f[:], [yf_b, ident_fb])
                        cp('act' if hf else 'dve', o_[:, hf * 512:(hf + 1) * 512], ps[:], [psb], [o_b])
                    P.dma('sp', out[bi, t0 + cc * 128:t0 + (cc + 1) * 128, :], o_, reads=[o_b], writes=[out_b])
        for bi in range(nbatch):
            A.reset()
            stg = [A.alloc([128, D], F32) for _ in range(2)]
            for c in range(NCH):
                s_, s_b = stg[c % 2]
                src_ = I['x'][bi, c * 128:(c + 1) * 128, :] if c < 16 else I['ctx'][bi, (c - 16) * 128:(c - 15) * 128, :]
                P.dma('sp', s_[:], src_, writes=[s_b])
                for hf in range(2):
                    ps, psb = psum()
                    for kk in range(4):
                        k = hf * 4 + kk
                        transp(ps[:, kk * 128:(kk + 1) * 128], psb, s_[:, k * 128:(k + 1) * 128], ident_f[:], [s_b, ident_fb])
                    cp('act' if hf else 'dve', xT[:, hf * 4:(hf + 1) * 4, c * 128:(c + 1) * 128],
                       ps[:].rearrange("p (a b) -> p a b", a=4), [psb], [xT_b])
            P.barrier()

            for l in range(nlayers):
                A.reset()
                hT, hT_b = A.alloc([128, 8, TT], BF16)
                sq, sq_b = A.alloc([128, 8, 512], F32)
                rs, rs_b = A.alloc([128, 512], F32)
                tmp, _ = A.alloc([128, 2, 512], F32)
                tmp_b = [Buf(), Buf()]
                norm_mod(lambda k, t0, w: hT[:, k, t0:t0 + w], hT_b,
                         lambda k, r: (modA[:, l, 0, k, (bi if r is None else r):(bi if r is None else r) + 1], [modA_b]),
                         lambda k, r: (modT[:, l, 0 + k, (bi if r is None else r):(bi if r is None else r) + 1], [modT_b]),
                         sq, sq_b, rs, rs_b, tmp, tmp_b, 1e-6, TILES)
                if 'dbg_h' in debug:
                    for k in range(8):
                        P.dma('sp', dbg_h[k], hT[:, k, :], reads=[hT_b], writes=[dbg_hb])
                P.barrier()
                A.off = 8 * TT // 2
                rc, rc_b = A.alloc([128, TT], F32)
                rsn, rsn_b = A.alloc([128, TT], F32)
                P.dma('sp', rc[:], C['ropecos'], writes=[rc_b])
                P.dma('sp', rsn[:], C['ropesin'], writes=[rsn_b])
                wch = [A.alloc([128, 8, 128], BF16) for _ in range(4)]
                wn = [0]

                def loadw(src2d):
                    w_, w_b = wch[wn[0] % 4]
                    wn[0] += 1
                    P.dma('pool', w_, src2d.rearrange("(k p) n -> p k n", p=128), writes=[w_b])
                    return w_, w_b
                hsrc = lambda k, t0, w: hT[:, k, t0:t0 + w]
                stgb = [A.alloc([128, TT], BF16) for _ in range(2)]
                stgf = [A.alloc([128, TT], F32) for _ in range(2)]
                t1, t1_b = A.alloc([128, 512], F32)
                t2, t2_b = A.alloc([128, 512], F32)
                sn = [0]
                for qk in range(2):
                    for j in range(4):
                        c0 = qk * 512 + j * 128
                        w_, w_b = loadw(I['w_in'][l, :, c0:c0 + 128])
                        wr_, wr_b = loadw(I['w_rot'][l, :, c0:c0 + 128])
                        so, so_b = stgb[sn[0] % 2]
                        sn[0] += 1
                        for (t0, w) in TILES:
                            ps, psb = psum()
                            ps2, ps2b = psum()
                            for k in range(8):
                                mm(ps[:, 0:w], psb, w_[:, k, :], hsrc(k, t0, w), k == 0, k == 7, [w_b, hT_b])
                            for k in range(8):
                                mm(ps2[:, 0:w], ps2b, wr_[:, k, :], hsrc(k, t0, w), k == 0, k == 7, [wr_b, hT_b])
                            tt('dve', t1[:, 0:w], ps[:, 0:w], rc[:, t0:t0 + w], ALU.mult, [psb, rc_b], [t1_b])
                            tt('dve', t2[:, 0:w], ps2[:, 0:w], rsn[:, t0:t0 + w], ALU.mult, [ps2b, rsn_b], [t2_b])
                            tt('dve', so[:, t0:t0 + w], t1[:, 0:w], t2[:, 0:w], ALU.add, [t1_b, t2_b], [so_b])
                        P.dma('sp', qk_s[qk * 4 + j], so[:], reads=[so_b], writes=[qk_b])
                for g in range(3):
                    base = [2048, 4992, 6016][g]
                    fn = AF.Silu if g == 0 else AF.Sigmoid
                    for j in range(8):
                        w_, w_b = loadw(I['w_in'][l, :, base + j * 128:base + (j + 1) * 128])
                        so, so_b = stgb[sn[0] % 2]
                        sn[0] += 1
                        dense_fm(w_, w_b, 8, hsrc, hT_b, TILES,
                                 lambda ps, psb, t0, w, so=so, so_b=so_b, fn=fn: act(so[:, t0:t0 + w], ps[:, 0:w], fn, [psb], [so_b]))
                        P.dma('sp', gate_s[g * 8 + j], so[:], reads=[so_b], writes=[gate_b])
                for j in range(15):
                    w_, w_b = loadw(I['w_in'][l, :, 3072 + j * 128:3072 + (j + 1) * 128])
                    so, so_b = stgf[j % 2]
                    dense_fm(w_, w_b, 8, hsrc, hT_b, TILES,
                             lambda ps, psb, t0, w, so=so, so_b=so_b: cp('act', so[:, t0:t0 + w], ps[:, 0:w], [psb], [so_b]))
                    P.dma('sp', zrw_s[j], so[:], reads=[so_b], writes=[zrw_b])
                P.barrier()
                A.off = 8 * TT // 2
                wv = [A.alloc([128, 8, 512], BF16) for _ in range(2)]
                for hf in range(2):
                    P.dma('pool', wv[hf][0], I['w_in'][l, :, 1024 + hf * 512:1024 + (hf + 1) * 512].rearrange("(k p) n -> p k n", p=128),
                          writes=[wv[hf][1]])
                vst = [A.alloc([128, 1024], BF16) for _ in range(2)]
                for c in range(NCH):
                    vo, vo_b = vst[c % 2]
                    for hf in range(2):
                        ps, psb = psum()
                        for k in range(8):
                            mm(ps[:], psb, hT[:, k, c * 128:(c + 1) * 128], wv[hf][0][:, k, :], k == 0, k == 7, [hT_b, wv[hf][1]])
                        cp('act' if hf else 'dve', vo[:, hf * 512:(hf + 1) * 512], ps[:], [psb], [vo_b])
                    P.dma('sp', v_s[c], vo[:], reads=[vo_b], writes=[v_b])
                P.barrier()
                if stop_after == 'A':
                    break

                phase_ret(bi, l)
                P.barrier()
                if stop_after == 'R':
                    break
                phase_rwkv(bi, l)
                P.barrier()
                if stop_after == 'W':
                    break
                phase_merge(bi, l)
                P.barrier()
                if stop_after == 'G':
                    break
                phase_ffn(bi, l)
                P.barrier()
            if 'dbg_x' in debug:
                for k in range(8):
                    P.dma('sp', dbg_x[k], xT[:, k, :], reads=[xT_b], writes=[dbg_xb])
            if stop_after is None:
                phase_final(bi)
            P.barrier()

        P.barrier()
        with nc.Block() as block:
            @block.sync
            def _(e):
                for f in P.stream['sp']:
                    f(e)

            @block.tensor
            def _(e):
                for f in P.stream['pe']:
                    f(e)

            @block.scalar
            def _(e):
                for f in P.stream['act']:
                    f(e)

            @block.vector
            def _(e):
                for f in P.stream['dve']:
                    f(e)

            @block.gpsimd
            def _(e):
                for f in P.stream['pool']:
                    f(e)
    return nc


def prep_inputs(inputs):
    consts = _host_consts()
    f = lambda a: np.ascontiguousarray(np.asarray(a, dtype=np.float32))
    w_in = f(inputs['w_in'])
    perm = np.zeros(1024, np.int64)
    for cidx in range(1024):
        h, i = divmod(cidx % 512, 64)
        ii = i % 32
        partner = i + 16 if ii < 16 else i - 16
        perm[cidx] = (cidx // 512) * 512 + h * 64 + partner
    w_rot = np.ascontiguousarray(w_in[:, :, perm])
    shared = {
        'c_ctx': f(inputs['c_ctx']), 'mod_w': f(inputs['mod_w']), 'mod_b': f(inputs['mod_b']),
        'norm1_g': f(inputs['norm1_g']), 'norm2_g': f(inputs['norm2_g']), 'w_in': w_in, 'w_rot': w_rot,
        'ret_decay': f(inputs['ret_decay']).reshape(NL, 16), 'ret_norm_g': f(inputs['ret_norm_g']),
        'rwkv_mu': f(inputs['rwkv_mu']), 'rwkv_w0': f(inputs['rwkv_w0']),
        'rwkv_w2': f(inputs['rwkv_w2']).reshape(NL, 128, 512), 'rwkv_a0': f(inputs['rwkv_a0']),
        'rwkv_a2': f(inputs['rwkv_a2']).reshape(NL, 128, 512), 'rwkv_g2': f(inputs['rwkv_g2']),
        'rwkv_k_k': f(inputs['rwkv_k_k']), 'rwkv_k_a': f(inputs['rwkv_k_a']),
        'rwkv_r_k': f(inputs['rwkv_r_k']).reshape(NL, 512), 'rwkv_norm_g': f(inputs['rwkv_norm_g']),
        'w_branch_a': f(inputs['w_branch_a']), 'w_branch_b': f(inputs['w_branch_b']), 'w_out': f(inputs['w_out']),
        'ffn_w13': f(inputs['ffn_w13']), 'ffn_w2': f(inputs['ffn_w2']), 'final_norm_g': f(inputs['final_norm_g']),
    }
    shared.update(consts)
    x = f(inputs['x']); c = f(inputs['c']); ctx = f(inputs['ctx'])
    in_maps = []
    for i in range(8):
        m = dict(shared)
        m['x'] = x[i * NB:(i + 1) * NB]
        m['c'] = c[i * NB:(i + 1) * NB]
        m['ctx'] = ctx[i * NB:(i + 1) * NB]
        in_maps.append(m)
    return in_maps


def kernel(**inputs):
    in_maps = prep_inputs(inputs)
    nc = build()
    res = run_bass_kernel_spmd(nc, in_maps, core_ids=list(range(8)))
    return np.concatenate([np.asarray(r['out'], dtype=np.float32) for r in res.results], axis=0)
```
